# Optimizing a Trainium2 kernel written in Bass

```python
import jax, jax.numpy as jnp
from jax import lax
import numpy as np

D_MODEL = 1024
BATCH = 4
SEQ = 4096
DEPTH = 2

GRID_W = 64
CTX_LEN = 256
GROUP_W = D_MODEL // 4
HEAD_DIM = 64
N_HEADS_GROUP = GROUP_W // HEAD_DIM
SC_KERNEL = 3
MLA_Q_RANK = 256
MLA_KV_RANK = 128
MLA_NOPE = 64
MLA_ROPE = 32
MLA_V = 64
MLA_SCALE = (MLA_NOPE + MLA_ROPE) ** -0.5
Q_BLOCK = 128
CF_KERNEL = 31
NA_ROWS = 8
NA_COLS = 16
NA_SCALE = HEAD_DIM ** -0.5
ROPE_BASE = 10000.0
MLP_HIDDEN = 4 * D_MODEL
N_MOD = 6
EPS = 1e-6

SC_END = 3 * GROUP_W
MLA_Q_END = SC_END + MLA_Q_RANK
MLA_KV_END = MLA_Q_END + MLA_KV_RANK + MLA_ROPE
CF_END = MLA_KV_END + 2 * GROUP_W
NA_Q_END = CF_END + GROUP_W
P_IN = NA_Q_END + 2 * GROUP_W

kernel_name = "hybrid_parallel_groups_dit_block"


def rmsnorm(x, g):
    xf = x.astype(jnp.float32)
    y = xf * lax.rsqrt(jnp.mean(xf * xf, axis=-1, keepdims=True) + EPS)
    return y.astype(x.dtype) * g


def layernorm(x, g, b):
    xf = x.astype(jnp.float32)
    mu = jnp.mean(xf, axis=-1, keepdims=True)
    var = jnp.mean(jnp.square(xf - mu), axis=-1, keepdims=True)
    return ((xf - mu) * lax.rsqrt(var + EPS)).astype(x.dtype) * g + b


def dwconv(x, w):
    k = w.shape[0]
    return lax.conv_general_dilated(
        x, w[:, None, :], window_strides=(1,), padding=[(k // 2, k // 2)],
        dimension_numbers=("NWC", "WIO", "NWC"), feature_group_count=x.shape[-1])


def heads(t):
    return t.reshape(*t.shape[:-1], N_HEADS_GROUP, -1)


def rope_rotate(x, ang):
    x1, x2 = jnp.split(x, 2, axis=-1)
    cos, sin = jnp.cos(ang).astype(x.dtype), jnp.sin(ang).astype(x.dtype)
    return jnp.concatenate([x1 * cos - x2 * sin, x1 * sin + x2 * cos], axis=-1)


def axial_rope(x, ang_row, ang_col):
    xr, xc = jnp.split(x, 2, axis=-1)
    return jnp.concatenate([rope_rotate(xr, ang_row), rope_rotate(xc, ang_col)], axis=-1)


def short_conv_mixer(u_sc, w_sc):
    x_in, b_gate, c_gate = jnp.split(u_sc, 3, axis=-1)
    return b_gate * dwconv(c_gate * x_in, w_sc)


def conformer_conv_mixer(u_cf, w_dw, b_dw, ln_g, ln_b, w_pw):
    a, g = jnp.split(u_cf, 2, axis=-1)
    y = dwconv(a * jax.nn.sigmoid(g), w_dw) + b_dw
    return jax.nn.silu(layernorm(y, ln_g, ln_b)) @ w_pw


def mla_queries(cq, g_q, w_uq):
    q = heads(rmsnorm(cq, g_q) @ w_uq)
    return q[..., :MLA_NOPE], q[..., MLA_NOPE:]


def mla_keys_values(u_kv, g_kv, w_ukv):
    ckv, k_rope = u_kv[..., :MLA_KV_RANK], u_kv[..., MLA_KV_RANK:]
    kv = heads(rmsnorm(ckv, g_kv) @ w_ukv)
    return kv[..., :MLA_NOPE], k_rope, kv[..., MLA_NOPE:]


def mla_attend(qn, qr, kn, kr, v):
    s = (jnp.einsum("bqhd,bkhd->bhqk", qn, kn)
         + jnp.einsum("bqhr,bkr->bhqk", qr, kr)) * MLA_SCALE
    p = jax.nn.softmax(s.astype(jnp.float32), axis=-1).astype(v.dtype)
    return jnp.einsum("bhqk,bkhd->bqhd", p, v)


def mla_latent(qn, qr, kn, kr, v):
    b, l = qn.shape[:2]
    nb = l // Q_BLOCK

    def to_blocks(t):
        return jnp.moveaxis(t.reshape(b, nb, Q_BLOCK, *t.shape[2:]), 1, 0)

    out = lax.map(lambda qs: mla_attend(qs[0], qs[1], kn, kr, v), (to_blocks(qn), to_blocks(qr)))
    return jnp.moveaxis(out, 0, 1).reshape(b, l, -1)


def dense_attend(q, k, v):
    s = jnp.einsum("bqhd,bkhd->bhqk", q, k) * NA_SCALE
    p = jax.nn.softmax(s.astype(jnp.float32), axis=-1).astype(v.dtype)
    o = jnp.einsum("bhqk,bkhd->bqhd", p, v)
    return o.reshape(*o.shape[:2], -1)


def neighbourhood_attend(q, k, v, k_c, v_c, rpb):
    b, l, h, dh = q.shape
    rows = l // GRID_W
    wh = min(NA_ROWS, rows)
    r = jnp.arange(rows)
    row_idx = jnp.clip(r - wh // 2, 0, rows - wh)[:, None] + jnp.arange(wh)[None, :]
    col = jnp.arange(GRID_W)
    col_start = jnp.clip(col - NA_COLS // 2, 0, GRID_W - NA_COLS)
    in_win = (col[None, :] >= col_start[:, None]) & (col[None, :] < col_start[:, None] + NA_COLS)
    off_r = row_idx - r[:, None] + (NA_ROWS - 1)
    off_c = jnp.clip(col[None, :] - col[:, None] + (NA_COLS - 1), 0, 2 * NA_COLS - 2)
    bias = rpb[:, off_r[:, None, :, None], off_c[None, :, None, :]]

    qg = q.reshape(b, rows, GRID_W, h, dh)
    kw = k.reshape(b, rows, GRID_W, h, dh)[:, row_idx]
    vw = v.reshape(b, rows, GRID_W, h, dh)[:, row_idx]
    s_loc = (jnp.einsum("brqhd,brikhd->bhrqik", qg, kw).astype(jnp.float32) * NA_SCALE
             + bias.astype(jnp.float32)[None])
    s_loc = jnp.where(in_win[:, None, :], s_loc, -jnp.inf)
    s_ctx = jnp.einsum("brqhd,bchd->bhrqc", qg, k_c).astype(jnp.float32) * NA_SCALE
    n_loc = wh * GRID_W
    p = jax.nn.softmax(
        jnp.concatenate([s_loc.reshape(b, h, rows, GRID_W, n_loc), s_ctx], axis=-1),
        axis=-1).astype(v.dtype)
    out = (jnp.einsum("bhrqik,brikhd->brqhd",
                      p[..., :n_loc].reshape(b, h, rows, GRID_W, wh, GRID_W), vw)
           + jnp.einsum("bhrqc,bchd->brqhd", p[..., n_loc:], v_c))
    return out.reshape(b, l, h * dh)


def sq_relu_mlp(h, w1, w2):
    return jnp.square(jax.nn.relu(h @ w1)) @ w2


def trunk_layer(x, xc, mod, mod_c, p, ang_row, ang_col, last):
    sh1, sc1, ga1, sh2, sc2, ga2 = mod
    sh1c, sc1c, ga1c, sh2c, sc2c, ga2c = mod_c
    w_in = p["w_in"]

    h = rmsnorm(x, p["g_pre_mix"]) * (1 + sc1) + sh1
    hc = rmsnorm(xc, p["g_pre_mix"]) * (1 + sc1c) + sh1c
    u = h @ w_in
    if last:
        uc_mla_kv = hc @ w_in[:, MLA_Q_END:MLA_KV_END]
        uc_na_kv = hc @ w_in[:, NA_Q_END:]
    else:
        uc = hc @ w_in
        uc_mla_kv, uc_na_kv = uc[..., MLA_Q_END:MLA_KV_END], uc[..., NA_Q_END:]

    kn_c, kr_c, v_c = mla_keys_values(uc_mla_kv, p["g_kv"], p["w_ukv"])
    nk_c, nv_c = (heads(t) for t in jnp.split(uc_na_kv, 2, axis=-1))

    y_sc = short_conv_mixer(u[..., :SC_END], p["w_sc"])
    qn, qr = mla_queries(u[..., SC_END:MLA_Q_END], p["g_q"], p["w_uq"])
    qr = axial_rope(qr, ang_row[:, None], ang_col[:, None])
    kn, kr, v = mla_keys_values(u[..., MLA_Q_END:MLA_KV_END], p["g_kv"], p["w_ukv"])
    kr = axial_rope(kr, ang_row, ang_col)
    y_mla = mla_latent(qn, qr, jnp.concatenate([kn, kn_c], axis=1),
                       jnp.concatenate([kr, kr_c], axis=1), jnp.concatenate([v, v_c], axis=1))
    y_cf = conformer_conv_mixer(u[..., MLA_KV_END:CF_END], p["cf_w_dw"], p["cf_b_dw"],
                                p["cf_ln_g"], p["cf_ln_b"], p["cf_w_pw"])
    nq, nk, nv = (heads(t) for t in jnp.split(u[..., CF_END:], 3, axis=-1))
    y_na = neighbourhood_attend(nq, nk, nv, nk_c, nv_c, p["na_rpb"])
    y = jnp.concatenate([y_sc, y_mla, y_cf, y_na], axis=-1) @ p["w_out"]
    x = x + ga1 * rmsnorm(y, p["g_post_mix"])

    if not last:
        yc_sc = short_conv_mixer(uc[..., :SC_END], p["w_sc"])
        qn_c, qr_c = mla_queries(uc[..., SC_END:MLA_Q_END], p["g_q"], p["w_uq"])
        yc_mla = mla_attend(qn_c, qr_c, kn_c, kr_c, v_c)
        yc_mla = yc_mla.reshape(*yc_mla.shape[:2], -1)
        yc_cf = conformer_conv_mixer(uc[..., MLA_KV_END:CF_END], p["cf_w_dw"], p["cf_b_dw"],
                                     p["cf_ln_g"], p["cf_ln_b"], p["cf_w_pw"])
        yc_na = dense_attend(heads(uc[..., CF_END:NA_Q_END]), nk_c, nv_c)
        yc = jnp.concatenate([yc_sc, yc_mla, yc_cf, yc_na], axis=-1) @ p["w_out"]
        xc = xc + ga1c * rmsnorm(yc, p["g_post_mix"])

    hm = rmsnorm(x, p["g_pre_mlp"]) * (1 + sc2) + sh2
    x = x + ga2 * rmsnorm(sq_relu_mlp(hm, p["w_mlp1"], p["w_mlp2"]), p["g_post_mlp"])
    if not last:
        hmc = rmsnorm(xc, p["g_pre_mlp"]) * (1 + sc2c) + sh2c
        xc = xc + ga2c * rmsnorm(sq_relu_mlp(hmc, p["w_mlp1"], p["w_mlp2"]), p["g_post_mlp"])
    return x, xc


def setup_inputs(seed: int = 0) -> dict:
    key = jax.random.key(seed)
    ks = jax.random.split(key, 25)
    f32 = jnp.float32

    def nrm(k, shape, s):
        return jax.random.normal(k, shape, f32) * s

    def gain(k, shape):
        return 1.0 + 0.1 * jax.random.normal(k, shape, f32)

    L, H = DEPTH, N_HEADS_GROUP
    return {
        "x": nrm(ks[0], (BATCH, SEQ, D_MODEL), 1.0),
        "c": nrm(ks[1], (BATCH, D_MODEL), 1.0),
        "ctx": nrm(ks[2], (BATCH, CTX_LEN, D_MODEL), 1.0),
        "c_ctx": nrm(ks[3], (D_MODEL,), 1.0),
        "w_mod": nrm(ks[4], (L, D_MODEL, N_MOD * D_MODEL), 0.5 * D_MODEL ** -0.5),
        "b_mod": nrm(ks[5], (L, N_MOD * D_MODEL), 0.02),
        "g_pre_mix": gain(ks[6], (L, D_MODEL)),
        "w_in": nrm(ks[7], (L, D_MODEL, P_IN), D_MODEL ** -0.5),
        "w_sc": nrm(ks[8], (L, SC_KERNEL, GROUP_W), SC_KERNEL ** -0.5),
        "g_q": gain(ks[9], (L, MLA_Q_RANK)),
        "w_uq": nrm(ks[10], (L, MLA_Q_RANK, H * (MLA_NOPE + MLA_ROPE)), MLA_Q_RANK ** -0.5),
        "g_kv": gain(ks[11], (L, MLA_KV_RANK)),
        "w_ukv": nrm(ks[12], (L, MLA_KV_RANK, H * (MLA_NOPE + MLA_V)), MLA_KV_RANK ** -0.5),
        "cf_w_dw": nrm(ks[13], (L, CF_KERNEL, GROUP_W), CF_KERNEL ** -0.5),
        "cf_b_dw": nrm(ks[14], (L, GROUP_W), 0.02),
        "cf_ln_g": gain(ks[15], (L, GROUP_W)),
        "cf_ln_b": nrm(ks[16], (L, GROUP_W), 0.02),
        "cf_w_pw": nrm(ks[17], (L, GROUP_W, GROUP_W), GROUP_W ** -0.5),
        "na_rpb": nrm(ks[18], (L, H, 2 * NA_ROWS - 1, 2 * NA_COLS - 1), 0.1),
        "w_out": nrm(ks[19], (L, D_MODEL, D_MODEL), D_MODEL ** -0.5),
        "g_post_mix": gain(ks[20], (L, D_MODEL)),
        "g_pre_mlp": gain(ks[21], (L, D_MODEL)),
        "w_mlp1": nrm(ks[22], (L, D_MODEL, MLP_HIDDEN), D_MODEL ** -0.5),
        "w_mlp2": nrm(ks[23], (L, MLP_HIDDEN, D_MODEL), MLP_HIDDEN ** -0.5),
        "g_post_mlp": gain(ks[24], (L, D_MODEL)),
    }


def reference(x, c, ctx, c_ctx, w_mod, b_mod, g_pre_mix, w_in, w_sc, g_q, w_uq, g_kv, w_ukv,
              cf_w_dw, cf_b_dw, cf_ln_g, cf_ln_b, cf_w_pw, na_rpb, w_out, g_post_mix,
              g_pre_mlp, w_mlp1, w_mlp2, g_post_mlp):
    l = x.shape[1]
    pos = jnp.arange(l)
    n_freq = MLA_ROPE // 4
    inv_freq = ROPE_BASE ** (-jnp.arange(n_freq, dtype=jnp.float32) / n_freq)
    ang_row = (pos // GRID_W).astype(jnp.float32)[:, None] * inv_freq
    ang_col = (pos % GRID_W).astype(jnp.float32)[:, None] * inv_freq

    xc = ctx
    for i in range(DEPTH):
        mod = [m[:, None, :] for m in jnp.split(jax.nn.silu(c) @ w_mod[i] + b_mod[i], N_MOD, axis=-1)]
        mod_c = jnp.split(jax.nn.silu(c_ctx) @ w_mod[i] + b_mod[i], N_MOD, axis=-1)
        p = {
            "g_pre_mix": g_pre_mix[i], "w_in": w_in[i], "w_sc": w_sc[i],
            "g_q": g_q[i], "w_uq": w_uq[i], "g_kv": g_kv[i], "w_ukv": w_ukv[i],
            "cf_w_dw": cf_w_dw[i], "cf_b_dw": cf_b_dw[i], "cf_ln_g": cf_ln_g[i],
            "cf_ln_b": cf_ln_b[i], "cf_w_pw": cf_w_pw[i], "na_rpb": na_rpb[i],
            "w_out": w_out[i], "g_post_mix": g_post_mix[i], "g_pre_mlp": g_pre_mlp[i],
            "w_mlp1": w_mlp1[i], "w_mlp2": w_mlp2[i], "g_post_mlp": g_post_mlp[i],
        }
        x, xc = trunk_layer(x, xc, mod, mod_c, p, ang_row, ang_col, last=(i == DEPTH - 1))
    return x
```

```python
import bisect
import contextlib
import numpy as np
import ml_dtypes
import concourse.bass as bass
import concourse.mybir as mybir
from concourse.bass_utils import run_bass_kernel_spmd

F32 = mybir.dt.float32
BF16 = mybir.dt.bfloat16
AF = mybir.ActivationFunctionType
ALU = mybir.AluOpType

D = 1024
T = 2048
TC = 256
TA = T + TC
NEG = -30000.0
MLA_SCALE = 96.0 ** -0.5
EPS = 1e-6
BLOCKS = [(0, 512, 0), (512, 512, 0), (1024, 512, 0), (1536, 512, 0), (2048, 256, 1)]

C_GPRE1, C_GPOST1, C_GPRE2, C_GPOST2, C_BMOD = 0, 8, 16, 24, 32
C_GQ, C_GKV, C_WSC, C_WDW, C_BDW, C_LNG, C_LNB, C_EPS = 80, 82, 83, 89, 151, 153, 155, 157
NCOL = 160
K_C, K_CC, K_ML, K_MR, K_MB = 0, 8, 16, 17, 18


def na_slots(j):
    if j == 0:
        return list(range(-2, 4))
    if j == 15:
        return list(range(-3, 3))
    return list(range(-2, 3))


NA_IDX = {}
for _j in range(16):
    for _d in na_slots(_j):
        for _q in range(2):
            NA_IDX[(_j, _d, _q)] = len(NA_IDX)
NCC = K_MB + len(NA_IDX)


class Buf:
    __slots__ = ("name", "w", "r")
    _n = 0

    def __init__(self, name="b"):
        Buf._n += 1
        self.name = "%s_%d" % (name, Buf._n)
        self.w = None
        self.r = []


class Sched:
    def __init__(self, nc, stack):
        self.nc = nc
        self.stack = stack
        self.h = {"pe": nc.tensor, "act": nc.scalar, "dve": nc.vector, "pool": nc.gpsimd, "sp": nc.sync}
        self.sem, self.cnt, self.seen, self.hist = {}, {}, {}, {}
        for e in self.h:
            self.sem[e] = stack.enter_context(nc.semaphore("s_" + e))
            self.cnt[e] = 0
            self.seen[e] = {}
            self.hist[e] = ([0], [{}])
        self.dclock = {}
        self.free = []
        self.free_sw = []
        self.sw_keys = set()
        self.gen = {}
        self._mark = set()
        self.nwait = 0
        self.nins = 0
        self.uid = 0

    def _clock_of(self, ev):
        k, v = ev
        if k in self.h:
            cs, ds = self.hist[k]
            return ds[bisect.bisect_right(cs, v) - 1]
        return self.dclock.get(ev, {})

    def _wait(self, en, deps):
        seen = self.seen[en]
        changed = False
        for k, v in deps.items():
            if k == en and en == "pe":
                continue
            if k not in self.sem:
                continue
            if seen.get(k, 0) >= v:
                continue
            self.h[en].wait_ge(self.sem[k], v)
            self.nwait += 1
            changed = True
            seen[k] = v
            for k2, v2 in self._clock_of((k, v)).items():
                if seen.get(k2, 0) < v2:
                    seen[k2] = v2
        if changed:
            cs, ds = self.hist[en]
            cs.append(self.cnt[en] + 1)
            ds.append(dict(seen))

    def _sync(self, en, reads, writes, skip_key=None):
        deps = {}

        def add(ev):
            if ev is not None and ev[0] != skip_key and deps.get(ev[0], 0) < ev[1]:
                deps[ev[0]] = ev[1]

        for b in reads:
            add(b.w)
        for b in writes:
            add(b.w)
            for r in b.r:
                add(r)
        self._wait(en, deps)

    def _record(self, ev, reads, writes):
        for b in reads:
            b.r.append(ev)
        for b in writes:
            b.w = ev
            b.r = []

    def op(self, en, fn, reads=(), writes=()):
        self._sync(en, reads, writes)
        ins = fn(self.h[en])
        self.cnt[en] += 1
        ins.then_inc(self.sem[en], 1)
        self._record((en, self.cnt[en]), reads, writes)
        self.nins += 1

    def dma(self, out, in_, reads=(), writes=(), qn="sp", slow=False):
        kb = ("ld_" + writes[0].name) if writes else ("st_" + reads[0].name)
        key = "%s#%d" % (kb, self.gen.get(kb, 0))
        self._sync(qn, reads, writes)
        if key not in self.sem:
            fl = self.free_sw if qn == "pool" else self.free
            if qn == "pool":
                self.sw_keys.add(key)
            if fl:
                self.sem[key], self.cnt[key] = fl.pop()
            else:
                self.uid += 1
                self.sem[key] = self.stack.enter_context(self.nc.semaphore("d%d" % self.uid))
                self.cnt[key] = 0
        self.cnt[key] += 16
        if slow:
            self.h[qn].dma_start(out=out, in_=in_, allow_slow_non_contiguous=True).then_inc(self.sem[key], 16)
        else:
            self.h[qn].dma_start(out=out, in_=in_).then_inc(self.sem[key], 16)
        ev = (key, self.cnt[key])
        self.dclock[ev] = dict(self.seen[qn])
        self._record(ev, reads, writes)
        self.nins += 1

    def collective(self, ins_ap, outs_ap, groups, wait=True):
        if wait:
            self._wait("pool", {k: v for k, v in self.cnt.items() if v > 0 and k != "pool"})
        if "cc" not in self.sem:
            self.sem["cc"] = self.stack.enter_context(self.nc.semaphore("cc"))
            self.cnt["cc"] = 0
        self.cnt["cc"] += 1
        self.nc.gpsimd.collective_compute("AllGather", ALU.bypass, replica_groups=groups, ins=[ins_ap.opt()], outs=[outs_ap.opt()]).then_inc(self.sem["cc"], 1)
        self.dclock[("cc", self.cnt["cc"])] = dict(self.seen["pool"])
        self.nins += 1

    def mark(self):
        self._mark = set(self.sem.keys())

    def release(self):
        for k in list(self.sem.keys()):
            if k not in self._mark and k != "cc":
                (self.free_sw if k in self.sw_keys else self.free).append((self.sem.pop(k), self.cnt.pop(k)))
                self.sw_keys.discard(k)
                kb = k.rsplit("#", 1)[0]
                self.gen[kb] = self.gen.get(kb, 0) + 1
                for e in self.h:
                    self.seen[e].pop(k, None)

    def barrier(self, engines=("pe", "act", "dve", "pool", "sp")):
        for e in engines:
            deps = {k: v for k, v in self.cnt.items() if v > 0 and k != e}
            self._wait(e, deps)


class Prog:
    def __init__(self):
        self.nc = bass.Bass("TRN2", target_bir_lowering=False)
        self.n = 0
        self.io = {}

    def dram(self, name, shape, dt, kind):
        self.io[name] = (tuple(shape), dt, kind)
        if kind == "int":
            return self.nc.dram_tensor(name, list(shape), dt).ap()
        k = {"in": "ExternalInput", "out": "ExternalOutput"}[kind]
        return self.nc.dram_tensor(name, list(shape), dt, kind=k).ap()

    def sb(self, st, name, shape, dt):
        self.n += 1
        t = st.enter_context(self.nc.sbuf_tensor("%s_%d" % (name, self.n), list(shape), dt))
        return t, Buf(name)

    def ps(self, st, name, shape, dt):
        self.n += 1
        t = st.enter_context(self.nc.psum_tensor("%s_%d" % (name, self.n), list(shape), dt))
        return t, Buf(name)


class Ring:
    def __init__(self, items):
        self.items = items
        self.i = 0

    def next(self):
        it = self.items[self.i % len(self.items)]
        self.i += 1
        return it


def make_psum(P, st):
    f = Ring([P.ps(st, "pf", [128, 512], F32) for _ in range(7)])
    b = Ring([P.ps(st, "pb", [128, 1024], BF16) for _ in range(1)])
    return f, b


def load_cast(P, S, st, name, dram_ap_fn, nk, ncols, stg_ring=None, chunk=512, pre=None, ks=8, per_k=False):
    wb = pre if pre is not None else P.sb(st, name, [128, nk, ncols], BF16)[0]
    bufs = []
    for c0 in range(0, ncols, chunk):
        cw = min(chunk, ncols - c0)
        bb = Buf(name + "c")
        bl = []
        for k0 in range(0, nk, ks):
            kn = min(ks, nk - k0)
            if per_k:
                bb = Buf(name + "ck")
                bl.append(bb)
            S.dma(wb[:, k0:k0 + kn, c0:c0 + cw], dram_ap_fn(k0, kn, c0, cw), writes=[bb], qn="pool")
        bufs.append(bl if per_k else bb)
    return wb, bufs, chunk


def wsrc(ap):
    return lambda k0, kn, c0, cw: ap[k0 * 128:(k0 + kn) * 128, c0:c0 + cw].rearrange("(k p) n -> p k n", p=128)


def wbuf(bufs, chunk, c0, c1):
    return bufs[c0 // chunk:(c1 - 1) // chunk + 1]


class XLoader:
    def __init__(self, S, ring):
        self.S, self.ring, self.pending = S, ring, {}

    def prefetch(self, key, src):
        if key is None or key in self.pending:
            return
        xt, b = self.ring.next()
        self.S.dma(xt[:], src, writes=[b], qn="act")
        self.pending[key] = (xt, b)

    def get(self, key, src):
        self.prefetch(key, src)
        return self.pending.pop(key)


def prologue_a(P, S, x_src, xl, wk, cp, b_cp, key=None, nxt=None):
    xt, b_xt = xl.get(key, x_src)
    junk, b_junk, ss, b_ss, xn, b_xn = wk
    S.op("act", lambda e: e.activation(out=junk[:], in_=xt[:], func=AF.Square, accum_out=ss[:, 0:1]), reads=[b_xt], writes=[b_junk, b_ss])
    S.op("act", lambda e: e.activation(out=ss[:, 1:2], in_=ss[:, 0:1], func=AF.Sqrt, scale=1.0 / D, bias=cp[:, C_EPS:C_EPS + 1]), reads=[b_ss, b_cp], writes=[b_ss])
    S.op("dve", lambda e: e.reciprocal(out=ss[:, 2:3], in_=ss[:, 1:2]), reads=[b_ss], writes=[b_ss])
    S.op("dve", lambda e: e.tensor_scalar(out=xn[:], in0=xt[:], scalar1=ss[:, 2:3], scalar2=None, op0=ALU.mult), reads=[b_xt, b_ss], writes=[b_xn])
    if nxt is not None:
        xl.prefetch(nxt[0], nxt[1])


def prologue_b(P, S, wk, ident_b, b_ident, psb_ring, hT, b_hT, col_off, A_col, Sh_col, b_mod):
    junk, b_junk, ss, b_ss, xn, b_xn = wk
    pt, b_pt = psb_ring.next()
    for k in range(8):
        S.op("pe", lambda e: e.transpose(out=pt[:, k * 128:(k + 1) * 128], in_=xn[:, k * 128:(k + 1) * 128], identity=ident_b[:]), reads=[b_xn, b_ident], writes=[b_pt])
    for k in range(8):
        if k % 2 == 0:
            S.op("act", lambda e: e.activation(out=hT[:, k, col_off:col_off + 128], in_=pt[:, k * 128:(k + 1) * 128], func=AF.Identity,
                                               scale=A_col(k), bias=Sh_col(k)), reads=[b_pt, b_mod], writes=[b_hT])
        else:
            S.op("dve", lambda e: e.tensor_scalar(out=hT[:, k, col_off:col_off + 128], in0=pt[:, k * 128:(k + 1) * 128], scalar1=A_col(k), scalar2=Sh_col(k),
                                                  op0=ALU.mult, op1=ALU.add), reads=[b_pt, b_mod], writes=[b_hT])


class ProPipe:
    def __init__(self):
        self.pending = None

    def step(self, a_fn, b_fn):
        if self.pending is not None:
            self.pending()
        self.pending = None
        if a_fn is not None:
            a_fn()
            self.pending = b_fn

    def flush(self):
        self.step(None, None)


def fm_mm(S, ps, b_ps, w, wb_list, hT, b_hT, col0, ncols, W, nk=8):
    for k in range(nk):
        S.op("pe", lambda e: e.matmul(out=ps[0:ncols, 0:W], lhsT=w[:, k, col0:col0 + ncols], rhs=hT[:, k, 0:W], start=(k == 0), stop=(k == nk - 1)),
             reads=list(wb_list) + [b_hT], writes=[b_ps])


def load_consts(P, S, st, d):
    cp, b_cp = P.sb(st, "cp", [128, NCOL], F32)
    S.dma(cp[:], d["colpack"][:, :], writes=[b_cp])
    kp, b_kp = P.sb(st, "kp", [128, NCC], F32)
    S.dma(kp[:], d["corepack"][:, :], writes=[b_kp])
    idf, b_idf = P.sb(st, "idf", [128, 128], F32)
    S.dma(idf[:], d["ident"][:, :], writes=[b_idf])
    idb, b_idb = P.sb(st, "idb", [128, 128], BF16)
    S.op("dve", lambda e: e.tensor_copy(out=idb[:], in_=idf[:]), reads=[b_idf], writes=[b_idb])
    onef, b_onef = P.sb(st, "onef", [128, 128], F32)
    S.op("dve", lambda e: e.memset(onef[:], 1.0), writes=[b_onef])
    return dict(cp=cp, b_cp=b_cp, kp=kp, b_kp=b_kp, idf=idf, b_idf=b_idf, idb=idb, b_idb=b_idb, onef=onef, b_onef=b_onef)


def p0_steps(P, S, st, d, kp, b_kp, pm, b_pm, mod, b_mod, A1, b_A1, split=False, gcols=512, nbuf=3):
    sT, b_sT = P.sb(st, "sT", [128, 8, 2], BF16)
    cpn, b_cpn = P.sb(st, "cpn", [128, NCOL], F32)
    stg = Ring([P.sb(st, "wmstg", [128, 8, gcols], BF16) for _ in range(nbuf)])
    tiles = {}
    ng = 6 * D // gcols

    def load(g):
        if g < ng and g not in tiles:
            tiles[g] = stg.next()
            S.dma(tiles[g][0][:], d["wmod"][:, g * gcols:(g + 1) * gcols].rearrange("(k p) n -> p k n", p=128), writes=[tiles[g][1]], qn="pool")

    def group(g):
        if g == 0:
            S.dma(cpn[:], d["colpack"][:, :], writes=[b_cpn])
            S.op("act", lambda e: e.activation(out=sT[:, :, 0], in_=kp[:, K_C:K_C + 8], func=AF.Silu), reads=[b_kp], writes=[b_sT])
            S.op("act", lambda e: e.activation(out=sT[:, :, 1], in_=kp[:, K_CC:K_CC + 8], func=AF.Silu), reads=[b_kp], writes=[b_sT])
            for g_ in range(nbuf - 1):
                load(g_)
        load(g)
        wt, b_wt = tiles[g]
        for cc in range(gcols // 128):
            ch = g * (gcols // 128) + cc
            for k in range(8):
                S.op("pe", lambda e: e.matmul(out=pm[:, ch * 2:ch * 2 + 2], lhsT=wt[:, k, cc * 128:(cc + 1) * 128], rhs=sT[:, k, :],
                                              start=(k == 0), stop=(k == 7)), reads=[b_wt, b_sT], writes=[b_pm])
        load(g + nbuf - 1)

    def fin(c0=0, c1=48, a1=True):
        for j in range(2):
            S.op("dve", lambda e: e.tensor_tensor(out=mod[:, c0:c1, j], in0=pm[:, 0:96].rearrange("p (c j) -> p c j", j=2)[:, c0:c1, j],
                                                  in1=cpn[:, C_BMOD + c0:C_BMOD + c1], op=ALU.add), reads=[b_pm, b_cpn], writes=[b_mod])
        if a1:
            for j in range(2):
                S.op("dve", lambda e: e.scalar_tensor_tensor(out=A1[:, :, j], in0=mod[:, 8:16, j], scalar=1.0, in1=cpn[:, C_GPRE1:C_GPRE1 + 8],
                                                             op0=ALU.add, op1=ALU.mult), reads=[b_mod, b_cpn], writes=[b_A1])

    groups = [(lambda g=g: group(g)) for g in range(ng)]
    if split:
        n1 = 2048 // gcols
        return groups[:n1] + [lambda: fin(0, 16, True)], groups[n1:] + [lambda: fin(16, 48, False)]
    return groups + [fin]


def emit_A(P, S, st0, d, K, psf, psb, modt, need_p0, after_latent=None):
    if True:
        cp, b_cp, kp, b_kp = K["cp"], K["b_cp"], K["kp"], K["b_kp"]
        mod, b_mod, A1, b_A1 = modt
        S.mark()
        with contextlib.ExitStack() as st:
            late_p0 = []
            if need_p0:
                early, late_p0 = p0_steps(P, S, st, d, kp, b_kp, psf.items[0][0], psf.items[0][1], mod, b_mod, A1, b_A1, split=True)
                for step in early:
                    step()
                psf = Ring(psf.items[1:])
            stg = None
            w, wbs, wch = load_cast(P, S, st, "winx", wsrc(d["winx"]), 8, 2560, stg)
            wuq, wuq_b, _ = load_cast(P, S, st, "wuq", wsrc(d["wuq"]), 2, 384, stg)
            wuqs, wuqs_b, _ = load_cast(P, S, st, "wuqs", wsrc(d["wuqs"]), 2, 384, stg)
            wkk, wkk_b, _ = load_cast(P, S, st, "wukvk", wsrc(d["wukvk"]), 1, 256, stg)
            wkv, wkv_b, _ = load_cast(P, S, st, "wukvv", wsrc(d["wukvv"]), 1, 256, stg)
            cosT, b_cos = P.sb(st, "cos", [96, TA], F32)
            sinT, b_sin = P.sb(st, "sin", [96, TA], F32)
            S.dma(cosT[64:96, :], d["rope"][0, :, :], writes=[b_cos])
            S.dma(sinT[64:96, :], d["rope"][1, :, :], writes=[b_sin])
            xt_ring = Ring([P.sb(st, "xt", [128, D], F32) for _ in range(2)])
            junk, b_junk = P.sb(st, "junk", [128, D], F32)
            ss, b_ss = P.sb(st, "ss", [128, 4], F32)
            xn, b_xn = P.sb(st, "xn", [128, D], BF16)
            wk = (junk, b_junk, ss, b_ss, xn, b_xn)
            hT_ring = Ring([P.sb(st, "hT", [128, 8, 512], BF16) for _ in range(2)])
            f32r = Ring([P.sb(st, "of", [128, 512], F32) for _ in range(6)])
            bfr = Ring([P.sb(st, "ob", [128, 512], BF16) for _ in range(6)])
            nvr = Ring([P.sb(st, "onv", [128, 272], BF16) for _ in range(3)])
            mvr = Ring([P.sb(st, "omv", [128, 4, 128], BF16) for _ in range(3)])
            for t_, b_ in nvr.items:
                S.op("pool", lambda e: e.memset(t_[:, 260:272], 0.0), writes=[b_])
                S.op("pool", lambda e: e.memset(t_[:, 0:260].rearrange("p (h e) -> p h e", h=4)[:, :, 64:65], 1.0), writes=[b_])
            for t_, b_ in mvr.items:
                S.op("pool", lambda e: e.memset(t_[:, :, 64:128], 1.0), writes=[b_])
            cqf = [P.sb(st, "cqf", [128, 512], F32) for _ in range(2)]
            sqf = [P.sb(st, "sqf", [128, 512], F32) for _ in range(2)]
            cqn = [P.sb(st, "cqn", [128, 512], BF16) for _ in range(2)]
            rs, b_rs = P.sb(st, "rs", [128, 512], F32)
            ropt_ring = Ring([[P.sb(st, "ropt", [96, 512], F32) for _ in range(2)] for _ in range(2)])
            kT = [P.sb(st, "kT", [96, 512], BF16) for _ in range(4)]
            qTr = Ring([P.sb(st, "qT", [96, 512], BF16) for _ in range(2)])

            ckf = [P.sb(st, "ckf", [128, 512], F32)]
            skf = [P.sb(st, "skf", [128, 512], F32)]
            ckn = [P.sb(st, "ckn", [128, 512], BF16)]
            rs2 = P.sb(st, "rs2", [128, 512], F32)

            def rms_fm(srcs, nfeat, gcol0, outs, W, cqf=cqf, sqf=sqf, rsb=(rs, b_rs)):
                rs, b_rs = rsb
                n = len(srcs)
                for c, (ps_, b_ps_) in enumerate(srcs):
                    S.op("act", lambda e: e.activation(out=cqf[c][0][:, 0:W], in_=ps_[:, 0:W], func=AF.Copy), reads=[b_ps_], writes=[cqf[c][1]])
                    S.op("act", lambda e: e.activation(out=sqf[c][0][:, 0:W], in_=ps_[:, 0:W], func=AF.Square), reads=[b_ps_], writes=[sqf[c][1]])
                pS, b_pS = psf.next()
                for c in range(n):
                    S.op("pe", lambda e: e.matmul(out=pS[:, 0:W], lhsT=K["onef"][:, :], rhs=sqf[c][0][:, 0:W], start=(c == 0), stop=(c == n - 1)),
                         reads=[K["b_onef"], sqf[c][1]], writes=[b_pS])
                S.op("act", lambda e: e.activation(out=rs[:, 0:W], in_=pS[:, 0:W], func=AF.Sqrt, scale=1.0 / nfeat, bias=cp[:, C_EPS:C_EPS + 1]), reads=[b_pS, b_cp], writes=[b_rs])
                S.op("dve", lambda e: e.reciprocal(out=rs[:, 0:W], in_=rs[:, 0:W]), reads=[b_rs], writes=[b_rs])
                for c in range(n):
                    S.op("dve", lambda e: e.scalar_tensor_tensor(out=outs[c][0][:, 0:W], in0=cqf[c][0][:, 0:W], scalar=cp[:, gcol0 + c:gcol0 + c + 1], in1=rs[:, 0:W],
                                                                 op0=ALU.mult, op1=ALU.mult), reads=[cqf[c][1], b_cp, b_rs], writes=[outs[c][1]])

            def rope96(pa, b_pa, pb, b_pb, t0, W, out_tile, out_buf):
                (r0, b_r0), (r1, b_r1) = ropt_ring.next()
                S.op("dve", lambda e: e.tensor_tensor(out=r0[64:96, 0:W], in0=pa[64:96, 0:W], in1=cosT[64:96, t0:t0 + W], op=ALU.mult), reads=[b_pa, b_cos], writes=[b_r0])
                S.op("dve", lambda e: e.tensor_tensor(out=r1[64:96, 0:W], in0=pb[64:96, 0:W], in1=sinT[64:96, t0:t0 + W], op=ALU.mult), reads=[b_pb, b_sin], writes=[b_r1])
                S.op("pool", lambda e: e.tensor_tensor(out=out_tile[64:96, 0:W], in0=r0[64:96, 0:W], in1=r1[64:96, 0:W], op=ALU.add), reads=[b_r0, b_r1], writes=[out_buf])

            hTs = {}
            xl = XLoader(S, xt_ring)
            pseq = [(bi_, sub_) for bi_, (t0_, W_, seg_) in enumerate(BLOCKS) for sub_ in range(W_ // 128)]

            pp = ProPipe()

            def pro(bi, sub):
                if bi >= len(BLOCKS):
                    pp.flush()
                    return
                t0_, W_, seg_ = BLOCKS[bi]
                if sub >= W_ // 128:
                    pp.flush()
                    return
                if bi not in hTs:
                    hTs[bi] = hT_ring.next()
                hT_, b_hT_ = hTs[bi]
                i_ = pseq.index((bi, sub))
                nxt = None
                if i_ + 1 < len(pseq):
                    nb_, ns_ = pseq[i_ + 1]
                    nt0 = BLOCKS[nb_][0] + ns_ * 128
                    nxt = ((nb_, ns_), d["xin"][nt0:nt0 + 128, :])
                pp.step(lambda: prologue_a(P, S, d["xin"][t0_ + sub * 128:t0_ + (sub + 1) * 128, :], xl, wk, cp, b_cp, key=(bi, sub), nxt=nxt),
                        lambda: prologue_b(P, S, wk, K["idb"], K["b_idb"], psb, hT_, b_hT_, sub * 128,
                                           lambda k: A1[:, k, seg_:seg_ + 1], lambda k: mod[:, k, seg_:seg_ + 1], b_A1))

            for sub in range(4):
                pro(0, sub)
            pp.flush()
            for bi, (t0, W, seg) in enumerate(BLOCKS):
                hT, b_hT = hTs[bi]
                srcs = []
                for c in range(2):
                    pc_, b_pc_ = psf.next()
                    fm_mm(S, pc_, b_pc_, w, wbuf(wbs, wch, 768 + c * 128, 768 + c * 128 + 128), hT, b_hT, 768 + c * 128, 128, W)
                    srcs.append((pc_, b_pc_))
                rms_fm(srcs, 256, C_GQ, cqn, W)
                pc_, b_pc_ = psf.next()
                fm_mm(S, pc_, b_pc_, w, wbuf(wbs, wch, 1024, 1152), hT, b_hT, 1024, 128, W)
                rms_fm([(pc_, b_pc_)], 128, C_GKV, ckn, W, cqf=ckf, sqf=skf, rsb=rs2)
                ckvn, b_ckvn = ckn[0]
                pa, b_pa = psf.next()
                fm_mm(S, pa, b_pa, w, wbuf(wbs, wch, 1088, 1184), hT, b_hT, 1088, 96, W)
                pb, b_pb = psf.next()
                fm_mm(S, pb, b_pb, w, wbuf(wbs, wch, 2464, 2560), hT, b_hT, 2464, 96, W)
                rope96(pa, b_pa, pb, b_pb, t0, W, kT[0][0], kT[0][1])
                for h in range(1, 4):
                    S.op("pool", lambda e: e.tensor_copy(out=kT[h][0][64:96, 0:W], in_=kT[0][0][64:96, 0:W]), reads=[kT[0][1]], writes=[kT[h][1]])
                def sec_sc(c):
                    pX, b_pX = psf.next()
                    fm_mm(S, pX, b_pX, w, wbuf(wbs, wch, c * 128, c * 128 + 128), hT, b_hT, c * 128, 128, W)
                    xf, b_xf = f32r.next()
                    S.op("act", lambda e: e.activation(out=xf[:, 0:W], in_=pX[:, 0:W], func=AF.Copy), reads=[b_pX], writes=[b_xf])
                    pC, b_pC = psf.next()
                    fm_mm(S, pC, b_pC, w, wbuf(wbs, wch, 512 + c * 128, 512 + c * 128 + 128), hT, b_hT, 512 + c * 128, 128, W)
                    zt, b_zt = f32r.next()
                    S.op("dve", lambda e: e.tensor_tensor(out=zt[:, 0:W], in0=pC[:, 0:W], in1=xf[:, 0:W], op=ALU.mult), reads=[b_pC, b_xf], writes=[b_zt])
                    S.dma(d["zsc"][c * 128:(c + 1) * 128, t0:t0 + W], zt[:, 0:W], reads=[b_zt])
                    if t0 == 0:
                        S.dma(d["send_f"][c * 128:(c + 1) * 128, 0:1], zt[:, 0:1], reads=[b_zt], slow=True)
                    if t0 + W == T:
                        S.dma(d["send_f"][c * 128:(c + 1) * 128, 1:2], zt[:, W - 1:W], reads=[b_zt], slow=True)
                    pB, b_pB = psf.next()
                    fm_mm(S, pB, b_pB, w, wbuf(wbs, wch, 256 + c * 128, 256 + c * 128 + 128), hT, b_hT, 256 + c * 128, 128, W)
                    bt, b_bt = f32r.next()
                    S.op("act", lambda e: e.activation(out=bt[:, 0:W], in_=pB[:, 0:W], func=AF.Copy), reads=[b_pB], writes=[b_bt])
                    S.dma(d["bsc"][c * 128:(c + 1) * 128, t0:t0 + W], bt[:, 0:W], reads=[b_bt])
                def sec_cf(c):
                    pG, b_pG = psf.next()
                    fm_mm(S, pG, b_pG, w, wbuf(wbs, wch, 1440 + c * 128, 1440 + c * 128 + 128), hT, b_hT, 1440 + c * 128, 128, W)
                    sg, b_sg = f32r.next()
                    S.op("act", lambda e: e.activation(out=sg[:, 0:W], in_=pG[:, 0:W], func=AF.Sigmoid), reads=[b_pG], writes=[b_sg])
                    pA, b_pA = psf.next()
                    fm_mm(S, pA, b_pA, w, wbuf(wbs, wch, 1184 + c * 128, 1184 + c * 128 + 128), hT, b_hT, 1184 + c * 128, 128, W)
                    zt, b_zt = f32r.next()
                    S.op("dve", lambda e: e.tensor_tensor(out=zt[:, 0:W], in0=pA[:, 0:W], in1=sg[:, 0:W], op=ALU.mult), reads=[b_pA, b_sg], writes=[b_zt])
                    S.dma(d["zcf"][c * 128:(c + 1) * 128, t0:t0 + W], zt[:, 0:W], reads=[b_zt])
                    if t0 == 0:
                        S.dma(d["send_f"][c * 128:(c + 1) * 128, 2:17], zt[:, 0:15], reads=[b_zt])
                    if t0 + W == T:
                        S.dma(d["send_f"][c * 128:(c + 1) * 128, 17:32], zt[:, W - 15:W], reads=[b_zt])
                def sec_naqk(c):
                    pQ, b_pQ = psf.next()
                    fm_mm(S, pQ, b_pQ, w, wbuf(wbs, wch, 1696 + c * 128, 1696 + c * 128 + 128), hT, b_hT, 1696 + c * 128, 128, W)
                    ob, b_ob = bfr.next()
                    S.op("act", lambda e: e.activation(out=ob[:, 0:W], in_=pQ[:, 0:W], func=AF.Identity, scale=0.125), reads=[b_pQ], writes=[b_ob])
                    S.dma(d["nq"][c * 128:(c + 1) * 128, t0:t0 + W], ob[:, 0:W], reads=[b_ob])
                    pK, b_pK = psf.next()
                    fm_mm(S, pK, b_pK, w, wbuf(wbs, wch, 1952 + c * 128, 1952 + c * 128 + 128), hT, b_hT, 1952 + c * 128, 128, W)
                    ob, b_ob = bfr.next()
                    S.op("dve", lambda e: e.tensor_copy(out=ob[:, 0:W], in_=pK[:, 0:W]), reads=[b_pK], writes=[b_ob])
                    S.dma(d["nk"][c * 128:(c + 1) * 128, t0:t0 + W], ob[:, 0:W], reads=[b_ob])
                    if t0 == 0:
                        S.dma(d["send_nk"][c * 128:(c + 1) * 128, 0:256], ob[:, 0:256], reads=[b_ob])
                    if t0 + W == T:
                        S.dma(d["send_nk"][c * 128:(c + 1) * 128, 256:512], ob[:, W - 256:W], reads=[b_ob])
                def sec_nav(sub):
                    pV, b_pV = psf.next()
                    for k in range(8):
                        S.op("pe", lambda e: e.matmul(out=pV[:, 0:256], lhsT=hT[:, k, sub * 128:(sub + 1) * 128], rhs=w[:, k, 2208:2464], start=(k == 0), stop=(k == 7)),
                             reads=wbuf(wbs, wch, 2208, 2464) + [b_hT], writes=[b_pV])
                    ot, b_ot = nvr.next()
                    S.op("act", lambda e: e.activation(out=ot[:, 0:260].rearrange("p (h e) -> p h e", h=4)[:, :, 0:64], in_=pV[:, 0:256].rearrange("p (h e) -> p h e", h=4), func=AF.Copy),
                         reads=[b_pV], writes=[b_ot])
                    tt = t0 + sub * 128
                    S.dma(d["nv"][tt:tt + 128, :], ot[:, :], reads=[b_ot])
                    if tt < 256:
                        S.dma(d["send_nv"][tt:tt + 128, :], ot[:, :], reads=[b_ot])
                    if T - 256 <= tt < T:
                        S.dma(d["send_nv"][256 + tt - (T - 256):256 + tt - (T - 256) + 128, :], ot[:, :], reads=[b_ot])
                def sec_head(h):
                    pa, b_pa = psf.next()
                    pb, b_pb = psf.next()
                    for c in range(2):
                        S.op("pe", lambda e: e.matmul(out=pa[0:96, 0:W], lhsT=wuq[:, c, h * 96:(h + 1) * 96], rhs=cqn[c][0][:, 0:W], start=(c == 0), stop=(c == 1)),
                             reads=wuq_b + [cqn[c][1]], writes=[b_pa])
                    for c in range(2):
                        S.op("pe", lambda e: e.matmul(out=pb[0:96, 0:W], lhsT=wuqs[:, c, h * 96:(h + 1) * 96], rhs=cqn[c][0][:, 0:W], start=(c == 0), stop=(c == 1)),
                             reads=wuqs_b + [cqn[c][1]], writes=[b_pb])
                    qT, b_qT = qTr.next()
                    S.op("act", lambda e: e.activation(out=qT[0:64, 0:W], in_=pa[0:64, 0:W], func=AF.Copy), reads=[b_pa], writes=[b_qT])
                    rope96(pa, b_pa, pb, b_pb, t0, W, qT, b_qT)
                    S.dma(d["mq"][h * 96:(h + 1) * 96, t0:t0 + W], qT[:, 0:W], reads=[b_qT])
                def sec_kn(h):
                    pk_, b_pk_ = psf.next()
                    S.op("pe", lambda e: e.matmul(out=pk_[0:64, 0:W], lhsT=wkk[:, 0, h * 64:(h + 1) * 64], rhs=ckvn[:, 0:W], start=True, stop=True),
                         reads=wkk_b + [b_ckvn], writes=[b_pk_])
                    S.op("act", lambda e: e.activation(out=kT[h][0][0:64, 0:W], in_=pk_[0:64, 0:W], func=AF.Copy), reads=[b_pk_], writes=[kT[h][1]])
                    S.dma(d["mk"][h * 96:(h + 1) * 96, t0:t0 + W], kT[h][0][:, 0:W], reads=[kT[h][1]])
                def sec_mv(sub):
                    pV, b_pV = psf.next()
                    S.op("pe", lambda e: e.matmul(out=pV[:, 0:256], lhsT=ckvn[:, sub * 128:(sub + 1) * 128], rhs=wkv[:, 0, :], start=True, stop=True),
                         reads=wkv_b + [b_ckvn], writes=[b_pV])
                    ot, b_ot = mvr.next()
                    S.op("dve", lambda e: e.tensor_copy(out=ot[:, :, 0:64], in_=pV[:, 0:256].rearrange("p (h e) -> p h e", h=4)), reads=[b_pV], writes=[b_ot])
                    tt = t0 + sub * 128
                    mvn, mvo = ("mva", tt) if tt < 1024 else (("mvb", tt - 1024) if tt < T else ("mvc", tt - T))
                    S.dma(d[mvn][mvo:mvo + 128, :], ot[:].rearrange("p h e -> p (h e)"), reads=[b_ot])

                sec_sc(0)
                sec_sc(1)
                pro(bi + 1, 0)
                sec_head(0)
                sec_cf(0)
                sec_head(1)
                sec_cf(1)
                pro(bi + 1, 1)
                sec_head(2)
                sec_naqk(0)
                sec_head(3)
                sec_naqk(1)
                pro(bi + 1, 2)
                for sub in range(W // 128):
                    sec_nav(sub)
                for h in range(4):
                    sec_kn(h)
                pro(bi + 1, 3)
                for sub in range(W // 128):
                    sec_mv(sub)
                pp.flush()
                if after_latent is not None and t0 + W == T:
                    after_latent()
                for _ in range(2):
                    if late_p0:
                        late_p0.pop(0)()
            while late_p0:
                late_p0.pop(0)()
            S.barrier()
        S.release()
    return mod, b_mod


def emit_B(P, S, st0, d, K, psf, psb, mod, b_mod, last, p0_next=None):
    nblk = 4 if last else 5
    ntile = 16 if last else 18
    if True:
        S.mark()
        cp, b_cp, kp, b_kp = K["cp"], K["b_cp"], K["kp"], K["b_kp"]
        idb, b_idb, idf, b_idf, onef, b_onef = K["idb"], K["b_idb"], K["idf"], K["b_idf"], K["onef"], K["b_onef"]
        A2, b_A2 = P.sb(st0, "A2", [128, 8, 2], F32)
        for j in range(2):
            S.op("dve", lambda e: e.scalar_tensor_tensor(out=A2[:, :, j], in0=mod[:, 32:40, j], scalar=1.0, in1=cp[:, C_GPRE2:C_GPRE2 + 8],
                                                         op0=ALU.add, op1=ALU.mult), reads=[b_mod, b_cp], writes=[b_A2])
        gfm, b_gfm = P.sb(st0, "gfm", [128, 8], F32)
        dg, b_dg = P.sb(st0, "dg", [128, 128], F32)

        def build_G(st, gc, mc, ring=None):
            ring = ring or psf
            Gs = [P.sb(st, "G", [128, D], F32) for _ in range(2)]
            for seg in range(2):
                S.op("dve", lambda e: e.tensor_tensor(out=gfm[:, :], in0=mod[:, mc:mc + 8, seg], in1=cp[:, gc:gc + 8], op=ALU.mult), reads=[b_mod, b_cp], writes=[b_gfm])
                for k in range(8):
                    S.op("dve", lambda e: e.tensor_scalar(out=dg[:, :], in0=idf[:, :], scalar1=gfm[:, k:k + 1], scalar2=None, op0=ALU.mult), reads=[b_idf, b_gfm], writes=[b_dg])
                    pg, b_pg = ring.next()
                    S.op("pe", lambda e: e.matmul(out=pg[:, 0:128], lhsT=onef[:, :], rhs=dg[:, :], start=True, stop=True), reads=[b_onef, b_dg], writes=[b_pg])
                    S.op("act", lambda e: e.activation(out=Gs[seg][0][:, k * 128:(k + 1) * 128], in_=pg[:, 0:128], func=AF.Copy), reads=[b_pg], writes=[Gs[seg][1]])
            return Gs

        stW = contextlib.ExitStack()
        w1_t = P.sb(stW, "w1", [128, 8, 4096], BF16)[0]
        stgW = None
        stY = contextlib.ExitStack()
        YT, b_YT = P.sb(stY, "YT", [128, 8, TA], BF16)
        b_YTm = [Buf("ytm") for _ in range(4)]

        def epilogue(pY, xtile, out_dst, Gt, wk):
            junk, b_junk, ss, b_ss, ot_ring = wk
            if isinstance(junk, Ring):
                junk, b_junk = junk.next()
            if isinstance(ss, Ring):
                ss, b_ss = ss.next()
            xt, b_xt = xtile
            for i, (p_, b_p) in enumerate(pY):
                S.op("act", lambda e: e.activation(out=junk[:, 0:512], in_=p_[:, 0:512], func=AF.Square, accum_out=ss[:, i:i + 1]), reads=[b_p], writes=[b_junk, b_ss])
            S.op("dve", lambda e: e.tensor_tensor(out=ss[:, 2:3], in0=ss[:, 0:1], in1=ss[:, 1:2], op=ALU.add), reads=[b_ss], writes=[b_ss])
            S.op("act", lambda e: e.activation(out=ss[:, 3:4], in_=ss[:, 2:3], func=AF.Sqrt, scale=1.0 / D, bias=cp[:, C_EPS:C_EPS + 1]), reads=[b_ss, b_cp], writes=[b_ss])
            S.op("dve", lambda e: e.reciprocal(out=ss[:, 4:5], in_=ss[:, 3:4]), reads=[b_ss], writes=[b_ss])
            ot, b_ot = ot_ring.next()
            for i, (p_, b_p) in enumerate(pY):
                S.op("dve", lambda e: e.scalar_tensor_tensor(out=ot[:, i * 512:(i + 1) * 512], in0=p_[:, 0:512], scalar=ss[:, 4:5], in1=Gt[0][:, i * 512:(i + 1) * 512],
                                                             op0=ALU.mult, op1=ALU.mult), reads=[b_p, b_ss, Gt[1]], writes=[b_ot])
            S.op("pool", lambda e: e.tensor_tensor(out=ot[:, :], in0=ot[:, :], in1=xt[:, :], op=ALU.add), reads=[b_ot, b_xt], writes=[b_ot])
            S.dma(out_dst, ot[:, :], reads=[b_ot])

        def tm_to_YT(ytm_ap_fn, b_ytm, chunk0, col0, b_y):
            pt, b_pt = psb.next()
            for c in range(2):
                S.op("pe", lambda e: e.transpose(out=pt[:, c * 128:(c + 1) * 128], in_=ytm_ap_fn(c), identity=idb[:]), reads=[b_ytm, b_idb], writes=[b_pt])
            S.op("dve", lambda e: e.tensor_copy(out=YT[:, chunk0:chunk0 + 2, col0:col0 + 128], in_=pt[:, 0:256].rearrange("p (c t) -> p c t", c=2)), reads=[b_pt], writes=[b_y])

        stSC = contextlib.ExitStack()
        st = stSC
        if True:
            zs, b_zs = P.sb(st, "zs", [128, 2, T + 2], F32)
            b_zsl = []
            b_Bsl = []
            Bs, b_Bs = P.sb(st, "Bs", [128, 2, T], F32)
            acc, b_acc = P.sb(st, "acc", [128, 2, T], F32)
            for c in range(2):
                b_zsl.append(Buf("zsl"))
                S.dma(zs[:, c, 1:T + 1], d["zsc"][c * 128:(c + 1) * 128, 0:T], writes=[b_zsl[-1]])
                S.dma(zs[:, c, 0:1], d["g_f"][c * 128:(c + 1) * 128, 1:2], reads=[d["b_g_f"]], writes=[b_zs], slow=True)
                S.dma(zs[:, c, T + 1:T + 2], d["g_f"][256 + c * 128:256 + (c + 1) * 128, 0:1], reads=[d["b_g_f"]], writes=[b_zs], slow=True)
                b_Bsl.append(Buf("Bsl"))
                S.dma(Bs[:, c, :], d["bsc"][c * 128:(c + 1) * 128, 0:T], writes=[b_Bsl[-1]])
        if True:
            zc, b_zc = P.sb(st, "zc", [128, 2, T + 30], BF16)
            b_acc_sc = b_acc
            stg = None
            wpw, wpw_b, _ = load_cast(P, S, st, "wpw", wsrc(d["wpw"]), 2, 256, stg)
            for c in range(2):
                S.dma(zc[:, c, 15:T + 15], d["zcf"][c * 128:(c + 1) * 128, 0:T], writes=[b_zc], qn="pool")
                S.dma(zc[:, c, 0:15], d["g_f"][c * 128:(c + 1) * 128, 17:32], reads=[d["b_g_f"]], writes=[b_zc], qn="pool")
                S.dma(zc[:, c, T + 15:T + 30], d["g_f"][256 + c * 128:256 + (c + 1) * 128, 2:17], reads=[d["b_g_f"]], writes=[b_zc], qn="pool")
            S.op("dve", lambda e: e.tensor_scalar(out=zc[:, :, 0:15], in0=zc[:, :, 0:15], scalar1=kp[:, K_ML:K_ML + 1], scalar2=None, op0=ALU.mult), reads=[b_zc, b_kp], writes=[b_zc])
            S.op("dve", lambda e: e.tensor_scalar(out=zc[:, :, T + 15:T + 30], in0=zc[:, :, T + 15:T + 30], scalar1=kp[:, K_MR:K_MR + 1], scalar2=None, op0=ALU.mult), reads=[b_zc, b_kp], writes=[b_zc])
            Dg, b_Dg = P.sb(st, "Dg", [128, 2, 31, 128], BF16)
            b_Dgc = [Buf("Dg0"), Buf("Dg1")]
            for c in range(2):
                for j in range(31):
                    col = C_WDW + c * 31 + j
                    if j % 2 == 0:
                        S.op("dve", lambda e: e.tensor_scalar(out=Dg[:, c, j, :], in0=idf[:, :], scalar1=cp[:, col:col + 1], scalar2=None, op0=ALU.mult),
                             reads=[b_idf, b_cp], writes=[b_Dgc[c]])
                    else:
                        S.op("act", lambda e: e.activation(out=Dg[:, c, j, :], in_=idf[:, :], func=AF.Copy, scale=cp[:, col:col + 1]),
                             reads=[b_idf, b_cp], writes=[b_Dgc[c]])
            S.op("dve", lambda e: e.tensor_scalar(out=zs[:, :, 0:1], in0=zs[:, :, 0:1], scalar1=kp[:, K_ML:K_ML + 1], scalar2=None, op0=ALU.mult), reads=[b_zs, b_kp] + b_zsl, writes=[b_zs])
            S.op("dve", lambda e: e.tensor_scalar(out=zs[:, :, T + 1:T + 2], in0=zs[:, :, T + 1:T + 2], scalar1=kp[:, K_MR:K_MR + 1], scalar2=None, op0=ALU.mult), reads=[b_zs, b_kp], writes=[b_zs])

            def sc_conv(z_t, b_z, B_t, b_B, n, ycol0):
                for c in range(2):
                    wc = lambda j: cp[:, C_WSC + c * 3 + j:C_WSC + c * 3 + j + 1]
                    S.op("dve", lambda e: e.tensor_scalar(out=acc[:, c, 0:n], in0=z_t[:, c, 0:n], scalar1=wc(0), scalar2=None, op0=ALU.mult), reads=[b_z, b_cp], writes=[b_acc])
                    for j in (1, 2):
                        S.op("dve", lambda e: e.scalar_tensor_tensor(out=acc[:, c, 0:n], in0=z_t[:, c, j:j + n], scalar=wc(j), in1=acc[:, c, 0:n], op0=ALU.mult, op1=ALU.add),
                             reads=[b_z, b_cp, b_acc], writes=[b_acc])
                    S.op("dve", lambda e: e.tensor_tensor(out=YT[:, c, ycol0:ycol0 + n], in0=acc[:, c, 0:n], in1=B_t[:, c, 0:n], op=ALU.mult), reads=[b_acc] + list(b_B), writes=[b_YTm[0]])

            sc_conv(zs, b_zs, Bs, b_Bsl, T, 0)
            if not last:
                zsc_, b_zsc_ = P.sb(st, "zsc", [128, 2, TC + 2], F32)
                Bc, b_Bc = P.sb(st, "Bc", [128, 2, TC], F32)
                S.op("pool", lambda e: e.memset(zsc_[:], 0.0), writes=[b_zsc_])
                for c in range(2):
                    S.dma(zsc_[:, c, 1:TC + 1], d["zsc"][c * 128:(c + 1) * 128, T:TA], writes=[b_zsc_])
                    S.dma(Bc[:, c, :], d["bsc"][c * 128:(c + 1) * 128, T:TA], writes=[b_Bc])
                sc_conv(zsc_, b_zsc_, Bc, [b_Bc], TC, T)
            d["late_gathers"]()
            sqf = [P.sb(st, "sqf", [128, 512], F32) for _ in range(2)]
            mean, b_mean = P.sb(st, "mean", [128, 512], F32)
            msq, b_msq = P.sb(st, "msq", [128, 512], F32)
            rstd, b_rstd = P.sb(st, "rstd", [128, 512], F32)
            tn = [P.sb(st, "tn", [128, 512], F32) for _ in range(2)]
            sbf = [P.sb(st, "sbf", [128, 512], BF16) for _ in range(2)]

            def cf_conv(z_t, b_z, n, ycol0):
                blocks = [(b0, min(512, n - b0)) for b0 in range(0, n, 512)]
                b_accs = [Buf("accb") for _ in blocks]

                def conv(i):
                    b0, W = blocks[i]
                    for c in range(2):
                        pc, b_pc = psf.next()
                        for j in range(31):
                            S.op("pe", lambda e: e.matmul(out=pc[:, 0:W], lhsT=Dg[:, c, j, :], rhs=z_t[:, c, b0 + j:b0 + j + W], start=(j == 0), stop=(j == 30)),
                                 reads=[b_Dgc[c], b_z], writes=[b_pc])
                        S.op("act", lambda e: e.activation(out=acc[:, c, b0:b0 + W], in_=pc[:, 0:W], func=AF.Identity, bias=cp[:, C_BDW + c:C_BDW + c + 1]),
                             reads=[b_pc, b_cp], writes=[b_accs[i], b_acc_sc])

                def ln(i):
                    b0, W = blocks[i]
                    b_acc = b_accs[i]
                    for c in range(2):
                        S.op("act", lambda e: e.activation(out=sqf[c][0][:, 0:W], in_=acc[:, c, b0:b0 + W], func=AF.Square), reads=[b_acc], writes=[sqf[c][1]])
                    p1, b_p1 = psf.next()
                    p2, b_p2 = psf.next()
                    for c in range(2):
                        S.op("pe", lambda e: e.matmul(out=p1[:, 0:W], lhsT=onef[:, :], rhs=acc[:, c, b0:b0 + W], start=(c == 0), stop=(c == 1)), reads=[b_onef, b_acc], writes=[b_p1])
                    for c in range(2):
                        S.op("pe", lambda e: e.matmul(out=p2[:, 0:W], lhsT=onef[:, :], rhs=sqf[c][0][:, 0:W], start=(c == 0), stop=(c == 1)), reads=[b_onef, sqf[c][1]], writes=[b_p2])
                    S.op("act", lambda e: e.activation(out=mean[:, 0:W], in_=p1[:, 0:W], func=AF.Copy, scale=1.0 / 256), reads=[b_p1], writes=[b_mean])
                    S.op("dve", lambda e: e.tensor_tensor(out=msq[:, 0:W], in0=mean[:, 0:W], in1=mean[:, 0:W], op=ALU.mult), reads=[b_mean], writes=[b_msq])
                    S.op("dve", lambda e: e.scalar_tensor_tensor(out=rstd[:, 0:W], in0=p2[:, 0:W], scalar=1.0 / 256, in1=msq[:, 0:W], op0=ALU.mult, op1=ALU.subtract),
                         reads=[b_p2, b_msq], writes=[b_rstd])
                    S.op("act", lambda e: e.activation(out=rstd[:, 0:W], in_=rstd[:, 0:W], func=AF.Sqrt, bias=cp[:, C_EPS:C_EPS + 1]), reads=[b_rstd, b_cp], writes=[b_rstd])
                    S.op("dve", lambda e: e.reciprocal(out=rstd[:, 0:W], in_=rstd[:, 0:W]), reads=[b_rstd], writes=[b_rstd])
                    for c in range(2):
                        S.op("dve", lambda e: e.tensor_tensor(out=tn[c][0][:, 0:W], in0=acc[:, c, b0:b0 + W], in1=mean[:, 0:W], op=ALU.subtract), reads=[b_acc, b_mean], writes=[tn[c][1]])
                        S.op("dve", lambda e: e.tensor_tensor(out=tn[c][0][:, 0:W], in0=tn[c][0][:, 0:W], in1=rstd[:, 0:W], op=ALU.mult), reads=[tn[c][1], b_rstd], writes=[tn[c][1]])
                        S.op("act", lambda e: e.activation(out=sbf[c][0][:, 0:W], in_=tn[c][0][:, 0:W], func=AF.Silu, scale=cp[:, C_LNG + c:C_LNG + c + 1], bias=cp[:, C_LNB + c:C_LNB + c + 1]),
                             reads=[tn[c][1], b_cp], writes=[sbf[c][1]])
                    for co in range(2):
                        po, b_po = psf.next()
                        for c in range(2):
                            S.op("pe", lambda e: e.matmul(out=po[:, 0:W], lhsT=wpw[:, c, co * 128:(co + 1) * 128], rhs=sbf[c][0][:, 0:W], start=(c == 0), stop=(c == 1)),
                                 reads=wpw_b + [sbf[c][1]], writes=[b_po])
                        S.op("act", lambda e: e.activation(out=YT[:, 4 + co, ycol0 + b0:ycol0 + b0 + W], in_=po[:, 0:W], func=AF.Copy), reads=[b_po], writes=[b_YTm[2]])

                for i in range(len(blocks) + 1):
                    if i < len(blocks):
                        conv(i)
                    if i >= 1:
                        ln(i - 1)

            cf_conv(zc, b_zc, T, 0)
            if not last:
                zcc, b_zcc = P.sb(st, "zcc", [128, 2, TC + 30], BF16)
                S.op("pool", lambda e: e.memset(zcc[:], 0.0), writes=[b_zcc])
                for c in range(2):
                    S.dma(zcc[:, c, 15:TC + 15], d["zcf"][c * 128:(c + 1) * 128, T:TA], writes=[b_zcc], qn="pool")
                cf_conv(zcc, b_zcc, TC, T)
            S.barrier()
        stSC.close()
        S.release()
        S.mark()

        NKT = 34
        stKT = contextlib.ExitStack()
        KT = P.sb(stKT, "KT", [96, 4, NKT * 128], BF16)[0]
        b_KTh = [Buf("KTh") for _ in range(4)]

        with contextlib.ExitStack() as st:
            NK, b_NK = P.sb(st, "NK", [128, 2, T + 512], BF16)
            b_NKs = []
            b_NQs = []
            NKc, b_NKc = P.sb(st, "NKc", [128, 2, TC], BF16)
            NV, b_NV = P.sb(st, "NV", [128, 20, 4, 65], BF16)
            NVc, b_NVc = P.sb(st, "NVc", [128, 2, 4, 65], BF16)
            NQ, b_NQ = P.sb(st, "NQ", [128, 2, TA], BF16)
            RB, b_RB = P.sb(st, "RB", [128, 4, 16, 64], F32)
            S.dma(RB[:].rearrange("p h o c -> p (h o c)"), d["rbt"][:, :], writes=[b_RB])
            CB, b_CB = P.sb(st, "CB", [128, 4, 5, 128], BF16)
            for h in range(4):
                for di, dk in enumerate(range(-2, 3)):
                    for qr in range(2):
                        idx = K_MB + NA_IDX[(2, dk, qr)]
                        S.op("dve", lambda e: e.tensor_scalar(out=CB[:, h, di, qr * 64:(qr + 1) * 64], in0=RB[:, h, 8 - 2 * dk + qr, :], scalar1=kp[:, idx:idx + 1], scalar2=None, op0=ALU.add),
                             reads=[b_RB, b_kp], writes=[b_CB])
            b_NVg = [Buf("NVg") for _ in range(5)]
            for c in range(2):
                b_NKs.append(Buf("NKa"))
                S.dma(NK[:, c, 256:T + 256], d["nk"][c * 128:(c + 1) * 128, 0:T], writes=[b_NKs[-1]])
                b_NKs.append(Buf("NKb"))
                S.dma(NK[:, c, 0:256], d["g_nk"][c * 128:(c + 1) * 128, 256:512], reads=[d["b_g_nk"]], writes=[b_NKs[-1]])
                b_NKs.append(Buf("NKc_"))
                S.dma(NK[:, c, T + 256:T + 512], d["g_nk"][256 + c * 128:256 + (c + 1) * 128, 0:256], reads=[d["b_g_nk"]], writes=[b_NKs[-1]])
                S.dma(NKc[:, c, :], d["nk"][c * 128:(c + 1) * 128, T:TA], writes=[b_NKc])
                b_NQs.append(Buf("NQl"))
                S.dma(NQ[:, c, :], d["nq"][c * 128:(c + 1) * 128, :], writes=[b_NQs[-1]])
            nvsrc = lambda ap: ap.rearrange("(i p) c -> p i c", p=128)
            S.dma(NV[:, 0:2, :, :].rearrange("p i h e -> p i (h e)"), nvsrc(d["g_nv"][256:512, 0:260]), reads=[d["b_g_nv"]], writes=[b_NVg[0]], qn="act")
            for g_ in range(5):
                i0, i1 = max(2, g_ * 4), min(18, g_ * 4 + 4)
                S.dma(NV[:, i0:i1, :, :].rearrange("p i h e -> p i (h e)"), nvsrc(d["nv"][(i0 - 2) * 128:(i1 - 2) * 128, 0:260]), writes=[b_NVg[g_]], qn="act")
            S.dma(NV[:, 18:20, :, :].rearrange("p i h e -> p i (h e)"), nvsrc(d["g_nv"][512:768, 0:260]), reads=[d["b_g_nv"]], writes=[b_NVg[4]], qn="act")
            S.dma(NVc[:, :, :, :].rearrange("p i h e -> p i (h e)"), nvsrc(d["nv"][T:TA, 0:260]), writes=[b_NVc])
            for h in range(4):
                for r in range(2):
                    S.dma(KT[:, h, r * T:(r + 1) * T], d["g_mk"][r * 384 + h * 96:r * 384 + (h + 1) * 96, 0:T], reads=[d["b_g_mk"]], writes=[b_KTh[h]])
                S.dma(KT[:, h, 2 * T:2 * T + TC], d["mk"][h * 96:(h + 1) * 96, T:TA], writes=[b_KTh[h]])
            sbr = Ring([P.sb(st, "nsb", [128, 6, 128], F32) for _ in range(2)])
            ptr = Ring([P.sb(st, "npt", [128, 8, 128], BF16) for _ in range(2)])
            rec, b_rec = P.sb(st, "nrec", [128, 4], F32)
            ytr = Ring([P.sb(st, "nyt", [128, 256], BF16) for _ in range(2)])

            na_po = Ring(psf.items[0:2])
            na_pl = Ring(psf.items[2:6])

            tiles = [(j * 128, na_slots(j), j) for j in range(16)]
            if not last:
                tiles += [(T + jc * 128, [], 0) for jc in range(2)]
            units = []
            for ti, (qcol, slots, j) in enumerate(tiles):
                tctx = {}
                for h in range(4):
                    units.append(dict(qcol=qcol, slots=slots, j=j, h=h, t=tctx))

            def na_stage1(u):
                qcol, slots, j, h = u["qcol"], u["slots"], u["j"], u["h"]
                nl = len(slots)
                hp, c = (h % 2) * 64, h // 2
                pl = [na_pl.next(), na_pl.next()]
                u["pl"] = pl
                q_ap = NQ[hp:hp + 64, c, qcol:qcol + 128]
                interior_all = (2 <= j <= 13) and len(slots) == 5
                interior = interior_all and h % 2 == 0
                for si, dk in enumerate(slots):
                    p_, b_p = pl[si // 4]
                    kc = (j + dk + 2) * 128
                    S.op("pe", lambda e: e.matmul(out=p_[:, (si % 4) * 128:(si % 4 + 1) * 128], lhsT=NK[hp:hp + 64, c, kc:kc + 128], rhs=q_ap, start=True, stop=not interior),
                         reads=b_NKs + b_NQs, writes=[b_p])
                    if interior:
                        S.op("pe", lambda e: e.matmul(out=p_[:, (si % 4) * 128:(si % 4 + 1) * 128], lhsT=idb[:, :], rhs=CB[:, h, dk + 2, :], start=False, stop=True),
                             reads=[b_idb, b_CB], writes=[b_p])
                for i in range(2):
                    si = nl + i
                    p_, b_p = pl[si // 4]
                    S.op("pe", lambda e: e.matmul(out=p_[:, (si % 4) * 128:(si % 4 + 1) * 128], lhsT=NKc[hp:hp + 64, c, i * 128:(i + 1) * 128], rhs=q_ap, start=True, stop=True),
                         reads=[b_NKc] + b_NQs, writes=[b_p])
                sb_, b_sb = sbr.next()
                pt, b_pt = ptr.next()
                u["pt"] = (pt, b_pt)
                if interior:
                    S.op("act", lambda e: e.activation(out=pt[:, 0:4, :], in_=pl[0][0][:, 0:512].rearrange("p (s q) -> p s q", s=4), func=AF.Exp), reads=[pl[0][1]], writes=[b_pt])
                    S.op("act", lambda e: e.activation(out=pt[:, 4:7, :], in_=pl[1][0][:, 0:384].rearrange("p (s q) -> p s q", s=3), func=AF.Exp), reads=[pl[1][1]], writes=[b_pt])
                    return
                for si, dk in enumerate(slots):
                    p_, b_p = pl[si // 4]
                    o0 = 8 - 2 * dk
                    if interior_all:
                        S.op("dve", lambda e: e.tensor_tensor(out=sb_[:, si, :], in0=p_[:, (si % 4) * 128:(si % 4 + 1) * 128], in1=CB[:, h, dk + 2, :], op=ALU.add),
                             reads=[b_p, b_CB], writes=[b_sb])
                        continue
                    for qr in range(2):
                        idx = K_MB + NA_IDX[(j, dk, qr)]
                        S.op("dve", lambda e: e.scalar_tensor_tensor(out=sb_[:, si, qr * 64:(qr + 1) * 64], in0=p_[:, (si % 4) * 128 + qr * 64:(si % 4) * 128 + (qr + 1) * 64],
                                                                     scalar=kp[:, idx:idx + 1], in1=RB[:, h, o0 + qr, :], op0=ALU.add, op1=ALU.add),
                             reads=[b_p, b_kp, b_RB], writes=[b_sb])
                if nl:
                    S.op("act", lambda e: e.activation(out=pt[:, 0:nl, :], in_=sb_[:, 0:nl, :], func=AF.Exp), reads=[b_sb], writes=[b_pt])
                for i in range(2):
                    si = nl + i
                    p_, b_p = pl[si // 4]
                    S.op("act", lambda e: e.activation(out=pt[:, si, :], in_=p_[:, (si % 4) * 128:(si % 4 + 1) * 128], func=AF.Exp), reads=[b_p], writes=[b_pt])

            def na_stage2(u):
                qcol, slots, j, h, tctx = u["qcol"], u["slots"], u["j"], u["h"], u["t"]
                nl = len(slots)
                if h == 0:
                    tctx["po"] = na_po.next()
                    tctx["yt"] = ytr.next()
                po, b_po = tctx["po"]
                yt, b_yt = tctx["yt"]
                pt, b_pt = u["pt"]
                for si in range(nl + 2):
                    if si < nl:
                        v_ap, bv = NV[:, j + slots[si] + 2, h, :], b_NVg[(j + slots[si] + 2) // 4]
                    else:
                        v_ap, bv = NVc[:, si - nl, h, :], b_NVc
                    S.op("pe", lambda e: e.matmul(out=po[:, h * 65:(h + 1) * 65], lhsT=pt[:, si, :], rhs=v_ap, start=(si == 0), stop=(si == nl + 1)),
                         reads=[b_pt, bv], writes=[b_po])
                if h == 3:
                    S.op("dve", lambda e: e.reciprocal(out=rec[:, :], in_=po[:, 0:260].rearrange("p (h e) -> p h e", e=65)[:, :, 64]), reads=[b_po], writes=[b_rec])
                    for hh in range(4):
                        S.op("dve", lambda e: e.tensor_scalar(out=yt[:, hh * 64:(hh + 1) * 64], in0=po[:, hh * 65:hh * 65 + 64], scalar1=rec[:, hh:hh + 1], scalar2=None, op0=ALU.mult),
                             reads=[b_po, b_rec], writes=[b_yt])
                    tm_to_YT(lambda c: yt[:, c * 128:(c + 1) * 128], b_yt, 6, qcol, b_YTm[3])

            NSK = 1
            for i in range(len(units) + NSK):
                if i < len(units):
                    na_stage1(units[i])
                if i - NSK >= 0:
                    na_stage2(units[i - NSK])
            S.barrier()
        S.release()
        S.mark()

        with contextlib.ExitStack() as st:
            MV = P.sb(st, "MV", [128, NKT, 4, 128], BF16)[0]
            b_MVg = [Buf("MVg") for _ in range(5)]
            mvsrc = lambda ap: ap.rearrange("(i p) c -> p i c", p=128)
            for g_ in range(4):
                r, i0 = g_ // 2, (g_ % 2) * 8
                gn = "g_mva" if g_ % 2 == 0 else "g_mvb"
                S.dma(MV[:, g_ * 8:g_ * 8 + 8, :, :].rearrange("p i h e -> p i (h e)"), mvsrc(d[gn][r * 1024:(r + 1) * 1024, :]),
                      reads=[d["b_" + gn]], writes=[b_MVg[g_]], qn="act")
            S.dma(MV[:, 32:34, :, :].rearrange("p i h e -> p i (h e)"), mvsrc(d["mvc"][:, :]), writes=[b_MVg[4]], qn="act")
            p0s = p0_next(st, psf.items[5]) if p0_next else []
            mla_ps = Ring(psf.items[2:5] + psf.items[6:7]) if p0_next else Ring(psf.items[2:7])
            mla_po = Ring(psf.items[0:2])
            QTr = Ring([P.sb(st, "QT", [96, 4, 512], BF16) for _ in range(2)])
            PTr = Ring([P.sb(st, "PT", [128, 512], BF16) for _ in range(4)])
            recr = Ring([P.sb(st, "mrec", [64, 512], F32) for _ in range(1)])
            first = True
            for (t0, W, seg) in BLOCKS[:nblk]:
                QT, b_QT = QTr.next()
                for h in range(4):
                    S.dma(QT[:, h, 0:W], d["mq"][h * 96:(h + 1) * 96, t0:t0 + W], writes=[b_QT])
                kts = list(range(NKT)) if seg == 0 else [32, 33]
                its = [dict(h=h, ki=ki, kt=kt) for h in range(4) for ki, kt in enumerate(kts)]
                hctx = {}

                def mla_s1(it):
                    h, kt = it["h"], it["kt"]
                    pS_, b_pS = mla_ps.next()
                    S.op("pe", lambda e: e.matmul(out=pS_[:, 0:W], lhsT=KT[:, h, kt * 128:(kt + 1) * 128], rhs=QT[:, h, 0:W], start=True, stop=True),
                         reads=[b_KTh[h], b_QT], writes=[b_pS])
                    PT, b_PT = PTr.next()
                    it["PT"] = (PT, b_PT)
                    S.op("act", lambda e: e.activation(out=PT[:, 0:W], in_=pS_[:, 0:W], func=AF.Exp, scale=MLA_SCALE), reads=[b_pS], writes=[b_PT])

                def mla_s2(it):
                    h, ki, kt = it["h"], it["ki"], it["kt"]
                    if ki == 0:
                        hctx[h] = mla_po.next()
                    pO, b_pO = hctx[h]
                    PT, b_PT = it["PT"]
                    S.op("pe", lambda e: e.matmul(out=pO[:, 0:W], lhsT=MV[:, kt, h, :], rhs=PT[:, 0:W], start=(ki == 0), stop=(ki == len(kts) - 1)),
                         reads=[b_PT, b_MVg[kt // 8]], writes=[b_pO])
                    if ki == len(kts) - 1:
                        rec, b_rec = recr.next()
                        S.op("dve", lambda e: e.reciprocal(out=rec[0:64, 0:W], in_=pO[64:128, 0:W]), reads=[b_pO], writes=[b_rec])
                        hp = (h % 2) * 64
                        S.op("dve", lambda e: e.tensor_tensor(out=YT[hp:hp + 64, 2 + h // 2, t0:t0 + W], in0=pO[0:64, 0:W], in1=rec[0:64, 0:W], op=ALU.mult),
                             reads=[b_pO, b_rec], writes=[b_YTm[1]])

                MSK = 2
                for i in range(len(its) + MSK):
                    if i < len(its):
                        mla_s1(its[i])
                    if i - MSK >= 0:
                        mla_s2(its[i - MSK])
                    if i % 8 == 7 and p0s:
                        p0s.pop(0)()
                if first and (t0 >= 512 or not p0_next):
                    first = False
                    w1, w1_b, w1ch = load_cast(P, S, stW, "w1", wsrc(d["wmlp1"]), 8, 4096, stgW, chunk=512, pre=w1_t)
            while p0s:
                p0s.pop(0)()
            S.barrier()
        S.release()
        S.mark()

        stKT.close()
        with contextlib.ExitStack() as st:
            stg = None
            wo, wo_b, woch = load_cast(P, S, st, "wout", wsrc(d["wout"]), 8, 1024, stg)
            ring3 = psf
            G1 = build_G(st, C_GPOST1, 16, ring3)
            steps = []
            xt_ring = Ring([P.sb(st, "xt", [128, D], F32) for _ in range(3)])
            junk_ring = Ring([P.sb(st, "junk", [128, 512], F32) for _ in range(3)])
            ss_ring = Ring([P.sb(st, "ss", [128, 8], F32) for _ in range(3)])
            ot_ring = Ring([P.sb(st, "ot", [128, D], F32) for _ in range(3)])
            wk = (junk_ring, None, ss_ring, None, ot_ring)
            xl = XLoader(S, xt_ring)
            for ti in range(ntile):
                seg = 0 if ti < 16 else 1
                xl.prefetch(("e", ti), d["xin"][ti * 128:(ti + 1) * 128, :])
                pY = [ring3.next(), ring3.next()]
                for nb in range(2):
                    for k in range(8):
                        S.op("pe", lambda e: e.matmul(out=pY[nb][0][:, 0:512], lhsT=YT[:, k, ti * 128:(ti + 1) * 128], rhs=wo[:, k, nb * 512:(nb + 1) * 512], start=(k == 0), stop=(k == 7)),
                             reads=b_YTm + wbuf(wo_b, woch, nb * 512, nb * 512 + 512), writes=[pY[nb][1]])
                epilogue(pY, xl.get(("e", ti), None), d["xmid"][ti * 128:(ti + 1) * 128, :], G1[seg], wk)
                if steps:
                    steps.pop(0)()
            while steps:
                steps.pop(0)()
            S.barrier()
        S.release()
        S.mark()

        stY.close()
        with contextlib.ExitStack() as st:
            stg = stgW
            w2t, w2_b, w2ch = load_cast(P, S, st, "w2", wsrc(d["wmlp2"]), 32, 1024, stg, chunk=512, per_k=True)
            w2 = (w2t,)
            xt_ring = Ring([P.sb(st, "xt", [128, D], F32) for _ in range(2)])
            ss, b_ss = P.sb(st, "ss", [128, 8], F32)
            xn, b_xn = P.sb(st, "xn", [128, D], BF16)
            wkp = (xn, b_xn, ss, b_ss, xn, b_xn)
            ot_ring = Ring([P.sb(st, "ot", [128, D], F32) for _ in range(1)])
            wke = (ot_ring.items[0][0], ot_ring.items[0][1], ss, b_ss, ot_ring)
            hT_ring = Ring([P.sb(st, "hT", [128, 8, 512], BF16) for _ in range(2)])
            h1, b_h1 = P.sb(st, "h1", [128, 32, 512], BF16)
            rl_ring = Ring([P.sb(st, "rl", [128, 512], F32) for _ in range(2)])
            mblocks = BLOCKS[:nblk]
            hTs = {}

            pp = ProPipe()

            def pro2_parts(bi_, sub, wk_):
                t0_, W_, seg_ = mblocks[bi_]
                if bi_ not in hTs:
                    hTs[bi_] = hT_ring.next()
                hT_, b_hT_ = hTs[bi_]
                tt_ = t0_ + sub * 128
                nxt = (("p", tt_ + 128), d["xmid"][tt_ + 128:tt_ + 256, :]) if tt_ + 128 < ntile * 128 else None
                return (lambda: prologue_a(P, S, d["xmid"][tt_:tt_ + 128, :], xl, wk_, cp, b_cp, key=("p", tt_), nxt=nxt),
                        lambda: prologue_b(P, S, wk_, idb, b_idb, psb, hT_, b_hT_, sub * 128,
                                           lambda k: A2[:, k, seg_:seg_ + 1], lambda k: mod[:, 24 + k, seg_:seg_ + 1], b_A2))

            def pro2(bi_, sub):
                if bi_ >= len(mblocks) or sub >= mblocks[bi_][1] // 128:
                    pp.flush()
                    return
                a_fn, b_fn = pro2_parts(bi_, sub, wkp)
                pp.step(a_fn, b_fn)

            xl = XLoader(S, xt_ring)
            ss2, b_ss2 = P.sb(st, "ss2", [128, 4], F32)
            xn2 = h1[:, 0:2, :].rearrange("p a b -> p (a b)")
            wk_alt = (xn2, b_h1, ss2, b_ss2, xn2, b_h1)
            parts = [pro2_parts(0, sub, wkp if sub % 2 == 0 else wk_alt) for sub in range(mblocks[0][1] // 128)]
            parts[0][0]()
            for i_ in range(len(parts)):
                if i_ + 1 < len(parts):
                    parts[i_ + 1][0]()
                parts[i_][1]()
            for bi, (t0, WB, seg) in enumerate(mblocks):
                hT, b_hT = hTs[bi]
                for hc in range(32):
                    ph, b_ph = psf.next()
                    fm_mm(S, ph, b_ph, w1, wbuf(w1_b, w1ch, hc * 128, hc * 128 + 128), hT, b_hT, hc * 128, 128, WB)
                    rl, b_rl = rl_ring.next()
                    S.op("act", lambda e: e.activation(out=rl[:, 0:WB], in_=ph[:, 0:WB], func=AF.Relu), reads=[b_ph], writes=[b_rl])
                    S.op("dve", lambda e: e.tensor_tensor(out=h1[:, hc, 0:WB], in0=rl[:, 0:WB], in1=rl[:, 0:WB], op=ALU.mult), reads=[b_rl], writes=[b_h1])
                if bi == 0:
                    G2 = build_G(st, C_GPOST2, 40)
                nsub = WB // 128
                for sub in range(nsub):
                    for ps_ in range(sub * 4 // nsub, (sub + 1) * 4 // nsub):
                        pro2(bi + 1, ps_)
                    xl.prefetch(("e", t0 + sub * 128), d["xmid"][t0 + sub * 128:t0 + (sub + 1) * 128, :])
                    pY = [psf.next(), psf.next()]
                    for nb in range(2):
                        for hc in range(32):
                            S.op("pe", lambda e: e.matmul(out=pY[nb][0][:, 0:512], lhsT=h1[:, hc, sub * 128:(sub + 1) * 128], rhs=w2[0][:, hc, nb * 512:(nb + 1) * 512], start=(hc == 0), stop=(hc == 31)),
                                 reads=[b_h1, w2_b[nb][hc // 8]], writes=[pY[nb][1]])
                    tt = t0 + sub * 128
                    epilogue(pY, xl.get(("e", tt), None), d["xout"][tt:tt + 128, :], G2[seg], wke)
                pp.flush()
            S.barrier()
        S.release()
        stW.close()


def fm_cols(v):
    return np.ascontiguousarray(np.asarray(v, np.float32).reshape(-1, 128).T)


def rope_perm():
    d = np.arange(32)
    g, i = d // 16, d % 16
    return g * 16 + (i + 8) % 16


def host_tables(core):
    b, s = core // 2, core % 2
    pos = s * T + np.arange(T)
    inv = (10000.0 ** (-np.arange(8, dtype=np.float32) / 8)).astype(np.float32)
    ang_row = (pos // 64).astype(np.float32)[:, None] * inv
    ang_col = (pos % 64).astype(np.float32)[:, None] * inv
    cos = np.ones((32, TA), np.float32)
    sin = np.zeros((32, TA), np.float32)
    for dd in range(32):
        g, i = dd // 16, dd % 16
        ang = (ang_row if g == 0 else ang_col)[:, i % 8]
        cos[dd, :T] = np.cos(ang)
        sin[dd, :T] = np.sin(ang) * (-1.0 if i < 8 else 1.0)
    mb = np.zeros((128, len(NA_IDX)), np.float32)
    for (j, dk, qr), idx in NA_IDX.items():
        r = 32 * s + 2 * j + qr
        w0 = min(max(r - 4, 0), 56)
        for kr in range(2):
            rk = 32 * s + 2 * (j + dk) + kr
            ok = (0 <= rk < 64) and (w0 <= rk < w0 + 8)
            mb[kr * 64:(kr + 1) * 64, idx] = 0.0 if ok else NEG
    return np.stack([cos, sin]), mb


def rb_table(rpb):
    col = np.arange(64)
    cs = np.clip(col - 8, 0, 48)
    in_win = (col[None, :] >= cs[:, None]) & (col[None, :] < cs[:, None] + 16)
    off_c = np.clip(col[None, :] - col[:, None] + 15, 0, 30)
    out = np.full((128, 4, 16, 64), NEG, np.float32)
    for kr in range(2):
        for o2 in range(16):
            off = 15 - o2 + kr
            if off > 14:
                continue
            blk = rpb[:, off][:, off_c]
            blk = np.where(in_win[None], blk, np.float32(NEG))
            out[kr * 64:(kr + 1) * 64, :, o2, :] = np.transpose(blk, (2, 0, 1))
    return out.reshape(128, -1)


LAYER_W = [("wmod", [D, 6 * D]), ("winx", [D, 2560]), ("wuq", [256, 384]), ("wuqs", [256, 384]), ("wukvk", [128, 256]),
           ("wukvv", [128, 256]), ("colpack", [128, NCOL]), ("wout", [D, D]), ("wpw", [256, 256]), ("wmlp1", [D, 4 * D]),
           ("wmlp2", [4 * D, D]), ("rbt", [128, 4 * 16 * 64])]
SCRATCH = [("zsc", [256, TA], F32), ("bsc", [256, TA], F32), ("zcf", [256, TA], F32), ("nq", [256, TA], BF16), ("nk", [256, TA], BF16),
           ("nv", [TA, 272], BF16), ("mq", [384, TA], BF16), ("mk", [384, TA], BF16), ("mva", [1024, 512], BF16), ("mvb", [1024, 512], BF16), ("mvc", [256, 512], BF16),
           ("send_f", [256, 32], F32), ("send_nk", [256, 512], BF16), ("send_nv", [512, 272], BF16),
           ("g_f", [512, 32], F32), ("g_nk", [512, 512], BF16), ("g_nv", [1024, 272], BF16), ("g_mk", [768, TA], BF16),
           ("g_mva", [2048, 512], BF16), ("g_mvb", [2048, 512], BF16), ("xmid", [TA, D], F32)]
PAIRS = [[0, 1], [2, 3], [4, 5], [6, 7]]


def build_fused():
    P = Prog()
    nc = P.nc
    g = {}
    for n, s_, t in [("xin", [TA, D], F32), ("corepack", [128, NCC], F32), ("rope", [2, 32, TA], F32), ("ident", [128, 128], F32)]:
        g[n] = P.dram(n, s_, t, "in")
    xout = P.dram("xout", [T, D], F32, "out")
    x1 = P.dram("x1", [TA, D], F32, "int")
    with contextlib.ExitStack() as st0:
        S = Sched(nc, st0)
        psf, psb = make_psum(P, st0)
        modts = []
        for l in range(2):
            mod_, b_mod_ = P.sb(st0, "mod", [128, 48, 2], F32)
            A1_, b_A1_ = P.sb(st0, "A1", [128, 8, 2], F32)
            modts.append((mod_, b_mod_, A1_, b_A1_))
        kp0, b_kp0 = P.sb(st0, "kp0", [128, NCC], F32)
        S.dma(kp0[:], g["corepack"][:, :], writes=[b_kp0])
        lw = {}
        for l in range(2):
            for n, s_ in LAYER_W:
                lw[(n, l)] = P.dram("%s%d" % (n, l), s_, F32, "in")
        for l in range(2):
            d = dict(g)
            for n, s_ in LAYER_W:
                d[n] = lw[(n, l)]
            for n, s_, t in SCRATCH:
                d[n] = P.dram("%s%d" % (n, l), s_, t, "int")
            d["xin"] = g["xin"] if l == 0 else x1
            d["xout"] = x1 if l == 0 else xout
            with contextlib.ExitStack() as stL:
                K = load_consts(P, S, stL, d)
                def gather(a, b, wait=True):
                    S.collective(d[a], d[b], PAIRS, wait=wait)
                    d["b_" + b] = Buf(b)
                    d["b_" + b].w = ("cc", S.cnt["cc"])

                def small_gathers():
                    gather("send_f", "g_f")
                    gather("send_nk", "g_nk", wait=False)
                    gather("send_nv", "g_nv", wait=False)

                mod, b_mod = emit_A(P, S, stL, d, K, psf, psb, modts[l], need_p0=(l == 0), after_latent=small_gathers)
                d["late_gathers"] = lambda: [gather(a, b, wait=False) for a, b in [("mk", "g_mk"), ("mva", "g_mva"), ("mvb", "g_mvb")]]
                p0n = None
                if l == 0:
                    dn = {"colpack": lw[("colpack", 1)], "wmod": lw[("wmod", 1)]}
                    p0n = lambda st_, pmb: p0_steps(P, S, st_, dn, kp0, b_kp0, pmb[0], pmb[1], *modts[1], gcols=256, nbuf=2)
                emit_B(P, S, stL, d, K, psf, psb, mod, b_mod, last=(l == 1), p0_next=p0n)
                S.barrier()
        print("instructions", S.nins, "waits", S.nwait, "sems", len(S.sem) + len(S.free) + len(S.free_sw))
    return P


_PROG = []


def kernel(x, c, ctx, c_ctx, w_mod, b_mod, g_pre_mix, w_in, w_sc, g_q, w_uq, g_kv, w_ukv, cf_w_dw, cf_b_dw, cf_ln_g, cf_ln_b,
           cf_w_pw, na_rpb, w_out, g_post_mix, g_pre_mlp, w_mlp1, w_mlp2, g_post_mlp):
    f = lambda a: np.ascontiguousarray(np.asarray(a, np.float32))
    x, c, ctx, c_ctx = f(x), f(c), f(ctx), f(c_ctx)
    ident = np.eye(128, dtype=np.float32)
    perm = rope_perm()
    tabs = [host_tables(k) for k in range(8)]
    shared = {}
    for l in range(2):
        wi = f(w_in[l])
        winx = np.concatenate([wi, wi[:, 1088:1152], wi[:, 1152:1184][:, perm]], axis=1)
        wuq = f(w_uq[l])
        wuqs = wuq.copy().reshape(256, 4, 96)
        wuqs[:, :, 64:] = wuqs[:, :, 64:][:, :, perm]
        wuqs = np.ascontiguousarray(wuqs.reshape(256, 384))
        wkv4 = f(w_ukv[l]).reshape(128, 4, 128)
        cpk = np.zeros((128, NCOL), np.float32)
        cpk[:, C_GPRE1:C_GPRE1 + 8] = fm_cols(g_pre_mix[l])
        cpk[:, C_GPOST1:C_GPOST1 + 8] = fm_cols(g_post_mix[l])
        cpk[:, C_GPRE2:C_GPRE2 + 8] = fm_cols(g_pre_mlp[l])
        cpk[:, C_GPOST2:C_GPOST2 + 8] = fm_cols(g_post_mlp[l])
        cpk[:, C_BMOD:C_BMOD + 48] = fm_cols(b_mod[l])
        cpk[:, C_GQ:C_GQ + 2] = fm_cols(g_q[l])
        cpk[:, C_GKV:C_GKV + 1] = fm_cols(g_kv[l])
        wsc = f(w_sc[l])
        wdw = f(cf_w_dw[l])
        for cc in range(2):
            cpk[:, C_WSC + cc * 3:C_WSC + cc * 3 + 3] = wsc[:, cc * 128:(cc + 1) * 128].T
            cpk[:, C_WDW + cc * 31:C_WDW + cc * 31 + 31] = wdw[:, cc * 128:(cc + 1) * 128].T
        cpk[:, C_BDW:C_BDW + 2] = fm_cols(cf_b_dw[l])
        cpk[:, C_LNG:C_LNG + 2] = fm_cols(cf_ln_g[l])
        cpk[:, C_LNB:C_LNB + 2] = fm_cols(cf_ln_b[l])
        cpk[:, C_EPS] = EPS
        lw = dict(wmod=f(w_mod[l]), winx=winx, wuq=wuq, wuqs=wuqs,
                  wukvk=np.ascontiguousarray(wkv4[:, :, :64].reshape(128, 256)),
                  wukvv=np.ascontiguousarray(wkv4[:, :, 64:].reshape(128, 256)),
                  colpack=cpk, wout=f(w_out[l]), wpw=f(cf_w_pw[l]), wmlp1=f(w_mlp1[l]), wmlp2=f(w_mlp2[l]),
                  rbt=rb_table(f(na_rpb[l])))
        for n, v in lw.items():
            shared["%s%d" % (n, l)] = v
    in_maps = []
    for k in range(8):
        kp = np.zeros((128, NCC), np.float32)
        kp[:, K_C:K_C + 8] = fm_cols(c[k // 2])
        kp[:, K_CC:K_CC + 8] = fm_cols(c_ctx)
        kp[:, K_ML] = float(k % 2 == 1)
        kp[:, K_MR] = float(k % 2 == 0)
        kp[:, K_MB:] = tabs[k][1]
        m = dict(shared)
        m.update(xin=np.concatenate([x[k // 2, (k % 2) * T:(k % 2 + 1) * T], ctx[k // 2]], axis=0), corepack=kp, rope=tabs[k][0], ident=ident)
        in_maps.append(m)
    if not _PROG:
        _PROG.append(build_fused())
    res = run_bass_kernel_spmd(_PROG[0].nc, in_maps, core_ids=list(range(8)))
    out = np.zeros((4, 2 * T, D), np.float32)
    for k in range(8):
        out[k // 2, (k % 2) * T:(k % 2 + 1) * T] = np.asarray(res.results[k]["xout"], np.float32)
    return out
```

```python
import bisect
import contextlib
import numpy as np
import ml_dtypes
import concourse.bass as bass
import concourse.mybir as mybir
from concourse.bass_utils import run_bass_kernel_spmd

F32 = mybir.dt.float32
BF16 = mybir.dt.bfloat16
AF = mybir.ActivationFunctionType
ALU = mybir.AluOpType

D = 1024
T = 2048
TC = 256
TA = T + TC
NEG = -30000.0
MLA_SCALE = 96.0 ** -0.5
EPS = 1e-6
BLOCKS = [(0, 512, 0), (512, 512, 0), (1024, 512, 0), (1536, 512, 0), (2048, 256, 1)]

C_GPRE1, C_GPOST1, C_GPRE2, C_GPOST2, C_BMOD = 0, 8, 16, 24, 32
C_GQ, C_GKV, C_WSC, C_WDW, C_BDW, C_LNG, C_LNB, C_EPS = 80, 82, 83, 89, 151, 153, 155, 157
NCOL = 160
K_C, K_CC, K_ML, K_MR, K_MB = 0, 8, 16, 17, 18


def na_slots(j):
    if j == 0:
        return list(range(-2, 4))
    if j == 15:
        return list(range(-3, 3))
    return list(range(-2, 3))


NA_IDX = {}
for _j in range(16):
    for _d in na_slots(_j):
        for _q in range(2):
            NA_IDX[(_j, _d, _q)] = len(NA_IDX)
NCC = K_MB + len(NA_IDX)


class Buf:
    __slots__ = ("name", "w", "r")
    _n = 0

    def __init__(self, name="b"):
        Buf._n += 1
        self.name = "%s_%d" % (name, Buf._n)
        self.w = None
        self.r = []


class Sched:
    def __init__(self, nc, stack):
        self.nc = nc
        self.stack = stack
        self.h = {"pe": nc.tensor, "act": nc.scalar, "dve": nc.vector, "pool": nc.gpsimd, "sp": nc.sync}
        self.sem, self.cnt, self.seen, self.hist = {}, {}, {}, {}
        for e in self.h:
            self.sem[e] = stack.enter_context(nc.semaphore("s_" + e))
            self.cnt[e] = 0
            self.seen[e] = {}
            self.hist[e] = ([0], [{}])
        self.dclock = {}
        self.free = []
        self.free_sw = []
        self.sw_keys = set()
        self.gen = {}
        self._mark = set()
        self.nwait = 0
        self.nins = 0
        self.uid = 0

    def _clock_of(self, ev):
        k, v = ev
        if k in self.h:
            cs, ds = self.hist[k]
            return ds[bisect.bisect_right(cs, v) - 1]
        return self.dclock.get(ev, {})

    def _wait(self, en, deps):
        seen = self.seen[en]
        changed = False
        for k, v in deps.items():
            if k == en and en == "pe":
                continue
            if k not in self.sem:
                continue
            if seen.get(k, 0) >= v:
                continue
            self.h[en].wait_ge(self.sem[k], v)
            self.nwait += 1
            changed = True
            seen[k] = v
            for k2, v2 in self._clock_of((k, v)).items():
                if seen.get(k2, 0) < v2:
                    seen[k2] = v2
        if changed:
            cs, ds = self.hist[en]
            cs.append(self.cnt[en] + 1)
            ds.append(dict(seen))

    def _sync(self, en, reads, writes, skip_key=None):
        deps = {}

        def add(ev):
            if ev is not None and ev[0] != skip_key and deps.get(ev[0], 0) < ev[1]:
                deps[ev[0]] = ev[1]

        for b in reads:
            add(b.w)
        for b in writes:
            add(b.w)
            for r in b.r:
                add(r)
        self._wait(en, deps)

    def _record(self, ev, reads, writes):
        for b in reads:
            b.r.append(ev)
        for b in writes:
            b.w = ev
            b.r = []

    def op(self, en, fn, reads=(), writes=()):
        self._sync(en, reads, writes)
        ins = fn(self.h[en])
        self.cnt[en] += 1
        ins.then_inc(self.sem[en], 1)
        self._record((en, self.cnt[en]), reads, writes)
        self.nins += 1

    def dma(self, out, in_, reads=(), writes=(), qn="sp", slow=False):
        kb = ("ld_" + writes[0].name) if writes else ("st_" + reads[0].name)
        key = "%s#%d" % (kb, self.gen.get(kb, 0))
        self._sync(qn, reads, writes)
        if key not in self.sem:
            fl = self.free_sw if qn == "pool" else self.free
            if qn == "pool":
                self.sw_keys.add(key)
            if fl:
                self.sem[key], self.cnt[key] = fl.pop()
            else:
                self.uid += 1
                self.sem[key] = self.stack.enter_context(self.nc.semaphore("d%d" % self.uid))
                self.cnt[key] = 0
        self.cnt[key] += 16
        if slow:
            self.h[qn].dma_start(out=out, in_=in_, allow_slow_non_contiguous=True).then_inc(self.sem[key], 16)
        else:
            self.h[qn].dma_start(out=out, in_=in_).then_inc(self.sem[key], 16)
        ev = (key, self.cnt[key])
        self.dclock[ev] = dict(self.seen[qn])
        self._record(ev, reads, writes)
        self.nins += 1

    def collective(self, ins_ap, outs_ap, groups, wait=True):
        if wait:
            self._wait("pool", {k: v for k, v in self.cnt.items() if v > 0 and k != "pool"})
        if "cc" not in self.sem:
            self.sem["cc"] = self.stack.enter_context(self.nc.semaphore("cc"))
            self.cnt["cc"] = 0
        self.cnt["cc"] += 1
        self.nc.gpsimd.collective_compute("AllGather", ALU.bypass, replica_groups=groups, ins=[ins_ap.opt()], outs=[outs_ap.opt()]).then_inc(self.sem["cc"], 1)
        self.dclock[("cc", self.cnt["cc"])] = dict(self.seen["pool"])
        self.nins += 1

    def mark(self):
        self._mark = set(self.sem.keys())

    def release(self):
        for k in list(self.sem.keys()):
            if k not in self._mark and k != "cc":
                (self.free_sw if k in self.sw_keys else self.free).append((self.sem.pop(k), self.cnt.pop(k)))
                self.sw_keys.discard(k)
                kb = k.rsplit("#", 1)[0]
                self.gen[kb] = self.gen.get(kb, 0) + 1
                for e in self.h:
                    self.seen[e].pop(k, None)

    def barrier(self, engines=("pe", "act", "dve", "pool", "sp")):
        for e in engines:
            deps = {k: v for k, v in self.cnt.items() if v > 0 and k != e}
            self._wait(e, deps)


class Prog:
    def __init__(self):
        self.nc = bass.Bass("TRN2", target_bir_lowering=False)
        self.n = 0
        self.io = {}

    def dram(self, name, shape, dt, kind):
        self.io[name] = (tuple(shape), dt, kind)
        if kind == "int":
            return self.nc.dram_tensor(name, list(shape), dt).ap()
        k = {"in": "ExternalInput", "out": "ExternalOutput"}[kind]
        return self.nc.dram_tensor(name, list(shape), dt, kind=k).ap()

    def sb(self, st, name, shape, dt):
        self.n += 1
        t = st.enter_context(self.nc.sbuf_tensor("%s_%d" % (name, self.n), list(shape), dt))
        return t, Buf(name)

    def ps(self, st, name, shape, dt):
        self.n += 1
        t = st.enter_context(self.nc.psum_tensor("%s_%d" % (name, self.n), list(shape), dt))
        return t, Buf(name)


class Ring:
    def __init__(self, items):
        self.items = items
        self.i = 0

    def next(self):
        it = self.items[self.i % len(self.items)]
        self.i += 1
        return it


def make_psum(P, st):
    f = Ring([P.ps(st, "pf", [128, 512], F32) for _ in range(7)])
    b = Ring([P.ps(st, "pb", [128, 1024], BF16) for _ in range(1)])
    return f, b


def load_cast(P, S, st, name, dram_ap_fn, nk, ncols, stg_ring=None, chunk=512, pre=None, ks=8, per_k=False):
    wb = pre if pre is not None else P.sb(st, name, [128, nk, ncols], BF16)[0]
    bufs = []
    for c0 in range(0, ncols, chunk):
        cw = min(chunk, ncols - c0)
        bb = Buf(name + "c")
        bl = []
        for k0 in range(0, nk, ks):
            kn = min(ks, nk - k0)
            if per_k:
                bb = Buf(name + "ck")
                bl.append(bb)
            S.dma(wb[:, k0:k0 + kn, c0:c0 + cw], dram_ap_fn(k0, kn, c0, cw), writes=[bb], qn="pool")
        bufs.append(bl if per_k else bb)
    return wb, bufs, chunk


def wsrc(ap):
    return lambda k0, kn, c0, cw: ap[k0 * 128:(k0 + kn) * 128, c0:c0 + cw].rearrange("(k p) n -> p k n", p=128)


def wbuf(bufs, chunk, c0, c1):
    return bufs[c0 // chunk:(c1 - 1) // chunk + 1]


class XLoader:
    def __init__(self, S, ring):
        self.S, self.ring, self.pending = S, ring, {}

    def prefetch(self, key, src):
        if key is None or key in self.pending:
            return
        xt, b = self.ring.next()
        self.S.dma(xt[:], src, writes=[b], qn="act")
        self.pending[key] = (xt, b)

    def get(self, key, src):
        self.prefetch(key, src)
        return self.pending.pop(key)


def prologue_a(P, S, x_src, xl, wk, cp, b_cp, key=None, nxt=None):
    xt, b_xt = xl.get(key, x_src)
    junk, b_junk, ss, b_ss, xn, b_xn = wk
    S.op("act", lambda e: e.activation(out=junk[:], in_=xt[:], func=AF.Square, accum_out=ss[:, 0:1]), reads=[b_xt], writes=[b_junk, b_ss])
    S.op("act", lambda e: e.activation(out=ss[:, 1:2], in_=ss[:, 0:1], func=AF.Sqrt, scale=1.0 / D, bias=cp[:, C_EPS:C_EPS + 1]), reads=[b_ss, b_cp], writes=[b_ss])
    S.op("dve", lambda e: e.reciprocal(out=ss[:, 2:3], in_=ss[:, 1:2]), reads=[b_ss], writes=[b_ss])
    S.op("dve", lambda e: e.tensor_scalar(out=xn[:], in0=xt[:], scalar1=ss[:, 2:3], scalar2=None, op0=ALU.mult), reads=[b_xt, b_ss], writes=[b_xn])
    if nxt is not None:
        xl.prefetch(nxt[0], nxt[1])


def prologue_b(P, S, wk, ident_b, b_ident, psb_ring, hT, b_hT, col_off, A_col, Sh_col, b_mod):
    junk, b_junk, ss, b_ss, xn, b_xn = wk
    pt, b_pt = psb_ring.next()
    for k in range(8):
        S.op("pe", lambda e: e.transpose(out=pt[:, k * 128:(k + 1) * 128], in_=xn[:, k * 128:(k + 1) * 128], identity=ident_b[:]), reads=[b_xn, b_ident], writes=[b_pt])
    for k in range(8):
        if k % 2 == 0:
            S.op("act", lambda e: e.activation(out=hT[:, k, col_off:col_off + 128], in_=pt[:, k * 128:(k + 1) * 128], func=AF.Identity,
                                               scale=A_col(k), bias=Sh_col(k)), reads=[b_pt, b_mod], writes=[b_hT])
        else:
            S.op("dve", lambda e: e.tensor_scalar(out=hT[:, k, col_off:col_off + 128], in0=pt[:, k * 128:(k + 1) * 128], scalar1=A_col(k), scalar2=Sh_col(k),
                                                  op0=ALU.mult, op1=ALU.add), reads=[b_pt, b_mod], writes=[b_hT])


class ProPipe:
    def __init__(self):
        self.pending = None

    def step(self, a_fn, b_fn):
        if self.pending is not None:
            self.pending()
        self.pending = None
        if a_fn is not None:
            a_fn()
            self.pending = b_fn

    def flush(self):
        self.step(None, None)


def fm_mm(S, ps, b_ps, w, wb_list, hT, b_hT, col0, ncols, W, nk=8):
    for k in range(nk):
        S.op("pe", lambda e: e.matmul(out=ps[0:ncols, 0:W], lhsT=w[:, k, col0:col0 + ncols], rhs=hT[:, k, 0:W], start=(k == 0), stop=(k == nk - 1)),
             reads=list(wb_list) + [b_hT], writes=[b_ps])


def load_consts(P, S, st, d):
    cp, b_cp = P.sb(st, "cp", [128, NCOL], F32)
    S.dma(cp[:], d["colpack"][:, :], writes=[b_cp])
    kp, b_kp = P.sb(st, "kp", [128, NCC], F32)
    S.dma(kp[:], d["corepack"][:, :], writes=[b_kp])
    idf, b_idf = P.sb(st, "idf", [128, 128], F32)
    S.dma(idf[:], d["ident"][:, :], writes=[b_idf])
    idb, b_idb = P.sb(st, "idb", [128, 128], BF16)
    S.op("dve", lambda e: e.tensor_copy(out=idb[:], in_=idf[:]), reads=[b_idf], writes=[b_idb])
    onef, b_onef = P.sb(st, "onef", [128, 128], F32)
    S.op("dve", lambda e: e.memset(onef[:], 1.0), writes=[b_onef])
    return dict(cp=cp, b_cp=b_cp, kp=kp, b_kp=b_kp, idf=idf, b_idf=b_idf, idb=idb, b_idb=b_idb, onef=onef, b_onef=b_onef)


def p0_steps(P, S, st, d, kp, b_kp, pm, b_pm, mod, b_mod, A1, b_A1, split=False, gcols=512, nbuf=3):
    sT, b_sT = P.sb(st, "sT", [128, 8, 2], BF16)
    cpn, b_cpn = P.sb(st, "cpn", [128, NCOL], F32)
    stg = Ring([P.sb(st, "wmstg", [128, 8, gcols], BF16) for _ in range(nbuf)])
    tiles = {}
    ng = 6 * D // gcols

    def load(g):
        if g < ng and g not in tiles:
            tiles[g] = stg.next()
            S.dma(tiles[g][0][:], d["wmod"][:, g * gcols:(g + 1) * gcols].rearrange("(k p) n -> p k n", p=128), writes=[tiles[g][1]], qn="pool")

    def group(g):
        if g == 0:
            S.dma(cpn[:], d["colpack"][:, :], writes=[b_cpn])
            S.op("act", lambda e: e.activation(out=sT[:, :, 0], in_=kp[:, K_C:K_C + 8], func=AF.Silu), reads=[b_kp], writes=[b_sT])
            S.op("act", lambda e: e.activation(out=sT[:, :, 1], in_=kp[:, K_CC:K_CC + 8], func=AF.Silu), reads=[b_kp], writes=[b_sT])
            for g_ in range(nbuf - 1):
                load(g_)
        load(g)
        wt, b_wt = tiles[g]
        for cc in range(gcols // 128):
            ch = g * (gcols // 128) + cc
            for k in range(8):
                S.op("pe", lambda e: e.matmul(out=pm[:, ch * 2:ch * 2 + 2], lhsT=wt[:, k, cc * 128:(cc + 1) * 128], rhs=sT[:, k, :],
                                              start=(k == 0), stop=(k == 7)), reads=[b_wt, b_sT], writes=[b_pm])
        load(g + nbuf - 1)

    def fin(c0=0, c1=48, a1=True):
        for j in range(2):
            S.op("dve", lambda e: e.tensor_tensor(out=mod[:, c0:c1, j], in0=pm[:, 0:96].rearrange("p (c j) -> p c j", j=2)[:, c0:c1, j],
                                                  in1=cpn[:, C_BMOD + c0:C_BMOD + c1], op=ALU.add), reads=[b_pm, b_cpn], writes=[b_mod])
        if a1:
            for j in range(2):
                S.op("dve", lambda e: e.scalar_tensor_tensor(out=A1[:, :, j], in0=mod[:, 8:16, j], scalar=1.0, in1=cpn[:, C_GPRE1:C_GPRE1 + 8],
                                                             op0=ALU.add, op1=ALU.mult), reads=[b_mod, b_cpn], writes=[b_A1])

    groups = [(lambda g=g: group(g)) for g in range(ng)]
    if split:
        n1 = 2048 // gcols
        return groups[:n1] + [lambda: fin(0, 16, True)], groups[n1:] + [lambda: fin(16, 48, False)]
    return groups + [fin]


def emit_A(P, S, st0, d, K, psf, psb, modt, need_p0, after_latent=None):
    if True:
        cp, b_cp, kp, b_kp = K["cp"], K["b_cp"], K["kp"], K["b_kp"]
        mod, b_mod, A1, b_A1 = modt
        S.mark()
        with contextlib.ExitStack() as st:
            late_p0 = []
            if need_p0:
                early, late_p0 = p0_steps(P, S, st, d, kp, b_kp, psf.items[0][0], psf.items[0][1], mod, b_mod, A1, b_A1, split=True)
                for step in early:
                    step()
                psf = Ring(psf.items[1:])
            stg = None
            w, wbs, wch = load_cast(P, S, st, "winx", wsrc(d["winx"]), 8, 2560, stg)
            wuq, wuq_b, _ = load_cast(P, S, st, "wuq", wsrc(d["wuq"]), 2, 384, stg)
            wuqs, wuqs_b, _ = load_cast(P, S, st, "wuqs", wsrc(d["wuqs"]), 2, 384, stg)
            wkk, wkk_b, _ = load_cast(P, S, st, "wukvk", wsrc(d["wukvk"]), 1, 256, stg)
            wkv, wkv_b, _ = load_cast(P, S, st, "wukvv", wsrc(d["wukvv"]), 1, 256, stg)
            cosT, b_cos = P.sb(st, "cos", [96, TA], F32)
            sinT, b_sin = P.sb(st, "sin", [96, TA], F32)
            S.dma(cosT[64:96, :], d["rope"][0, :, :], writes=[b_cos])
            S.dma(sinT[64:96, :], d["rope"][1, :, :], writes=[b_sin])
            xt_ring = Ring([P.sb(st, "xt", [128, D], F32) for _ in range(2)])
            junk, b_junk = P.sb(st, "junk", [128, D], F32)
            ss, b_ss = P.sb(st, "ss", [128, 4], F32)
            xn, b_xn = P.sb(st, "xn", [128, D], BF16)
            wk = (junk, b_junk, ss, b_ss, xn, b_xn)
            hT_ring = Ring([P.sb(st, "hT", [128, 8, 512], BF16) for _ in range(2)])
            f32r = Ring([P.sb(st, "of", [128, 512], F32) for _ in range(6)])
            bfr = Ring([P.sb(st, "ob", [128, 512], BF16) for _ in range(6)])
            nvr = Ring([P.sb(st, "onv", [128, 272], BF16) for _ in range(3)])
            mvr = Ring([P.sb(st, "omv", [128, 4, 128], BF16) for _ in range(3)])
            for t_, b_ in nvr.items:
                S.op("pool", lambda e: e.memset(t_[:, 260:272], 0.0), writes=[b_])
                S.op("pool", lambda e: e.memset(t_[:, 0:260].rearrange("p (h e) -> p h e", h=4)[:, :, 64:65], 1.0), writes=[b_])
            for t_, b_ in mvr.items:
                S.op("pool", lambda e: e.memset(t_[:, :, 64:128], 1.0), writes=[b_])
            cqf = [P.sb(st, "cqf", [128, 512], F32) for _ in range(2)]
            sqf = [P.sb(st, "sqf", [128, 512], F32) for _ in range(2)]
            cqn = [P.sb(st, "cqn", [128, 512], BF16) for _ in range(2)]
            rs, b_rs = P.sb(st, "rs", [128, 512], F32)
            ropt_ring = Ring([[P.sb(st, "ropt", [96, 512], F32) for _ in range(2)] for _ in range(2)])
            kT = [P.sb(st, "kT", [96, 512], BF16) for _ in range(4)]
            qTr = Ring([P.sb(st, "qT", [96, 512], BF16) for _ in range(2)])

            ckf = [P.sb(st, "ckf", [128, 512], F32)]
            skf = [P.sb(st, "skf", [128, 512], F32)]
            ckn = [P.sb(st, "ckn", [128, 512], BF16)]
            rs2 = P.sb(st, "rs2", [128, 512], F32)

            def rms_fm(srcs, nfeat, gcol0, outs, W, cqf=cqf, sqf=sqf, rsb=(rs, b_rs)):
                rs, b_rs = rsb
                n = len(srcs)
                for c, (ps_, b_ps_) in enumerate(srcs):
                    S.op("act", lambda e: e.activation(out=cqf[c][0][:, 0:W], in_=ps_[:, 0:W], func=AF.Copy), reads=[b_ps_], writes=[cqf[c][1]])
                    S.op("act", lambda e: e.activation(out=sqf[c][0][:, 0:W], in_=ps_[:, 0:W], func=AF.Square), reads=[b_ps_], writes=[sqf[c][1]])
                pS, b_pS = psf.next()
                for c in range(n):
                    S.op("pe", lambda e: e.matmul(out=pS[:, 0:W], lhsT=K["onef"][:, :], rhs=sqf[c][0][:, 0:W], start=(c == 0), stop=(c == n - 1)),
                         reads=[K["b_onef"], sqf[c][1]], writes=[b_pS])
                S.op("act", lambda e: e.activation(out=rs[:, 0:W], in_=pS[:, 0:W], func=AF.Sqrt, scale=1.0 / nfeat, bias=cp[:, C_EPS:C_EPS + 1]), reads=[b_pS, b_cp], writes=[b_rs])
                S.op("dve", lambda e: e.reciprocal(out=rs[:, 0:W], in_=rs[:, 0:W]), reads=[b_rs], writes=[b_rs])
                for c in range(n):
                    S.op("dve", lambda e: e.scalar_tensor_tensor(out=outs[c][0][:, 0:W], in0=cqf[c][0][:, 0:W], scalar=cp[:, gcol0 + c:gcol0 + c + 1], in1=rs[:, 0:W],
                                                                 op0=ALU.mult, op1=ALU.mult), reads=[cqf[c][1], b_cp, b_rs], writes=[outs[c][1]])

            def rope96(pa, b_pa, pb, b_pb, t0, W, out_tile, out_buf):
                (r0, b_r0), (r1, b_r1) = ropt_ring.next()
                S.op("dve", lambda e: e.tensor_tensor(out=r0[64:96, 0:W], in0=pa[64:96, 0:W], in1=cosT[64:96, t0:t0 + W], op=ALU.mult), reads=[b_pa, b_cos], writes=[b_r0])
                S.op("dve", lambda e: e.tensor_tensor(out=r1[64:96, 0:W], in0=pb[64:96, 0:W], in1=sinT[64:96, t0:t0 + W], op=ALU.mult), reads=[b_pb, b_sin], writes=[b_r1])
                S.op("pool", lambda e: e.tensor_tensor(out=out_tile[64:96, 0:W], in0=r0[64:96, 0:W], in1=r1[64:96, 0:W], op=ALU.add), reads=[b_r0, b_r1], writes=[out_buf])

            hTs = {}
            xl = XLoader(S, xt_ring)
            pseq = [(bi_, sub_) for bi_, (t0_, W_, seg_) in enumerate(BLOCKS) for sub_ in range(W_ // 128)]

            pp = ProPipe()

            def pro(bi, sub):
                if bi >= len(BLOCKS):
                    pp.flush()
                    return
                t0_, W_, seg_ = BLOCKS[bi]
                if sub >= W_ // 128:
                    pp.flush()
                    return
                if bi not in hTs:
                    hTs[bi] = hT_ring.next()
                hT_, b_hT_ = hTs[bi]
                i_ = pseq.index((bi, sub))
                nxt = None
                if i_ + 1 < len(pseq):
                    nb_, ns_ = pseq[i_ + 1]
                    nt0 = BLOCKS[nb_][0] + ns_ * 128
                    nxt = ((nb_, ns_), d["xin"][nt0:nt0 + 128, :])
                pp.step(lambda: prologue_a(P, S, d["xin"][t0_ + sub * 128:t0_ + (sub + 1) * 128, :], xl, wk, cp, b_cp, key=(bi, sub), nxt=nxt),
                        lambda: prologue_b(P, S, wk, K["idb"], K["b_idb"], psb, hT_, b_hT_, sub * 128,
                                           lambda k: A1[:, k, seg_:seg_ + 1], lambda k: mod[:, k, seg_:seg_ + 1], b_A1))

            for sub in range(4):
                pro(0, sub)
            pp.flush()
            for bi, (t0, W, seg) in enumerate(BLOCKS):
                hT, b_hT = hTs[bi]
                srcs = []
                for c in range(2):
                    pc_, b_pc_ = psf.next()
                    fm_mm(S, pc_, b_pc_, w, wbuf(wbs, wch, 768 + c * 128, 768 + c * 128 + 128), hT, b_hT, 768 + c * 128, 128, W)
                    srcs.append((pc_, b_pc_))
                rms_fm(srcs, 256, C_GQ, cqn, W)
                pc_, b_pc_ = psf.next()
                fm_mm(S, pc_, b_pc_, w, wbuf(wbs, wch, 1024, 1152), hT, b_hT, 1024, 128, W)
                rms_fm([(pc_, b_pc_)], 128, C_GKV, ckn, W, cqf=ckf, sqf=skf, rsb=rs2)
                ckvn, b_ckvn = ckn[0]
                pa, b_pa = psf.next()
                fm_mm(S, pa, b_pa, w, wbuf(wbs, wch, 1088, 1184), hT, b_hT, 1088, 96, W)
                pb, b_pb = psf.next()
                fm_mm(S, pb, b_pb, w, wbuf(wbs, wch, 2464, 2560), hT, b_hT, 2464, 96, W)
                rope96(pa, b_pa, pb, b_pb, t0, W, kT[0][0], kT[0][1])
                for h in range(1, 4):
                    S.op("pool", lambda e: e.tensor_copy(out=kT[h][0][64:96, 0:W], in_=kT[0][0][64:96, 0:W]), reads=[kT[0][1]], writes=[kT[h][1]])
                def sec_sc(c):
                    pX, b_pX = psf.next()
                    fm_mm(S, pX, b_pX, w, wbuf(wbs, wch, c * 128, c * 128 + 128), hT, b_hT, c * 128, 128, W)
                    xf, b_xf = f32r.next()
                    S.op("act", lambda e: e.activation(out=xf[:, 0:W], in_=pX[:, 0:W], func=AF.Copy), reads=[b_pX], writes=[b_xf])
                    pC, b_pC = psf.next()
                    fm_mm(S, pC, b_pC, w, wbuf(wbs, wch, 512 + c * 128, 512 + c * 128 + 128), hT, b_hT, 512 + c * 128, 128, W)
                    zt, b_zt = f32r.next()
                    S.op("dve", lambda e: e.tensor_tensor(out=zt[:, 0:W], in0=pC[:, 0:W], in1=xf[:, 0:W], op=ALU.mult), reads=[b_pC, b_xf], writes=[b_zt])
                    S.dma(d["zsc"][c * 128:(c + 1) * 128, t0:t0 + W], zt[:, 0:W], reads=[b_zt])
                    if t0 == 0:
                        S.dma(d["send_f"][c * 128:(c + 1) * 128, 0:1], zt[:, 0:1], reads=[b_zt], slow=True)
                    if t0 + W == T:
                        S.dma(d["send_f"][c * 128:(c + 1) * 128, 1:2], zt[:, W - 1:W], reads=[b_zt], slow=True)
                    pB, b_pB = psf.next()
                    fm_mm(S, pB, b_pB, w, wbuf(wbs, wch, 256 + c * 128, 256 + c * 128 + 128), hT, b_hT, 256 + c * 128, 128, W)
                    bt, b_bt = f32r.next()
                    S.op("act", lambda e: e.activation(out=bt[:, 0:W], in_=pB[:, 0:W], func=AF.Copy), reads=[b_pB], writes=[b_bt])
                    S.dma(d["bsc"][c * 128:(c + 1) * 128, t0:t0 + W], bt[:, 0:W], reads=[b_bt])
                def sec_cf(c):
                    pG, b_pG = psf.next()
                    fm_mm(S, pG, b_pG, w, wbuf(wbs, wch, 1440 + c * 128, 1440 + c * 128 + 128), hT, b_hT, 1440 + c * 128, 128, W)
                    sg, b_sg = f32r.next()
                    S.op("act", lambda e: e.activation(out=sg[:, 0:W], in_=pG[:, 0:W], func=AF.Sigmoid), reads=[b_pG], writes=[b_sg])
                    pA, b_pA = psf.next()
                    fm_mm(S, pA, b_pA, w, wbuf(wbs, wch, 1184 + c * 128, 1184 + c * 128 + 128), hT, b_hT, 1184 + c * 128, 128, W)
                    zt, b_zt = f32r.next()
                    S.op("dve", lambda e: e.tensor_tensor(out=zt[:, 0:W], in0=pA[:, 0:W], in1=sg[:, 0:W], op=ALU.mult), reads=[b_pA, b_sg], writes=[b_zt])
                    S.dma(d["zcf"][c * 128:(c + 1) * 128, t0:t0 + W], zt[:, 0:W], reads=[b_zt])
                    if t0 == 0:
                        S.dma(d["send_f"][c * 128:(c + 1) * 128, 2:17], zt[:, 0:15], reads=[b_zt])
                    if t0 + W == T:
                        S.dma(d["send_f"][c * 128:(c + 1) * 128, 17:32], zt[:, W - 15:W], reads=[b_zt])
                def sec_naqk(c):
                    pQ, b_pQ = psf.next()
                    fm_mm(S, pQ, b_pQ, w, wbuf(wbs, wch, 1696 + c * 128, 1696 + c * 128 + 128), hT, b_hT, 1696 + c * 128, 128, W)
                    ob, b_ob = bfr.next()
                    S.op("act", lambda e: e.activation(out=ob[:, 0:W], in_=pQ[:, 0:W], func=AF.Identity, scale=0.125), reads=[b_pQ], writes=[b_ob])
                    S.dma(d["nq"][c * 128:(c + 1) * 128, t0:t0 + W], ob[:, 0:W], reads=[b_ob])
                    pK, b_pK = psf.next()
                    fm_mm(S, pK, b_pK, w, wbuf(wbs, wch, 1952 + c * 128, 1952 + c * 128 + 128), hT, b_hT, 1952 + c * 128, 128, W)
                    ob, b_ob = bfr.next()
                    S.op("dve", lambda e: e.tensor_copy(out=ob[:, 0:W], in_=pK[:, 0:W]), reads=[b_pK], writes=[b_ob])
                    S.dma(d["nk"][c * 128:(c + 1) * 128, t0:t0 + W], ob[:, 0:W], reads=[b_ob])
                    if t0 == 0:
                        S.dma(d["send_nk"][c * 128:(c + 1) * 128, 0:256], ob[:, 0:256], reads=[b_ob])
                    if t0 + W == T:
                        S.dma(d["send_nk"][c * 128:(c + 1) * 128, 256:512], ob[:, W - 256:W], reads=[b_ob])
                def sec_nav(sub):
                    pV, b_pV = psf.next()
                    for k in range(8):
                        S.op("pe", lambda e: e.matmul(out=pV[:, 0:256], lhsT=hT[:, k, sub * 128:(sub + 1) * 128], rhs=w[:, k, 2208:2464], start=(k == 0), stop=(k == 7)),
                             reads=wbuf(wbs, wch, 2208, 2464) + [b_hT], writes=[b_pV])
                    ot, b_ot = nvr.next()
                    S.op("act", lambda e: e.activation(out=ot[:, 0:260].rearrange("p (h e) -> p h e", h=4)[:, :, 0:64], in_=pV[:, 0:256].rearrange("p (h e) -> p h e", h=4), func=AF.Copy),
                         reads=[b_pV], writes=[b_ot])
                    tt = t0 + sub * 128
                    S.dma(d["nv"][tt:tt + 128, :], ot[:, :], reads=[b_ot])
                    if tt < 256:
                        S.dma(d["send_nv"][tt:tt + 128, :], ot[:, :], reads=[b_ot])
                    if T - 256 <= tt < T:
                        S.dma(d["send_nv"][256 + tt - (T - 256):256 + tt - (T - 256) + 128, :], ot[:, :], reads=[b_ot])
                def sec_head(h):
                    pa, b_pa = psf.next()
                    pb, b_pb = psf.next()
                    for c in range(2):
                        S.op("pe", lambda e: e.matmul(out=pa[0:96, 0:W], lhsT=wuq[:, c, h * 96:(h + 1) * 96], rhs=cqn[c][0][:, 0:W], start=(c == 0), stop=(c == 1)),
                             reads=wuq_b + [cqn[c][1]], writes=[b_pa])
                    for c in range(2):
                        S.op("pe", lambda e: e.matmul(out=pb[0:96, 0:W], lhsT=wuqs[:, c, h * 96:(h + 1) * 96], rhs=cqn[c][0][:, 0:W], start=(c == 0), stop=(c == 1)),
                             reads=wuqs_b + [cqn[c][1]], writes=[b_pb])
                    qT, b_qT = qTr.next()
                    S.op("act", lambda e: e.activation(out=qT[0:64, 0:W], in_=pa[0:64, 0:W], func=AF.Copy), reads=[b_pa], writes=[b_qT])
                    rope96(pa, b_pa, pb, b_pb, t0, W, qT, b_qT)
                    S.dma(d["mq"][h * 96:(h + 1) * 96, t0:t0 + W], qT[:, 0:W], reads=[b_qT])
                def sec_kn(h):
                    pk_, b_pk_ = psf.next()
                    S.op("pe", lambda e: e.matmul(out=pk_[0:64, 0:W], lhsT=wkk[:, 0, h * 64:(h + 1) * 64], rhs=ckvn[:, 0:W], start=True, stop=True),
                         reads=wkk_b + [b_ckvn], writes=[b_pk_])
                    S.op("act", lambda e: e.activation(out=kT[h][0][0:64, 0:W], in_=pk_[0:64, 0:W], func=AF.Copy), reads=[b_pk_], writes=[kT[h][1]])
                    S.dma(d["mk"][h * 96:(h + 1) * 96, t0:t0 + W], kT[h][0][:, 0:W], reads=[kT[h][1]])
                def sec_mv(sub):
                    pV, b_pV = psf.next()
                    S.op("pe", lambda e: e.matmul(out=pV[:, 0:256], lhsT=ckvn[:, sub * 128:(sub + 1) * 128], rhs=wkv[:, 0, :], start=True, stop=True),
                         reads=wkv_b + [b_ckvn], writes=[b_pV])
                    ot, b_ot = mvr.next()
                    S.op("dve", lambda e: e.tensor_copy(out=ot[:, :, 0:64], in_=pV[:, 0:256].rearrange("p (h e) -> p h e", h=4)), reads=[b_pV], writes=[b_ot])
                    tt = t0 + sub * 128
                    mvn, mvo = ("mva", tt) if tt < 1024 else (("mvb", tt - 1024) if tt < T else ("mvc", tt - T))
                    S.dma(d[mvn][mvo:mvo + 128, :], ot[:].rearrange("p h e -> p (h e)"), reads=[b_ot])

                sec_sc(0)
                sec_sc(1)
                pro(bi + 1, 0)
                sec_head(0)
                sec_cf(0)
                sec_head(1)
                sec_cf(1)
                pro(bi + 1, 1)
                sec_head(2)
                sec_naqk(0)
                sec_head(3)
                sec_naqk(1)
                pro(bi + 1, 2)
                for sub in range(W // 128):
                    sec_nav(sub)
                for h in range(4):
                    sec_kn(h)
                pro(bi + 1, 3)
                for sub in range(W // 128):
                    sec_mv(sub)
                pp.flush()
                if after_latent is not None and t0 + W == T:
                    after_latent()
                for _ in range(2):
                    if late_p0:
                        late_p0.pop(0)()
            while late_p0:
                late_p0.pop(0)()
            S.barrier()
        S.release()
    return mod, b_mod


def emit_B(P, S, st0, d, K, psf, psb, mod, b_mod, last, p0_next=None):
    nblk = 4 if last else 5
    ntile = 16 if last else 18
    if True:
        S.mark()
        cp, b_cp, kp, b_kp = K["cp"], K["b_cp"], K["kp"], K["b_kp"]
        idb, b_idb, idf, b_idf, onef, b_onef = K["idb"], K["b_idb"], K["idf"], K["b_idf"], K["onef"], K["b_onef"]
        A2, b_A2 = P.sb(st0, "A2", [128, 8, 2], F32)
        for j in range(2):
            S.op("dve", lambda e: e.scalar_tensor_tensor(out=A2[:, :, j], in0=mod[:, 32:40, j], scalar=1.0, in1=cp[:, C_GPRE2:C_GPRE2 + 8],
                                                         op0=ALU.add, op1=ALU.mult), reads=[b_mod, b_cp], writes=[b_A2])
        gfm, b_gfm = P.sb(st0, "gfm", [128, 8], F32)
        dg, b_dg = P.sb(st0, "dg", [128, 128], F32)

        def build_G(st, gc, mc, ring=None):
            ring = ring or psf
            Gs = [P.sb(st, "G", [128, D], F32) for _ in range(2)]
            for seg in range(2):
                S.op("dve", lambda e: e.tensor_tensor(out=gfm[:, :], in0=mod[:, mc:mc + 8, seg], in1=cp[:, gc:gc + 8], op=ALU.mult), reads=[b_mod, b_cp], writes=[b_gfm])
                for k in range(8):
                    S.op("dve", lambda e: e.tensor_scalar(out=dg[:, :], in0=idf[:, :], scalar1=gfm[:, k:k + 1], scalar2=None, op0=ALU.mult), reads=[b_idf, b_gfm], writes=[b_dg])
                    pg, b_pg = ring.next()
                    S.op("pe", lambda e: e.matmul(out=pg[:, 0:128], lhsT=onef[:, :], rhs=dg[:, :], start=True, stop=True), reads=[b_onef, b_dg], writes=[b_pg])
                    S.op("act", lambda e: e.activation(out=Gs[seg][0][:, k * 128:(k + 1) * 128], in_=pg[:, 0:128], func=AF.Copy), reads=[b_pg], writes=[Gs[seg][1]])
            return Gs

        stW = contextlib.ExitStack()
        w1_t = P.sb(stW, "w1", [128, 8, 4096], BF16)[0]
        stgW = None
        stY = contextlib.ExitStack()
        YT, b_YT = P.sb(stY, "YT", [128, 8, TA], BF16)
        b_YTm = [Buf("ytm") for _ in range(4)]

        def epilogue(pY, xtile, out_dst, Gt, wk):
            junk, b_junk, ss, b_ss, ot_ring = wk
            if isinstance(junk, Ring):
                junk, b_junk = junk.next()
            if isinstance(ss, Ring):
                ss, b_ss = ss.next()
            xt, b_xt = xtile
            for i, (p_, b_p) in enumerate(pY):
                S.op("act", lambda e: e.activation(out=junk[:, 0:512], in_=p_[:, 0:512], func=AF.Square, accum_out=ss[:, i:i + 1]), reads=[b_p], writes=[b_junk, b_ss])
            S.op("dve", lambda e: e.tensor_tensor(out=ss[:, 2:3], in0=ss[:, 0:1], in1=ss[:, 1:2], op=ALU.add), reads=[b_ss], writes=[b_ss])
            S.op("act", lambda e: e.activation(out=ss[:, 3:4], in_=ss[:, 2:3], func=AF.Sqrt, scale=1.0 / D, bias=cp[:, C_EPS:C_EPS + 1]), reads=[b_ss, b_cp], writes=[b_ss])
            S.op("dve", lambda e: e.reciprocal(out=ss[:, 4:5], in_=ss[:, 3:4]), reads=[b_ss], writes=[b_ss])
            ot, b_ot = ot_ring.next()
            for i, (p_, b_p) in enumerate(pY):
                S.op("dve", lambda e: e.scalar_tensor_tensor(out=ot[:, i * 512:(i + 1) * 512], in0=p_[:, 0:512], scalar=ss[:, 4:5], in1=Gt[0][:, i * 512:(i + 1) * 512],
                                                             op0=ALU.mult, op1=ALU.mult), reads=[b_p, b_ss, Gt[1]], writes=[b_ot])
            S.op("pool", lambda e: e.tensor_tensor(out=ot[:, :], in0=ot[:, :], in1=xt[:, :], op=ALU.add), reads=[b_ot, b_xt], writes=[b_ot])
            S.dma(out_dst, ot[:, :], reads=[b_ot])

        def tm_to_YT(ytm_ap_fn, b_ytm, chunk0, col0, b_y):
            pt, b_pt = psb.next()
            for c in range(2):
                S.op("pe", lambda e: e.transpose(out=pt[:, c * 128:(c + 1) * 128], in_=ytm_ap_fn(c), identity=idb[:]), reads=[b_ytm, b_idb], writes=[b_pt])
            S.op("dve", lambda e: e.tensor_copy(out=YT[:, chunk0:chunk0 + 2, col0:col0 + 128], in_=pt[:, 0:256].rearrange("p (c t) -> p c t", c=2)), reads=[b_pt], writes=[b_y])

        stSC = contextlib.ExitStack()
        st = stSC
        if True:
            zs, b_zs = P.sb(st, "zs", [128, 2, T + 2], F32)
            b_zsl = []
            b_Bsl = []
            Bs, b_Bs = P.sb(st, "Bs", [128, 2, T], F32)
            acc, b_acc = P.sb(st, "acc", [128, 2, T], F32)
            for c in range(2):
                b_zsl.append(Buf("zsl"))
                S.dma(zs[:, c, 1:T + 1], d["zsc"][c * 128:(c + 1) * 128, 0:T], writes=[b_zsl[-1]])
                S.dma(zs[:, c, 0:1], d["g_f"][c * 128:(c + 1) * 128, 1:2], reads=[d["b_g_f"]], writes=[b_zs], slow=True)
                S.dma(zs[:, c, T + 1:T + 2], d["g_f"][256 + c * 128:256 + (c + 1) * 128, 0:1], reads=[d["b_g_f"]], writes=[b_zs], slow=True)
                b_Bsl.append(Buf("Bsl"))
                S.dma(Bs[:, c, :], d["bsc"][c * 128:(c + 1) * 128, 0:T], writes=[b_Bsl[-1]])
        if True:
            zc, b_zc = P.sb(st, "zc", [128, 2, T + 30], BF16)
            b_acc_sc = b_acc
            stg = None
            wpw, wpw_b, _ = load_cast(P, S, st, "wpw", wsrc(d["wpw"]), 2, 256, stg)
            for c in range(2):
                S.dma(zc[:, c, 15:T + 15], d["zcf"][c * 128:(c + 1) * 128, 0:T], writes=[b_zc], qn="pool")
                S.dma(zc[:, c, 0:15], d["g_f"][c * 128:(c + 1) * 128, 17:32], reads=[d["b_g_f"]], writes=[b_zc], qn="pool")
                S.dma(zc[:, c, T + 15:T + 30], d["g_f"][256 + c * 128:256 + (c + 1) * 128, 2:17], reads=[d["b_g_f"]], writes=[b_zc], qn="pool")
            S.op("dve", lambda e: e.tensor_scalar(out=zc[:, :, 0:15], in0=zc[:, :, 0:15], scalar1=kp[:, K_ML:K_ML + 1], scalar2=None, op0=ALU.mult), reads=[b_zc, b_kp], writes=[b_zc])
            S.op("dve", lambda e: e.tensor_scalar(out=zc[:, :, T + 15:T + 30], in0=zc[:, :, T + 15:T + 30], scalar1=kp[:, K_MR:K_MR + 1], scalar2=None, op0=ALU.mult), reads=[b_zc, b_kp], writes=[b_zc])
            Dg, b_Dg = P.sb(st, "Dg", [128, 2, 31, 128], BF16)
            b_Dgc = [Buf("Dg0"), Buf("Dg1")]
            for c in range(2):
                for j in range(31):
                    col = C_WDW + c * 31 + j
                    if j % 2 == 0:
                        S.op("dve", lambda e: e.tensor_scalar(out=Dg[:, c, j, :], in0=idf[:, :], scalar1=cp[:, col:col + 1], scalar2=None, op0=ALU.mult),
                             reads=[b_idf, b_cp], writes=[b_Dgc[c]])
                    else:
                        S.op("act", lambda e: e.activation(out=Dg[:, c, j, :], in_=idf[:, :], func=AF.Copy, scale=cp[:, col:col + 1]),
                             reads=[b_idf, b_cp], writes=[b_Dgc[c]])
            S.op("dve", lambda e: e.tensor_scalar(out=zs[:, :, 0:1], in0=zs[:, :, 0:1], scalar1=kp[:, K_ML:K_ML + 1], scalar2=None, op0=ALU.mult), reads=[b_zs, b_kp] + b_zsl, writes=[b_zs])
            S.op("dve", lambda e: e.tensor_scalar(out=zs[:, :, T + 1:T + 2], in0=zs[:, :, T + 1:T + 2], scalar1=kp[:, K_MR:K_MR + 1], scalar2=None, op0=ALU.mult), reads=[b_zs, b_kp], writes=[b_zs])

            def sc_conv(z_t, b_z, B_t, b_B, n, ycol0):
                for c in range(2):
                    wc = lambda j: cp[:, C_WSC + c * 3 + j:C_WSC + c * 3 + j + 1]
                    S.op("dve", lambda e: e.tensor_scalar(out=acc[:, c, 0:n], in0=z_t[:, c, 0:n], scalar1=wc(0), scalar2=None, op0=ALU.mult), reads=[b_z, b_cp], writes=[b_acc])
                    for j in (1, 2):
                        S.op("dve", lambda e: e.scalar_tensor_tensor(out=acc[:, c, 0:n], in0=z_t[:, c, j:j + n], scalar=wc(j), in1=acc[:, c, 0:n], op0=ALU.mult, op1=ALU.add),
                             reads=[b_z, b_cp, b_acc], writes=[b_acc])
                    S.op("dve", lambda e: e.tensor_tensor(out=YT[:, c, ycol0:ycol0 + n], in0=acc[:, c, 0:n], in1=B_t[:, c, 0:n], op=ALU.mult), reads=[b_acc] + list(b_B), writes=[b_YTm[0]])

            sc_conv(zs, b_zs, Bs, b_Bsl, T, 0)
            if not last:
                zsc_, b_zsc_ = P.sb(st, "zsc", [128, 2, TC + 2], F32)
                Bc, b_Bc = P.sb(st, "Bc", [128, 2, TC], F32)
                S.op("pool", lambda e: e.memset(zsc_[:], 0.0), writes=[b_zsc_])
                for c in range(2):
                    S.dma(zsc_[:, c, 1:TC + 1], d["zsc"][c * 128:(c + 1) * 128, T:TA], writes=[b_zsc_])
                    S.dma(Bc[:, c, :], d["bsc"][c * 128:(c + 1) * 128, T:TA], writes=[b_Bc])
                sc_conv(zsc_, b_zsc_, Bc, [b_Bc], TC, T)
            d["late_gathers"]()
            sqf = [P.sb(st, "sqf", [128, 512], F32) for _ in range(2)]
            mean, b_mean = P.sb(st, "mean", [128, 512], F32)
            msq, b_msq = P.sb(st, "msq", [128, 512], F32)
            rstd, b_rstd = P.sb(st, "rstd", [128, 512], F32)
            tn = [P.sb(st, "tn", [128, 512], F32) for _ in range(2)]
            sbf = [P.sb(st, "sbf", [128, 512], BF16) for _ in range(2)]

            def cf_conv(z_t, b_z, n, ycol0):
                blocks = [(b0, min(512, n - b0)) for b0 in range(0, n, 512)]
                b_accs = [Buf("accb") for _ in blocks]

                def conv(i):
                    b0, W = blocks[i]
                    for c in range(2):
                        pc, b_pc = psf.next()
                        for j in range(31):
                            S.op("pe", lambda e: e.matmul(out=pc[:, 0:W], lhsT=Dg[:, c, j, :], rhs=z_t[:, c, b0 + j:b0 + j + W], start=(j == 0), stop=(j == 30)),
                                 reads=[b_Dgc[c], b_z], writes=[b_pc])
                        S.op("act", lambda e: e.activation(out=acc[:, c, b0:b0 + W], in_=pc[:, 0:W], func=AF.Identity, bias=cp[:, C_BDW + c:C_BDW + c + 1]),
                             reads=[b_pc, b_cp], writes=[b_accs[i], b_acc_sc])

                def ln(i):
                    b0, W = blocks[i]
                    b_acc = b_accs[i]
                    for c in range(2):
                        S.op("act", lambda e: e.activation(out=sqf[c][0][:, 0:W], in_=acc[:, c, b0:b0 + W], func=AF.Square), reads=[b_acc], writes=[sqf[c][1]])
                    p1, b_p1 = psf.next()
                    p2, b_p2 = psf.next()
                    for c in range(2):
                        S.op("pe", lambda e: e.matmul(out=p1[:, 0:W], lhsT=onef[:, :], rhs=acc[:, c, b0:b0 + W], start=(c == 0), stop=(c == 1)), reads=[b_onef, b_acc], writes=[b_p1])
                    for c in range(2):
                        S.op("pe", lambda e: e.matmul(out=p2[:, 0:W], lhsT=onef[:, :], rhs=sqf[c][0][:, 0:W], start=(c == 0), stop=(c == 1)), reads=[b_onef, sqf[c][1]], writes=[b_p2])
                    S.op("act", lambda e: e.activation(out=mean[:, 0:W], in_=p1[:, 0:W], func=AF.Copy, scale=1.0 / 256), reads=[b_p1], writes=[b_mean])
                    S.op("dve", lambda e: e.tensor_tensor(out=msq[:, 0:W], in0=mean[:, 0:W], in1=mean[:, 0:W], op=ALU.mult), reads=[b_mean], writes=[b_msq])
                    S.op("dve", lambda e: e.scalar_tensor_tensor(out=rstd[:, 0:W], in0=p2[:, 0:W], scalar=1.0 / 256, in1=msq[:, 0:W], op0=ALU.mult, op1=ALU.subtract),
                         reads=[b_p2, b_msq], writes=[b_rstd])
                    S.op("act", lambda e: e.activation(out=rstd[:, 0:W], in_=rstd[:, 0:W], func=AF.Sqrt, bias=cp[:, C_EPS:C_EPS + 1]), reads=[b_rstd, b_cp], writes=[b_rstd])
                    S.op("dve", lambda e: e.reciprocal(out=rstd[:, 0:W], in_=rstd[:, 0:W]), reads=[b_rstd], writes=[b_rstd])
                    for c in range(2):
                        S.op("dve", lambda e: e.tensor_tensor(out=tn[c][0][:, 0:W], in0=acc[:, c, b0:b0 + W], in1=mean[:, 0:W], op=ALU.subtract), reads=[b_acc, b_mean], writes=[tn[c][1]])
                        S.op("dve", lambda e: e.tensor_tensor(out=tn[c][0][:, 0:W], in0=tn[c][0][:, 0:W], in1=rstd[:, 0:W], op=ALU.mult), reads=[tn[c][1], b_rstd], writes=[tn[c][1]])
                        S.op("act", lambda e: e.activation(out=sbf[c][0][:, 0:W], in_=tn[c][0][:, 0:W], func=AF.Silu, scale=cp[:, C_LNG + c:C_LNG + c + 1], bias=cp[:, C_LNB + c:C_LNB + c + 1]),
                             reads=[tn[c][1], b_cp], writes=[sbf[c][1]])
                    for co in range(2):
                        po, b_po = psf.next()
                        for c in range(2):
                            S.op("pe", lambda e: e.matmul(out=po[:, 0:W], lhsT=wpw[:, c, co * 128:(co + 1) * 128], rhs=sbf[c][0][:, 0:W], start=(c == 0), stop=(c == 1)),
                                 reads=wpw_b + [sbf[c][1]], writes=[b_po])
                        S.op("act", lambda e: e.activation(out=YT[:, 4 + co, ycol0 + b0:ycol0 + b0 + W], in_=po[:, 0:W], func=AF.Copy), reads=[b_po], writes=[b_YTm[2]])

                for i in range(len(blocks) + 1):
                    if i < len(blocks):
                        conv(i)
                    if i >= 1:
                        ln(i - 1)

            cf_conv(zc, b_zc, T, 0)
            if not last:
                zcc, b_zcc = P.sb(st, "zcc", [128, 2, TC + 30], BF16)
                S.op("pool", lambda e: e.memset(zcc[:], 0.0), writes=[b_zcc])
                for c in range(2):
                    S.dma(zcc[:, c, 15:TC + 15], d["zcf"][c * 128:(c + 1) * 128, T:TA], writes=[b_zcc], qn="pool")
                cf_conv(zcc, b_zcc, TC, T)
            S.barrier()
        stSC.close()
        S.release()
        S.mark()

        NKT = 34
        stKT = contextlib.ExitStack()
        KT = P.sb(stKT, "KT", [96, 4, NKT * 128], BF16)[0]
        b_KTh = [Buf("KTh") for _ in range(4)]

        with contextlib.ExitStack() as st:
            NK, b_NK = P.sb(st, "NK", [128, 2, T + 512], BF16)
            b_NKs = []
            b_NQs = []
            NKc, b_NKc = P.sb(st, "NKc", [128, 2, TC], BF16)
            NV, b_NV = P.sb(st, "NV", [128, 20, 4, 65], BF16)
            NVc, b_NVc = P.sb(st, "NVc", [128, 2, 4, 65], BF16)
            NQ, b_NQ = P.sb(st, "NQ", [128, 2, TA], BF16)
            RB, b_RB = P.sb(st, "RB", [128, 4, 16, 64], F32)
            S.dma(RB[:].rearrange("p h o c -> p (h o c)"), d["rbt"][:, :], writes=[b_RB])
            CB, b_CB = P.sb(st, "CB", [128, 4, 5, 128], BF16)
            for h in range(4):
                for di, dk in enumerate(range(-2, 3)):
                    for qr in range(2):
                        idx = K_MB + NA_IDX[(2, dk, qr)]
                        S.op("dve", lambda e: e.tensor_scalar(out=CB[:, h, di, qr * 64:(qr + 1) * 64], in0=RB[:, h, 8 - 2 * dk + qr, :], scalar1=kp[:, idx:idx + 1], scalar2=None, op0=ALU.add),
                             reads=[b_RB, b_kp], writes=[b_CB])
            b_NVg = [Buf("NVg") for _ in range(5)]
            for c in range(2):
                b_NKs.append(Buf("NKa"))
                S.dma(NK[:, c, 256:T + 256], d["nk"][c * 128:(c + 1) * 128, 0:T], writes=[b_NKs[-1]])
                b_NKs.append(Buf("NKb"))
                S.dma(NK[:, c, 0:256], d["g_nk"][c * 128:(c + 1) * 128, 256:512], reads=[d["b_g_nk"]], writes=[b_NKs[-1]])
                b_NKs.append(Buf("NKc_"))
                S.dma(NK[:, c, T + 256:T + 512], d["g_nk"][256 + c * 128:256 + (c + 1) * 128, 0:256], reads=[d["b_g_nk"]], writes=[b_NKs[-1]])
                S.dma(NKc[:, c, :], d["nk"][c * 128:(c + 1) * 128, T:TA], writes=[b_NKc])
                b_NQs.append(Buf("NQl"))
                S.dma(NQ[:, c, :], d["nq"][c * 128:(c + 1) * 128, :], writes=[b_NQs[-1]])
            nvsrc = lambda ap: ap.rearrange("(i p) c -> p i c", p=128)
            S.dma(NV[:, 0:2, :, :].rearrange("p i h e -> p i (h e)"), nvsrc(d["g_nv"][256:512, 0:260]), reads=[d["b_g_nv"]], writes=[b_NVg[0]], qn="act")
            for g_ in range(5):
                i0, i1 = max(2, g_ * 4), min(18, g_ * 4 + 4)
                S.dma(NV[:, i0:i1, :, :].rearrange("p i h e -> p i (h e)"), nvsrc(d["nv"][(i0 - 2) * 128:(i1 - 2) * 128, 0:260]), writes=[b_NVg[g_]], qn="act")
            S.dma(NV[:, 18:20, :, :].rearrange("p i h e -> p i (h e)"), nvsrc(d["g_nv"][512:768, 0:260]), reads=[d["b_g_nv"]], writes=[b_NVg[4]], qn="act")
            S.dma(NVc[:, :, :, :].rearrange("p i h e -> p i (h e)"), nvsrc(d["nv"][T:TA, 0:260]), writes=[b_NVc])
            for h in range(4):
                for r in range(2):
                    S.dma(KT[:, h, r * T:(r + 1) * T], d["g_mk"][r * 384 + h * 96:r * 384 + (h + 1) * 96, 0:T], reads=[d["b_g_mk"]], writes=[b_KTh[h]])
                S.dma(KT[:, h, 2 * T:2 * T + TC], d["mk"][h * 96:(h + 1) * 96, T:TA], writes=[b_KTh[h]])
            sbr = Ring([P.sb(st, "nsb", [128, 6, 128], F32) for _ in range(3)])
            ptr = Ring([P.sb(st, "npt", [128, 8, 128], BF16) for _ in range(3)])
            rec, b_rec = P.sb(st, "nrec", [128, 4], F32)
            ytr = Ring([P.sb(st, "nyt", [128, 256], BF16) for _ in range(2)])

            na_po = Ring(psf.items[0:1])
            na_pl = Ring(psf.items[1:7])

            tiles = [(j * 128, na_slots(j), j) for j in range(16)]
            if not last:
                tiles += [(T + jc * 128, [], 0) for jc in range(2)]
            units = []
            for ti, (qcol, slots, j) in enumerate(tiles):
                tctx = {}
                for h in range(4):
                    units.append(dict(qcol=qcol, slots=slots, j=j, h=h, t=tctx))

            def na_stage1(u):
                qcol, slots, j, h = u["qcol"], u["slots"], u["j"], u["h"]
                nl = len(slots)
                hp, c = (h % 2) * 64, h // 2
                pl = [na_pl.next(), na_pl.next()]
                u["pl"] = pl
                q_ap = NQ[hp:hp + 64, c, qcol:qcol + 128]
                interior_all = (2 <= j <= 13) and len(slots) == 5
                interior = interior_all and h % 2 == 0
                for si, dk in enumerate(slots):
                    p_, b_p = pl[si // 4]
                    kc = (j + dk + 2) * 128
                    S.op("pe", lambda e: e.matmul(out=p_[:, (si % 4) * 128:(si % 4 + 1) * 128], lhsT=NK[hp:hp + 64, c, kc:kc + 128], rhs=q_ap, start=True, stop=not interior),
                         reads=b_NKs + b_NQs, writes=[b_p])
                    if interior:
                        S.op("pe", lambda e: e.matmul(out=p_[:, (si % 4) * 128:(si % 4 + 1) * 128], lhsT=idb[:, :], rhs=CB[:, h, dk + 2, :], start=False, stop=True),
                             reads=[b_idb, b_CB], writes=[b_p])
                for i in range(2):
                    si = nl + i
                    p_, b_p = pl[si // 4]
                    S.op("pe", lambda e: e.matmul(out=p_[:, (si % 4) * 128:(si % 4 + 1) * 128], lhsT=NKc[hp:hp + 64, c, i * 128:(i + 1) * 128], rhs=q_ap, start=True, stop=True),
                         reads=[b_NKc] + b_NQs, writes=[b_p])
                sb_, b_sb = sbr.next()
                pt, b_pt = ptr.next()
                u["pt"] = (pt, b_pt)
                if interior:
                    S.op("act", lambda e: e.activation(out=pt[:, 0:4, :], in_=pl[0][0][:, 0:512].rearrange("p (s q) -> p s q", s=4), func=AF.Exp), reads=[pl[0][1]], writes=[b_pt])
                    S.op("act", lambda e: e.activation(out=pt[:, 4:7, :], in_=pl[1][0][:, 0:384].rearrange("p (s q) -> p s q", s=3), func=AF.Exp), reads=[pl[1][1]], writes=[b_pt])
                    return
                for si, dk in enumerate(slots):
                    p_, b_p = pl[si // 4]
                    o0 = 8 - 2 * dk
                    if interior_all:
                        S.op("dve", lambda e: e.tensor_tensor(out=sb_[:, si, :], in0=p_[:, (si % 4) * 128:(si % 4 + 1) * 128], in1=CB[:, h, dk + 2, :], op=ALU.add),
                             reads=[b_p, b_CB], writes=[b_sb])
                        continue
                    for qr in range(2):
                        idx = K_MB + NA_IDX[(j, dk, qr)]
                        S.op("dve", lambda e: e.scalar_tensor_tensor(out=sb_[:, si, qr * 64:(qr + 1) * 64], in0=p_[:, (si % 4) * 128 + qr * 64:(si % 4) * 128 + (qr + 1) * 64],
                                                                     scalar=kp[:, idx:idx + 1], in1=RB[:, h, o0 + qr, :], op0=ALU.add, op1=ALU.add),
                             reads=[b_p, b_kp, b_RB], writes=[b_sb])
                if nl:
                    S.op("act", lambda e: e.activation(out=pt[:, 0:nl, :], in_=sb_[:, 0:nl, :], func=AF.Exp), reads=[b_sb], writes=[b_pt])
                for i in range(2):
                    si = nl + i
                    p_, b_p = pl[si // 4]
                    S.op("act", lambda e: e.activation(out=pt[:, si, :], in_=p_[:, (si % 4) * 128:(si % 4 + 1) * 128], func=AF.Exp), reads=[b_p], writes=[b_pt])

            def na_stage2(u):
                qcol, slots, j, h, tctx = u["qcol"], u["slots"], u["j"], u["h"], u["t"]
                nl = len(slots)
                if h == 0:
                    tctx["po"] = na_po.next()
                    tctx["yt"] = ytr.next()
                po, b_po = tctx["po"]
                yt, b_yt = tctx["yt"]
                pt, b_pt = u["pt"]
                for si in range(nl + 2):
                    if si < nl:
                        v_ap, bv = NV[:, j + slots[si] + 2, h, :], b_NVg[(j + slots[si] + 2) // 4]
                    else:
                        v_ap, bv = NVc[:, si - nl, h, :], b_NVc
                    S.op("pe", lambda e: e.matmul(out=po[:, h * 65:(h + 1) * 65], lhsT=pt[:, si, :], rhs=v_ap, start=(si == 0), stop=(si == nl + 1)),
                         reads=[b_pt, bv], writes=[b_po])
                if h == 3:
                    S.op("dve", lambda e: e.reciprocal(out=rec[:, :], in_=po[:, 0:260].rearrange("p (h e) -> p h e", e=65)[:, :, 64]), reads=[b_po], writes=[b_rec])
                    for hh in range(4):
                        S.op("dve", lambda e: e.tensor_scalar(out=yt[:, hh * 64:(hh + 1) * 64], in0=po[:, hh * 65:hh * 65 + 64], scalar1=rec[:, hh:hh + 1], scalar2=None, op0=ALU.mult),
                             reads=[b_po, b_rec], writes=[b_yt])
                    tm_to_YT(lambda c: yt[:, c * 128:(c + 1) * 128], b_yt, 6, qcol, b_YTm[3])

            NSK = 2
            for i in range(len(units) + NSK):
                if i < len(units):
                    na_stage1(units[i])
                if i - NSK >= 0:
                    na_stage2(units[i - NSK])
            S.barrier()
        S.release()
        S.mark()

        with contextlib.ExitStack() as st:
            MV = P.sb(st, "MV", [128, NKT, 4, 128], BF16)[0]
            b_MVg = [Buf("MVg") for _ in range(5)]
            mvsrc = lambda ap: ap.rearrange("(i p) c -> p i c", p=128)
            for g_ in range(4):
                r, i0 = g_ // 2, (g_ % 2) * 8
                gn = "g_mva" if g_ % 2 == 0 else "g_mvb"
                S.dma(MV[:, g_ * 8:g_ * 8 + 8, :, :].rearrange("p i h e -> p i (h e)"), mvsrc(d[gn][r * 1024:(r + 1) * 1024, :]),
                      reads=[d["b_" + gn]], writes=[b_MVg[g_]], qn="act")
            S.dma(MV[:, 32:34, :, :].rearrange("p i h e -> p i (h e)"), mvsrc(d["mvc"][:, :]), writes=[b_MVg[4]], qn="act")
            p0s = p0_next(st, psf.items[5]) if p0_next else []
            mla_ps = Ring(psf.items[2:5] + psf.items[6:7]) if p0_next else Ring(psf.items[2:7])
            mla_po = Ring(psf.items[0:2])
            QTr = Ring([P.sb(st, "QT", [96, 4, 512], BF16) for _ in range(2)])
            PTr = Ring([P.sb(st, "PT", [128, 512], BF16) for _ in range(4)])
            recr = Ring([P.sb(st, "mrec", [64, 512], F32) for _ in range(1)])
            first = True
            for (t0, W, seg) in BLOCKS[:nblk]:
                QT, b_QT = QTr.next()
                for h in range(4):
                    S.dma(QT[:, h, 0:W], d["mq"][h * 96:(h + 1) * 96, t0:t0 + W], writes=[b_QT])
                kts = list(range(NKT)) if seg == 0 else [32, 33]
                its = [dict(h=h, ki=ki, kt=kt) for h in range(4) for ki, kt in enumerate(kts)]
                hctx = {}

                def mla_s1(it):
                    h, kt = it["h"], it["kt"]
                    pS_, b_pS = mla_ps.next()
                    S.op("pe", lambda e: e.matmul(out=pS_[:, 0:W], lhsT=KT[:, h, kt * 128:(kt + 1) * 128], rhs=QT[:, h, 0:W], start=True, stop=True),
                         reads=[b_KTh[h], b_QT], writes=[b_pS])
                    PT, b_PT = PTr.next()
                    it["PT"] = (PT, b_PT)
                    S.op("act", lambda e: e.activation(out=PT[:, 0:W], in_=pS_[:, 0:W], func=AF.Exp, scale=MLA_SCALE), reads=[b_pS], writes=[b_PT])

                def mla_s2(it):
                    h, ki, kt = it["h"], it["ki"], it["kt"]
                    if ki == 0:
                        hctx[h] = mla_po.next()
                    pO, b_pO = hctx[h]
                    PT, b_PT = it["PT"]
                    S.op("pe", lambda e: e.matmul(out=pO[:, 0:W], lhsT=MV[:, kt, h, :], rhs=PT[:, 0:W], start=(ki == 0), stop=(ki == len(kts) - 1)),
                         reads=[b_PT, b_MVg[kt // 8]], writes=[b_pO])
                    if ki == len(kts) - 1:
                        rec, b_rec = recr.next()
                        S.op("dve", lambda e: e.reciprocal(out=rec[0:64, 0:W], in_=pO[64:128, 0:W]), reads=[b_pO], writes=[b_rec])
                        hp = (h % 2) * 64
                        S.op("dve", lambda e: e.tensor_tensor(out=YT[hp:hp + 64, 2 + h // 2, t0:t0 + W], in0=pO[0:64, 0:W], in1=rec[0:64, 0:W], op=ALU.mult),
                             reads=[b_pO, b_rec], writes=[b_YTm[1]])

                MSK = 2
                for i in range(len(its) + MSK):
                    if i < len(its):
                        mla_s1(its[i])
                    if i - MSK >= 0:
                        mla_s2(its[i - MSK])
                    if i % 8 == 7 and p0s:
                        p0s.pop(0)()
                if first and (t0 >= 512 or not p0_next):
                    first = False
                    w1, w1_b, w1ch = load_cast(P, S, stW, "w1", wsrc(d["wmlp1"]), 8, 4096, stgW, chunk=512, pre=w1_t)
            while p0s:
                p0s.pop(0)()
            S.barrier()
        S.release()
        S.mark()

        stKT.close()
        with contextlib.ExitStack() as st:
            stg = None
            wo, wo_b, woch = load_cast(P, S, st, "wout", wsrc(d["wout"]), 8, 1024, stg)
            ring3 = psf
            G1 = build_G(st, C_GPOST1, 16, ring3)
            steps = []
            xt_ring = Ring([P.sb(st, "xt", [128, D], F32) for _ in range(3)])
            junk_ring = Ring([P.sb(st, "junk", [128, 512], F32) for _ in range(3)])
            ss_ring = Ring([P.sb(st, "ss", [128, 8], F32) for _ in range(3)])
            ot_ring = Ring([P.sb(st, "ot", [128, D], F32) for _ in range(3)])
            wk = (junk_ring, None, ss_ring, None, ot_ring)
            xl = XLoader(S, xt_ring)
            for ti in range(ntile):
                seg = 0 if ti < 16 else 1
                xl.prefetch(("e", ti), d["xin"][ti * 128:(ti + 1) * 128, :])
                pY = [ring3.next(), ring3.next()]
                for nb in range(2):
                    for k in range(8):
                        S.op("pe", lambda e: e.matmul(out=pY[nb][0][:, 0:512], lhsT=YT[:, k, ti * 128:(ti + 1) * 128], rhs=wo[:, k, nb * 512:(nb + 1) * 512], start=(k == 0), stop=(k == 7)),
                             reads=b_YTm + wbuf(wo_b, woch, nb * 512, nb * 512 + 512), writes=[pY[nb][1]])
                epilogue(pY, xl.get(("e", ti), None), d["xmid"][ti * 128:(ti + 1) * 128, :], G1[seg], wk)
                if steps:
                    steps.pop(0)()
            while steps:
                steps.pop(0)()
            S.barrier()
        S.release()
        S.mark()

        stY.close()
        with contextlib.ExitStack() as st:
            stg = stgW
            w2t, w2_b, w2ch = load_cast(P, S, st, "w2", wsrc(d["wmlp2"]), 32, 1024, stg, chunk=512, per_k=True)
            w2 = (w2t,)
            xt_ring = Ring([P.sb(st, "xt", [128, D], F32) for _ in range(2)])
            ss, b_ss = P.sb(st, "ss", [128, 8], F32)
            xn, b_xn = P.sb(st, "xn", [128, D], BF16)
            wkp = (xn, b_xn, ss, b_ss, xn, b_xn)
            ot_ring = Ring([P.sb(st, "ot", [128, D], F32) for _ in range(1)])
            wke = (ot_ring.items[0][0], ot_ring.items[0][1], ss, b_ss, ot_ring)
            hT_ring = Ring([P.sb(st, "hT", [128, 8, 512], BF16) for _ in range(2)])
            h1, b_h1 = P.sb(st, "h1", [128, 32, 512], BF16)
            rl_ring = Ring([P.sb(st, "rl", [128, 512], F32) for _ in range(2)])
            mblocks = BLOCKS[:nblk]
            hTs = {}

            pp = ProPipe()

            def pro2_parts(bi_, sub, wk_):
                t0_, W_, seg_ = mblocks[bi_]
                if bi_ not in hTs:
                    hTs[bi_] = hT_ring.next()
                hT_, b_hT_ = hTs[bi_]
                tt_ = t0_ + sub * 128
                nxt = (("p", tt_ + 128), d["xmid"][tt_ + 128:tt_ + 256, :]) if tt_ + 128 < ntile * 128 else None
                return (lambda: prologue_a(P, S, d["xmid"][tt_:tt_ + 128, :], xl, wk_, cp, b_cp, key=("p", tt_), nxt=nxt),
                        lambda: prologue_b(P, S, wk_, idb, b_idb, psb, hT_, b_hT_, sub * 128,
                                           lambda k: A2[:, k, seg_:seg_ + 1], lambda k: mod[:, 24 + k, seg_:seg_ + 1], b_A2))

            def pro2(bi_, sub):
                if bi_ >= len(mblocks) or sub >= mblocks[bi_][1] // 128:
                    pp.flush()
                    return
                a_fn, b_fn = pro2_parts(bi_, sub, wkp)
                pp.step(a_fn, b_fn)

            xl = XLoader(S, xt_ring)
            ss2, b_ss2 = P.sb(st, "ss2", [128, 4], F32)
            xn2 = h1[:, 0:2, :].rearrange("p a b -> p (a b)")
            wk_alt = (xn2, b_h1, ss2, b_ss2, xn2, b_h1)
            parts = [pro2_parts(0, sub, wkp if sub % 2 == 0 else wk_alt) for sub in range(mblocks[0][1] // 128)]
            parts[0][0]()
            for i_ in range(len(parts)):
                if i_ + 1 < len(parts):
                    parts[i_ + 1][0]()
                parts[i_][1]()
            for bi, (t0, WB, seg) in enumerate(mblocks):
                hT, b_hT = hTs[bi]
                for hc in range(32):
                    ph, b_ph = psf.next()
                    fm_mm(S, ph, b_ph, w1, wbuf(w1_b, w1ch, hc * 128, hc * 128 + 128), hT, b_hT, hc * 128, 128, WB)
                    rl, b_rl = rl_ring.next()
                    S.op("act", lambda e: e.activation(out=rl[:, 0:WB], in_=ph[:, 0:WB], func=AF.Relu), reads=[b_ph], writes=[b_rl])
                    S.op("dve", lambda e: e.tensor_tensor(out=h1[:, hc, 0:WB], in0=rl[:, 0:WB], in1=rl[:, 0:WB], op=ALU.mult), reads=[b_rl], writes=[b_h1])
                if bi == 0:
                    G2 = build_G(st, C_GPOST2, 40)
                nsub = WB // 128
                for sub in range(nsub):
                    for ps_ in range(sub * 4 // nsub, (sub + 1) * 4 // nsub):
                        pro2(bi + 1, ps_)
                    xl.prefetch(("e", t0 + sub * 128), d["xmid"][t0 + sub * 128:t0 + (sub + 1) * 128, :])
                    pY = [psf.next(), psf.next()]
                    for nb in range(2):
                        for hc in range(32):
                            S.op("pe", lambda e: e.matmul(out=pY[nb][0][:, 0:512], lhsT=h1[:, hc, sub * 128:(sub + 1) * 128], rhs=w2[0][:, hc, nb * 512:(nb + 1) * 512], start=(hc == 0), stop=(hc == 31)),
                                 reads=[b_h1, w2_b[nb][hc // 8]], writes=[pY[nb][1]])
                    tt = t0 + sub * 128
                    epilogue(pY, xl.get(("e", tt), None), d["xout"][tt:tt + 128, :], G2[seg], wke)
                pp.flush()
            S.barrier()
        S.release()
        stW.close()


def fm_cols(v):
    return np.ascontiguousarray(np.asarray(v, np.float32).reshape(-1, 128).T)


def rope_perm():
    d = np.arange(32)
    g, i = d // 16, d % 16
    return g * 16 + (i + 8) % 16


def host_tables(core):
    b, s = core // 2, core % 2
    pos = s * T + np.arange(T)
    inv = (10000.0 ** (-np.arange(8, dtype=np.float32) / 8)).astype(np.float32)
    ang_row = (pos // 64).astype(np.float32)[:, None] * inv
    ang_col = (pos % 64).astype(np.float32)[:, None] * inv
    cos = np.ones((32, TA), np.float32)
    sin = np.zeros((32, TA), np.float32)
    for dd in range(32):
        g, i = dd // 16, dd % 16
        ang = (ang_row if g == 0 else ang_col)[:, i % 8]
        cos[dd, :T] = np.cos(ang)
        sin[dd, :T] = np.sin(ang) * (-1.0 if i < 8 else 1.0)
    mb = np.zeros((128, len(NA_IDX)), np.float32)
    for (j, dk, qr), idx in NA_IDX.items():
        r = 32 * s + 2 * j + qr
        w0 = min(max(r - 4, 0), 56)
        for kr in range(2):
            rk = 32 * s + 2 * (j + dk) + kr
            ok = (0 <= rk < 64) and (w0 <= rk < w0 + 8)
            mb[kr * 64:(kr + 1) * 64, idx] = 0.0 if ok else NEG
    return np.stack([cos, sin]), mb


def rb_table(rpb):
    col = np.arange(64)
    cs = np.clip(col - 8, 0, 48)
    in_win = (col[None, :] >= cs[:, None]) & (col[None, :] < cs[:, None] + 16)
    off_c = np.clip(col[None, :] - col[:, None] + 15, 0, 30)
    out = np.full((128, 4, 16, 64), NEG, np.float32)
    for kr in range(2):
        for o2 in range(16):
            off = 15 - o2 + kr
            if off > 14:
                continue
            blk = rpb[:, off][:, off_c]
            blk = np.where(in_win[None], blk, np.float32(NEG))
            out[kr * 64:(kr + 1) * 64, :, o2, :] = np.transpose(blk, (2, 0, 1))
    return out.reshape(128, -1)


LAYER_W = [("wmod", [D, 6 * D]), ("winx", [D, 2560]), ("wuq", [256, 384]), ("wuqs", [256, 384]), ("wukvk", [128, 256]),
           ("wukvv", [128, 256]), ("colpack", [128, NCOL]), ("wout", [D, D]), ("wpw", [256, 256]), ("wmlp1", [D, 4 * D]),
           ("wmlp2", [4 * D, D]), ("rbt", [128, 4 * 16 * 64])]
SCRATCH = [("zsc", [256, TA], F32), ("bsc", [256, TA], F32), ("zcf", [256, TA], F32), ("nq", [256, TA], BF16), ("nk", [256, TA], BF16),
           ("nv", [TA, 272], BF16), ("mq", [384, TA], BF16), ("mk", [384, TA], BF16), ("mva", [1024, 512], BF16), ("mvb", [1024, 512], BF16), ("mvc", [256, 512], BF16),
           ("send_f", [256, 32], F32), ("send_nk", [256, 512], BF16), ("send_nv", [512, 272], BF16),
           ("g_f", [512, 32], F32), ("g_nk", [512, 512], BF16), ("g_nv", [1024, 272], BF16), ("g_mk", [768, TA], BF16),
           ("g_mva", [2048, 512], BF16), ("g_mvb", [2048, 512], BF16), ("xmid", [TA, D], F32)]
PAIRS = [[0, 1], [2, 3], [4, 5], [6, 7]]


def build_fused():
    P = Prog()
    nc = P.nc
    g = {}
    for n, s_, t in [("xin", [TA, D], F32), ("corepack", [128, NCC], F32), ("rope", [2, 32, TA], F32), ("ident", [128, 128], F32)]:
        g[n] = P.dram(n, s_, t, "in")
    xout = P.dram("xout", [T, D], F32, "out")
    x1 = P.dram("x1", [TA, D], F32, "int")
    with contextlib.ExitStack() as st0:
        S = Sched(nc, st0)
        psf, psb = make_psum(P, st0)
        modts = []
        for l in range(2):
            mod_, b_mod_ = P.sb(st0, "mod", [128, 48, 2], F32)
            A1_, b_A1_ = P.sb(st0, "A1", [128, 8, 2], F32)
            modts.append((mod_, b_mod_, A1_, b_A1_))
        kp0, b_kp0 = P.sb(st0, "kp0", [128, NCC], F32)
        S.dma(kp0[:], g["corepack"][:, :], writes=[b_kp0])
        lw = {}
        for l in range(2):
            for n, s_ in LAYER_W:
                lw[(n, l)] = P.dram("%s%d" % (n, l), s_, F32, "in")
        for l in range(2):
            d = dict(g)
            for n, s_ in LAYER_W:
                d[n] = lw[(n, l)]
            for n, s_, t in SCRATCH:
                d[n] = P.dram("%s%d" % (n, l), s_, t, "int")
            d["xin"] = g["xin"] if l == 0 else x1
            d["xout"] = x1 if l == 0 else xout
            with contextlib.ExitStack() as stL:
                K = load_consts(P, S, stL, d)
                def gather(a, b, wait=True):
                    S.collective(d[a], d[b], PAIRS, wait=wait)
                    d["b_" + b] = Buf(b)
                    d["b_" + b].w = ("cc", S.cnt["cc"])

                def small_gathers():
                    gather("send_f", "g_f")
                    gather("send_nk", "g_nk", wait=False)
                    gather("send_nv", "g_nv", wait=False)

                mod, b_mod = emit_A(P, S, stL, d, K, psf, psb, modts[l], need_p0=(l == 0), after_latent=small_gathers)
                d["late_gathers"] = lambda: [gather(a, b, wait=False) for a, b in [("mk", "g_mk"), ("mva", "g_mva"), ("mvb", "g_mvb")]]
                p0n = None
                if l == 0:
                    dn = {"colpack": lw[("colpack", 1)], "wmod": lw[("wmod", 1)]}
                    p0n = lambda st_, pmb: p0_steps(P, S, st_, dn, kp0, b_kp0, pmb[0], pmb[1], *modts[1], gcols=256, nbuf=2)
                emit_B(P, S, stL, d, K, psf, psb, mod, b_mod, last=(l == 1), p0_next=p0n)
                S.barrier()
        print("instructions", S.nins, "waits", S.nwait, "sems", len(S.sem) + len(S.free) + len(S.free_sw))
    return P


_PROG = []


def kernel(x, c, ctx, c_ctx, w_mod, b_mod, g_pre_mix, w_in, w_sc, g_q, w_uq, g_kv, w_ukv, cf_w_dw, cf_b_dw, cf_ln_g, cf_ln_b,
           cf_w_pw, na_rpb, w_out, g_post_mix, g_pre_mlp, w_mlp1, w_mlp2, g_post_mlp):
    f = lambda a: np.ascontiguousarray(np.asarray(a, np.float32))
    x, c, ctx, c_ctx = f(x), f(c), f(ctx), f(c_ctx)
    ident = np.eye(128, dtype=np.float32)
    perm = rope_perm()
    tabs = [host_tables(k) for k in range(8)]
    shared = {}
    for l in range(2):
        wi = f(w_in[l])
        winx = np.concatenate([wi, wi[:, 1088:1152], wi[:, 1152:1184][:, perm]], axis=1)
        wuq = f(w_uq[l])
        wuqs = wuq.copy().reshape(256, 4, 96)
        wuqs[:, :, 64:] = wuqs[:, :, 64:][:, :, perm]
        wuqs = np.ascontiguousarray(wuqs.reshape(256, 384))
        wkv4 = f(w_ukv[l]).reshape(128, 4, 128)
        cpk = np.zeros((128, NCOL), np.float32)
        cpk[:, C_GPRE1:C_GPRE1 + 8] = fm_cols(g_pre_mix[l])
        cpk[:, C_GPOST1:C_GPOST1 + 8] = fm_cols(g_post_mix[l])
        cpk[:, C_GPRE2:C_GPRE2 + 8] = fm_cols(g_pre_mlp[l])
        cpk[:, C_GPOST2:C_GPOST2 + 8] = fm_cols(g_post_mlp[l])
        cpk[:, C_BMOD:C_BMOD + 48] = fm_cols(b_mod[l])
        cpk[:, C_GQ:C_GQ + 2] = fm_cols(g_q[l])
        cpk[:, C_GKV:C_GKV + 1] = fm_cols(g_kv[l])
        wsc = f(w_sc[l])
        wdw = f(cf_w_dw[l])
        for cc in range(2):
            cpk[:, C_WSC + cc * 3:C_WSC + cc * 3 + 3] = wsc[:, cc * 128:(cc + 1) * 128].T
            cpk[:, C_WDW + cc * 31:C_WDW + cc * 31 + 31] = wdw[:, cc * 128:(cc + 1) * 128].T
        cpk[:, C_BDW:C_BDW + 2] = fm_cols(cf_b_dw[l])
        cpk[:, C_LNG:C_LNG + 2] = fm_cols(cf_ln_g[l])
        cpk[:, C_LNB:C_LNB + 2] = fm_cols(cf_ln_b[l])
        cpk[:, C_EPS] = EPS
        lw = dict(wmod=f(w_mod[l]), winx=winx, wuq=wuq, wuqs=wuqs,
                  wukvk=np.ascontiguousarray(wkv4[:, :, :64].reshape(128, 256)),
                  wukvv=np.ascontiguousarray(wkv4[:, :, 64:].reshape(128, 256)),
                  colpack=cpk, wout=f(w_out[l]), wpw=f(cf_w_pw[l]), wmlp1=f(w_mlp1[l]), wmlp2=f(w_mlp2[l]),
                  rbt=rb_table(f(na_rpb[l])))
        for n, v in lw.items():
            shared["%s%d" % (n, l)] = v
    in_maps = []
    for k in range(8):
        kp = np.zeros((128, NCC), np.float32)
        kp[:, K_C:K_C + 8] = fm_cols(c[k // 2])
        kp[:, K_CC:K_CC + 8] = fm_cols(c_ctx)
        kp[:, K_ML] = float(k % 2 == 1)
        kp[:, K_MR] = float(k % 2 == 0)
        kp[:, K_MB:] = tabs[k][1]
        m = dict(shared)
        m.update(xin=np.concatenate([x[k // 2, (k % 2) * T:(k % 2 + 1) * T], ctx[k // 2]], axis=0), corepack=kp, rope=tabs[k][0], ident=ident)
        in_maps.append(m)
    if not _PROG:
        _PROG.append(build_fused())
    res = run_bass_kernel_spmd(_PROG[0].nc, in_maps, core_ids=list(range(8)))
    out = np.zeros((4, 2 * T, D), np.float32)
    for k in range(8):
        out[k // 2, (k % 2) * T:(k % 2 + 1) * T] = np.asarray(res.results[k]["xout"], np.float32)
    return out
```

```python
import bisect
import contextlib
import numpy as np
import ml_dtypes
import concourse.bass as bass
import concourse.mybir as mybir
from concourse.bass_utils import run_bass_kernel_spmd

F32 = mybir.dt.float32
BF16 = mybir.dt.bfloat16
AF = mybir.ActivationFunctionType
ALU = mybir.AluOpType

D = 1024
T = 2048
TC = 256
TA = T + TC
NEG = -30000.0
MLA_SCALE = 96.0 ** -0.5
EPS = 1e-6
BLOCKS = [(0, 512, 0), (512, 512, 0), (1024, 512, 0), (1536, 512, 0), (2048, 256, 1)]

C_GPRE1, C_GPOST1, C_GPRE2, C_GPOST2, C_BMOD = 0, 8, 16, 24, 32
C_GQ, C_GKV, C_WSC, C_WDW, C_BDW, C_LNG, C_LNB, C_EPS = 80, 82, 83, 89, 151, 153, 155, 157
NCOL = 160
K_C, K_CC, K_ML, K_MR, K_MB = 0, 8, 16, 17, 18


def na_slots(j):
    if j == 0:
        return list(range(-2, 4))
    if j == 15:
        return list(range(-3, 3))
    return list(range(-2, 3))


NA_IDX = {}
for _j in range(16):
    for _d in na_slots(_j):
        for _q in range(2):
            NA_IDX[(_j, _d, _q)] = len(NA_IDX)
NCC = K_MB + len(NA_IDX)


class Buf:
    __slots__ = ("name", "w", "r")
    _n = 0

    def __init__(self, name="b"):
        Buf._n += 1
        self.name = "%s_%d" % (name, Buf._n)
        self.w = None
        self.r = []


class Sched:
    def __init__(self, nc, stack):
        self.nc = nc
        self.stack = stack
        self.h = {"pe": nc.tensor, "act": nc.scalar, "dve": nc.vector, "pool": nc.gpsimd, "sp": nc.sync}
        self.sem, self.cnt, self.seen, self.hist = {}, {}, {}, {}
        for e in self.h:
            self.sem[e] = stack.enter_context(nc.semaphore("s_" + e))
            self.cnt[e] = 0
            self.seen[e] = {}
            self.hist[e] = ([0], [{}])
        self.dclock = {}
        self.free = []
        self.free_sw = []
        self.sw_keys = set()
        self.gen = {}
        self._mark = set()
        self.nwait = 0
        self.nins = 0
        self.uid = 0

    def _clock_of(self, ev):
        k, v = ev
        if k in self.h:
            cs, ds = self.hist[k]
            return ds[bisect.bisect_right(cs, v) - 1]
        return self.dclock.get(ev, {})

    def _wait(self, en, deps):
        seen = self.seen[en]
        changed = False
        for k, v in deps.items():
            if k == en and en == "pe":
                continue
            if k not in self.sem:
                continue
            if seen.get(k, 0) >= v:
                continue
            self.h[en].wait_ge(self.sem[k], v)
            self.nwait += 1
            changed = True
            seen[k] = v
            for k2, v2 in self._clock_of((k, v)).items():
                if seen.get(k2, 0) < v2:
                    seen[k2] = v2
        if changed:
            cs, ds = self.hist[en]
            cs.append(self.cnt[en] + 1)
            ds.append(dict(seen))

    def _sync(self, en, reads, writes, skip_key=None):
        deps = {}

        def add(ev):
            if ev is not None and ev[0] != skip_key and deps.get(ev[0], 0) < ev[1]:
                deps[ev[0]] = ev[1]

        for b in reads:
            add(b.w)
        for b in writes:
            add(b.w)
            for r in b.r:
                add(r)
        self._wait(en, deps)

    def _record(self, ev, reads, writes):
        for b in reads:
            b.r.append(ev)
        for b in writes:
            b.w = ev
            b.r = []

    def op(self, en, fn, reads=(), writes=()):
        self._sync(en, reads, writes)
        ins = fn(self.h[en])
        self.cnt[en] += 1
        ins.then_inc(self.sem[en], 1)
        self._record((en, self.cnt[en]), reads, writes)
        self.nins += 1

    def dma(self, out, in_, reads=(), writes=(), qn="sp", slow=False):
        kb = ("ld_" + writes[0].name) if writes else ("st_" + reads[0].name)
        key = "%s#%d" % (kb, self.gen.get(kb, 0))
        self._sync(qn, reads, writes)
        if key not in self.sem:
            fl = self.free_sw if qn == "pool" else self.free
            if qn == "pool":
                self.sw_keys.add(key)
            if fl:
                self.sem[key], self.cnt[key] = fl.pop()
            else:
                self.uid += 1
                self.sem[key] = self.stack.enter_context(self.nc.semaphore("d%d" % self.uid))
                self.cnt[key] = 0
        self.cnt[key] += 16
        if slow:
            self.h[qn].dma_start(out=out, in_=in_, allow_slow_non_contiguous=True).then_inc(self.sem[key], 16)
        else:
            self.h[qn].dma_start(out=out, in_=in_).then_inc(self.sem[key], 16)
        ev = (key, self.cnt[key])
        self.dclock[ev] = dict(self.seen[qn])
        self._record(ev, reads, writes)
        self.nins += 1

    def collective(self, ins_ap, outs_ap, groups, wait=True):
        if wait:
            self._wait("pool", {k: v for k, v in self.cnt.items() if v > 0 and k != "pool"})
        if "cc" not in self.sem:
            self.sem["cc"] = self.stack.enter_context(self.nc.semaphore("cc"))
            self.cnt["cc"] = 0
        self.cnt["cc"] += 1
        self.nc.gpsimd.collective_compute("AllGather", ALU.bypass, replica_groups=groups, ins=[ins_ap.opt()], outs=[outs_ap.opt()]).then_inc(self.sem["cc"], 1)
        self.dclock[("cc", self.cnt["cc"])] = dict(self.seen["pool"])
        self.nins += 1

    def mark(self):
        self._mark = set(self.sem.keys())

    def release(self):
        for k in list(self.sem.keys()):
            if k not in self._mark and k != "cc":
                (self.free_sw if k in self.sw_keys else self.free).append((self.sem.pop(k), self.cnt.pop(k)))
                self.sw_keys.discard(k)
                kb = k.rsplit("#", 1)[0]
                self.gen[kb] = self.gen.get(kb, 0) + 1
                for e in self.h:
                    self.seen[e].pop(k, None)

    def barrier(self, engines=("pe", "act", "dve", "pool", "sp")):
        for e in engines:
            deps = {k: v for k, v in self.cnt.items() if v > 0 and k != e}
            self._wait(e, deps)


class Prog:
    def __init__(self):
        self.nc = bass.Bass("TRN2", target_bir_lowering=False)
        self.n = 0
        self.io = {}

    def dram(self, name, shape, dt, kind):
        self.io[name] = (tuple(shape), dt, kind)
        if kind == "int":
            return self.nc.dram_tensor(name, list(shape), dt).ap()
        k = {"in": "ExternalInput", "out": "ExternalOutput"}[kind]
        return self.nc.dram_tensor(name, list(shape), dt, kind=k).ap()

    def sb(self, st, name, shape, dt):
        self.n += 1
        t = st.enter_context(self.nc.sbuf_tensor("%s_%d" % (name, self.n), list(shape), dt))
        return t, Buf(name)

    def ps(self, st, name, shape, dt):
        self.n += 1
        t = st.enter_context(self.nc.psum_tensor("%s_%d" % (name, self.n), list(shape), dt))
        return t, Buf(name)


class Ring:
    def __init__(self, items):
        self.items = items
        self.i = 0

    def next(self):
        it = self.items[self.i % len(self.items)]
        self.i += 1
        return it


def make_psum(P, st):
    f = Ring([P.ps(st, "pf", [128, 512], F32) for _ in range(7)])
    b = Ring([P.ps(st, "pb", [128, 1024], BF16) for _ in range(1)])
    return f, b


def load_cast(P, S, st, name, dram_ap_fn, nk, ncols, stg_ring=None, chunk=512, pre=None, ks=8, per_k=False):
    wb = pre if pre is not None else P.sb(st, name, [128, nk, ncols], BF16)[0]
    bufs = []
    for c0 in range(0, ncols, chunk):
        cw = min(chunk, ncols - c0)
        bb = Buf(name + "c")
        bl = []
        for k0 in range(0, nk, ks):
            kn = min(ks, nk - k0)
            if per_k:
                bb = Buf(name + "ck")
                bl.append(bb)
            S.dma(wb[:, k0:k0 + kn, c0:c0 + cw], dram_ap_fn(k0, kn, c0, cw), writes=[bb], qn="pool")
        bufs.append(bl if per_k else bb)
    return wb, bufs, chunk


def wsrc(ap):
    return lambda k0, kn, c0, cw: ap[k0 * 128:(k0 + kn) * 128, c0:c0 + cw].rearrange("(k p) n -> p k n", p=128)


def wbuf(bufs, chunk, c0, c1):
    return bufs[c0 // chunk:(c1 - 1) // chunk + 1]


class XLoader:
    def __init__(self, S, ring):
        self.S, self.ring, self.pending = S, ring, {}

    def prefetch(self, key, src):
        if key is None or key in self.pending:
            return
        xt, b = self.ring.next()
        self.S.dma(xt[:], src, writes=[b], qn="act")
        self.pending[key] = (xt, b)

    def get(self, key, src):
        self.prefetch(key, src)
        return self.pending.pop(key)


def prologue_a(P, S, x_src, xl, wk, cp, b_cp, key=None, nxt=None):
    xt, b_xt = xl.get(key, x_src)
    junk, b_junk, ss, b_ss, xn, b_xn = wk
    S.op("act", lambda e: e.activation(out=junk[:], in_=xt[:], func=AF.Square, accum_out=ss[:, 0:1]), reads=[b_xt], writes=[b_junk, b_ss])
    S.op("act", lambda e: e.activation(out=ss[:, 1:2], in_=ss[:, 0:1], func=AF.Sqrt, scale=1.0 / D, bias=cp[:, C_EPS:C_EPS + 1]), reads=[b_ss, b_cp], writes=[b_ss])
    S.op("dve", lambda e: e.reciprocal(out=ss[:, 2:3], in_=ss[:, 1:2]), reads=[b_ss], writes=[b_ss])
    S.op("dve", lambda e: e.tensor_scalar(out=xn[:], in0=xt[:], scalar1=ss[:, 2:3], scalar2=None, op0=ALU.mult), reads=[b_xt, b_ss], writes=[b_xn])
    if nxt is not None:
        xl.prefetch(nxt[0], nxt[1])


def prologue_b(P, S, wk, ident_b, b_ident, psb_ring, hT, b_hT, col_off, A_col, Sh_col, b_mod):
    junk, b_junk, ss, b_ss, xn, b_xn = wk
    pt, b_pt = psb_ring.next()
    for k in range(8):
        S.op("pe", lambda e: e.transpose(out=pt[:, k * 128:(k + 1) * 128], in_=xn[:, k * 128:(k + 1) * 128], identity=ident_b[:]), reads=[b_xn, b_ident], writes=[b_pt])
    for k in range(8):
        if k % 2 == 0:
            S.op("act", lambda e: e.activation(out=hT[:, k, col_off:col_off + 128], in_=pt[:, k * 128:(k + 1) * 128], func=AF.Identity,
                                               scale=A_col(k), bias=Sh_col(k)), reads=[b_pt, b_mod], writes=[b_hT])
        else:
            S.op("dve", lambda e: e.tensor_scalar(out=hT[:, k, col_off:col_off + 128], in0=pt[:, k * 128:(k + 1) * 128], scalar1=A_col(k), scalar2=Sh_col(k),
                                                  op0=ALU.mult, op1=ALU.add), reads=[b_pt, b_mod], writes=[b_hT])


class ProPipe:
    def __init__(self):
        self.pending = None

    def step(self, a_fn, b_fn):
        if self.pending is not None:
            self.pending()
        self.pending = None
        if a_fn is not None:
            a_fn()
            self.pending = b_fn

    def flush(self):
        self.step(None, None)


def fm_mm(S, ps, b_ps, w, wb_list, hT, b_hT, col0, ncols, W, nk=8):
    for k in range(nk):
        S.op("pe", lambda e: e.matmul(out=ps[0:ncols, 0:W], lhsT=w[:, k, col0:col0 + ncols], rhs=hT[:, k, 0:W], start=(k == 0), stop=(k == nk - 1)),
             reads=list(wb_list) + [b_hT], writes=[b_ps])


def load_consts(P, S, st, d):
    cp, b_cp = P.sb(st, "cp", [128, NCOL], F32)
    S.dma(cp[:], d["colpack"][:, :], writes=[b_cp])
    kp, b_kp = P.sb(st, "kp", [128, NCC], F32)
    S.dma(kp[:], d["corepack"][:, :], writes=[b_kp])
    idf, b_idf = P.sb(st, "idf", [128, 128], F32)
    S.dma(idf[:], d["ident"][:, :], writes=[b_idf])
    idb, b_idb = P.sb(st, "idb", [128, 128], BF16)
    S.op("dve", lambda e: e.tensor_copy(out=idb[:], in_=idf[:]), reads=[b_idf], writes=[b_idb])
    onef, b_onef = P.sb(st, "onef", [128, 128], F32)
    S.op("dve", lambda e: e.memset(onef[:], 1.0), writes=[b_onef])
    return dict(cp=cp, b_cp=b_cp, kp=kp, b_kp=b_kp, idf=idf, b_idf=b_idf, idb=idb, b_idb=b_idb, onef=onef, b_onef=b_onef)


def p0_steps(P, S, st, d, kp, b_kp, pm, b_pm, mod, b_mod, A1, b_A1, split=False, gcols=512, nbuf=3):
    sT, b_sT = P.sb(st, "sT", [128, 8, 2], BF16)
    cpn, b_cpn = P.sb(st, "cpn", [128, NCOL], F32)
    stg = Ring([P.sb(st, "wmstg", [128, 8, gcols], BF16) for _ in range(nbuf)])
    tiles = {}
    ng = 6 * D // gcols

    def load(g):
        if g < ng and g not in tiles:
            tiles[g] = stg.next()
            S.dma(tiles[g][0][:], d["wmod"][:, g * gcols:(g + 1) * gcols].rearrange("(k p) n -> p k n", p=128), writes=[tiles[g][1]], qn="pool")

    def group(g):
        if g == 0:
            S.dma(cpn[:], d["colpack"][:, :], writes=[b_cpn])
            S.op("act", lambda e: e.activation(out=sT[:, :, 0], in_=kp[:, K_C:K_C + 8], func=AF.Silu), reads=[b_kp], writes=[b_sT])
            S.op("act", lambda e: e.activation(out=sT[:, :, 1], in_=kp[:, K_CC:K_CC + 8], func=AF.Silu), reads=[b_kp], writes=[b_sT])
            for g_ in range(nbuf - 1):
                load(g_)
        load(g)
        wt, b_wt = tiles[g]
        for cc in range(gcols // 128):
            ch = g * (gcols // 128) + cc
            for k in range(8):
                S.op("pe", lambda e: e.matmul(out=pm[:, ch * 2:ch * 2 + 2], lhsT=wt[:, k, cc * 128:(cc + 1) * 128], rhs=sT[:, k, :],
                                              start=(k == 0), stop=(k == 7)), reads=[b_wt, b_sT], writes=[b_pm])
        load(g + nbuf - 1)

    def fin(c0=0, c1=48, a1=True):
        for j in range(2):
            S.op("dve", lambda e: e.tensor_tensor(out=mod[:, c0:c1, j], in0=pm[:, 0:96].rearrange("p (c j) -> p c j", j=2)[:, c0:c1, j],
                                                  in1=cpn[:, C_BMOD + c0:C_BMOD + c1], op=ALU.add), reads=[b_pm, b_cpn], writes=[b_mod])
        if a1:
            for j in range(2):
                S.op("dve", lambda e: e.scalar_tensor_tensor(out=A1[:, :, j], in0=mod[:, 8:16, j], scalar=1.0, in1=cpn[:, C_GPRE1:C_GPRE1 + 8],
                                                             op0=ALU.add, op1=ALU.mult), reads=[b_mod, b_cpn], writes=[b_A1])

    groups = [(lambda g=g: group(g)) for g in range(ng)]
    if split:
        n1 = 2048 // gcols
        return groups[:n1] + [lambda: fin(0, 16, True)], groups[n1:] + [lambda: fin(16, 48, False)]
    return groups + [fin]


def emit_A(P, S, st0, d, K, psf, psb, modt, need_p0, after_latent=None):
    if True:
        cp, b_cp, kp, b_kp = K["cp"], K["b_cp"], K["kp"], K["b_kp"]
        mod, b_mod, A1, b_A1 = modt
        S.mark()
        with contextlib.ExitStack() as st:
            late_p0 = []
            if need_p0:
                early, late_p0 = p0_steps(P, S, st, d, kp, b_kp, psf.items[0][0], psf.items[0][1], mod, b_mod, A1, b_A1, split=True)
                for step in early:
                    step()
                psf = Ring(psf.items[1:])
            stg = None
            w, wbs, wch = load_cast(P, S, st, "winx", wsrc(d["winx"]), 8, 2560, stg)
            wuq, wuq_b, _ = load_cast(P, S, st, "wuq", wsrc(d["wuq"]), 2, 384, stg)
            wuqs, wuqs_b, _ = load_cast(P, S, st, "wuqs", wsrc(d["wuqs"]), 2, 384, stg)
            wkk, wkk_b, _ = load_cast(P, S, st, "wukvk", wsrc(d["wukvk"]), 1, 256, stg)
            wkv, wkv_b, _ = load_cast(P, S, st, "wukvv", wsrc(d["wukvv"]), 1, 256, stg)
            cosT, b_cos = P.sb(st, "cos", [96, TA], F32)
            sinT, b_sin = P.sb(st, "sin", [96, TA], F32)
            S.dma(cosT[64:96, :], d["rope"][0, :, :], writes=[b_cos])
            S.dma(sinT[64:96, :], d["rope"][1, :, :], writes=[b_sin])
            xt_ring = Ring([P.sb(st, "xt", [128, D], F32) for _ in range(2)])
            junk, b_junk = P.sb(st, "junk", [128, D], F32)
            ss, b_ss = P.sb(st, "ss", [128, 4], F32)
            xn, b_xn = P.sb(st, "xn", [128, D], BF16)
            wk = (junk, b_junk, ss, b_ss, xn, b_xn)
            hT_ring = Ring([P.sb(st, "hT", [128, 8, 512], BF16) for _ in range(2)])
            f32r = Ring([P.sb(st, "of", [128, 512], F32) for _ in range(6)])
            bfr = Ring([P.sb(st, "ob", [128, 512], BF16) for _ in range(6)])
            nvr = Ring([P.sb(st, "onv", [128, 272], BF16) for _ in range(3)])
            mvr = Ring([P.sb(st, "omv", [128, 4, 128], BF16) for _ in range(3)])
            for t_, b_ in nvr.items:
                S.op("pool", lambda e: e.memset(t_[:, 260:272], 0.0), writes=[b_])
                S.op("pool", lambda e: e.memset(t_[:, 0:260].rearrange("p (h e) -> p h e", h=4)[:, :, 64:65], 1.0), writes=[b_])
            for t_, b_ in mvr.items:
                S.op("pool", lambda e: e.memset(t_[:, :, 64:128], 1.0), writes=[b_])
            cqf = [P.sb(st, "cqf", [128, 512], F32) for _ in range(2)]
            sqf = [P.sb(st, "sqf", [128, 512], F32) for _ in range(2)]
            cqn = [P.sb(st, "cqn", [128, 512], BF16) for _ in range(2)]
            rs, b_rs = P.sb(st, "rs", [128, 512], F32)
            ropt_ring = Ring([[P.sb(st, "ropt", [96, 512], F32) for _ in range(2)] for _ in range(2)])
            kT = [P.sb(st, "kT", [96, 512], BF16) for _ in range(4)]
            qTr = Ring([P.sb(st, "qT", [96, 512], BF16) for _ in range(2)])

            ckf = [P.sb(st, "ckf", [128, 512], F32)]
            skf = [P.sb(st, "skf", [128, 512], F32)]
            ckn = [P.sb(st, "ckn", [128, 512], BF16)]
            rs2 = P.sb(st, "rs2", [128, 512], F32)

            def rms_fm(srcs, nfeat, gcol0, outs, W, cqf=cqf, sqf=sqf, rsb=(rs, b_rs)):
                rs, b_rs = rsb
                n = len(srcs)
                for c, (ps_, b_ps_) in enumerate(srcs):
                    S.op("act", lambda e: e.activation(out=cqf[c][0][:, 0:W], in_=ps_[:, 0:W], func=AF.Copy), reads=[b_ps_], writes=[cqf[c][1]])
                    S.op("act", lambda e: e.activation(out=sqf[c][0][:, 0:W], in_=ps_[:, 0:W], func=AF.Square), reads=[b_ps_], writes=[sqf[c][1]])
                pS, b_pS = psf.next()
                for c in range(n):
                    S.op("pe", lambda e: e.matmul(out=pS[:, 0:W], lhsT=K["onef"][:, :], rhs=sqf[c][0][:, 0:W], start=(c == 0), stop=(c == n - 1)),
                         reads=[K["b_onef"], sqf[c][1]], writes=[b_pS])
                S.op("act", lambda e: e.activation(out=rs[:, 0:W], in_=pS[:, 0:W], func=AF.Sqrt, scale=1.0 / nfeat, bias=cp[:, C_EPS:C_EPS + 1]), reads=[b_pS, b_cp], writes=[b_rs])
                S.op("dve", lambda e: e.reciprocal(out=rs[:, 0:W], in_=rs[:, 0:W]), reads=[b_rs], writes=[b_rs])
                for c in range(n):
                    S.op("dve", lambda e: e.scalar_tensor_tensor(out=outs[c][0][:, 0:W], in0=cqf[c][0][:, 0:W], scalar=cp[:, gcol0 + c:gcol0 + c + 1], in1=rs[:, 0:W],
                                                                 op0=ALU.mult, op1=ALU.mult), reads=[cqf[c][1], b_cp, b_rs], writes=[outs[c][1]])

            def rope96(pa, b_pa, pb, b_pb, t0, W, out_tile, out_buf):
                (r0, b_r0), (r1, b_r1) = ropt_ring.next()
                S.op("dve", lambda e: e.tensor_tensor(out=r0[64:96, 0:W], in0=pa[64:96, 0:W], in1=cosT[64:96, t0:t0 + W], op=ALU.mult), reads=[b_pa, b_cos], writes=[b_r0])
                S.op("dve", lambda e: e.tensor_tensor(out=r1[64:96, 0:W], in0=pb[64:96, 0:W], in1=sinT[64:96, t0:t0 + W], op=ALU.mult), reads=[b_pb, b_sin], writes=[b_r1])
                S.op("pool", lambda e: e.tensor_tensor(out=out_tile[64:96, 0:W], in0=r0[64:96, 0:W], in1=r1[64:96, 0:W], op=ALU.add), reads=[b_r0, b_r1], writes=[out_buf])

            hTs = {}
            xl = XLoader(S, xt_ring)
            pseq = [(bi_, sub_) for bi_, (t0_, W_, seg_) in enumerate(BLOCKS) for sub_ in range(W_ // 128)]

            pp = ProPipe()

            def pro(bi, sub):
                if bi >= len(BLOCKS):
                    pp.flush()
                    return
                t0_, W_, seg_ = BLOCKS[bi]
                if sub >= W_ // 128:
                    pp.flush()
                    return
                if bi not in hTs:
                    hTs[bi] = hT_ring.next()
                hT_, b_hT_ = hTs[bi]
                i_ = pseq.index((bi, sub))
                nxt = None
                if i_ + 1 < len(pseq):
                    nb_, ns_ = pseq[i_ + 1]
                    nt0 = BLOCKS[nb_][0] + ns_ * 128
                    nxt = ((nb_, ns_), d["xin"][nt0:nt0 + 128, :])
                pp.step(lambda: prologue_a(P, S, d["xin"][t0_ + sub * 128:t0_ + (sub + 1) * 128, :], xl, wk, cp, b_cp, key=(bi, sub), nxt=nxt),
                        lambda: prologue_b(P, S, wk, K["idb"], K["b_idb"], psb, hT_, b_hT_, sub * 128,
                                           lambda k: A1[:, k, seg_:seg_ + 1], lambda k: mod[:, k, seg_:seg_ + 1], b_A1))

            for sub in range(4):
                pro(0, sub)
            pp.flush()
            for bi, (t0, W, seg) in enumerate(BLOCKS):
                hT, b_hT = hTs[bi]
                srcs = []
                for c in range(2):
                    pc_, b_pc_ = psf.next()
                    fm_mm(S, pc_, b_pc_, w, wbuf(wbs, wch, 768 + c * 128, 768 + c * 128 + 128), hT, b_hT, 768 + c * 128, 128, W)
                    srcs.append((pc_, b_pc_))
                rms_fm(srcs, 256, C_GQ, cqn, W)
                pc_, b_pc_ = psf.next()
                fm_mm(S, pc_, b_pc_, w, wbuf(wbs, wch, 1024, 1152), hT, b_hT, 1024, 128, W)
                rms_fm([(pc_, b_pc_)], 128, C_GKV, ckn, W, cqf=ckf, sqf=skf, rsb=rs2)
                ckvn, b_ckvn = ckn[0]
                pa, b_pa = psf.next()
                fm_mm(S, pa, b_pa, w, wbuf(wbs, wch, 1088, 1184), hT, b_hT, 1088, 96, W)
                pb, b_pb = psf.next()
                fm_mm(S, pb, b_pb, w, wbuf(wbs, wch, 2464, 2560), hT, b_hT, 2464, 96, W)
                rope96(pa, b_pa, pb, b_pb, t0, W, kT[0][0], kT[0][1])
                for h in range(1, 4):
                    S.op("pool", lambda e: e.tensor_copy(out=kT[h][0][64:96, 0:W], in_=kT[0][0][64:96, 0:W]), reads=[kT[0][1]], writes=[kT[h][1]])
                def sec_sc(c):
                    pX, b_pX = psf.next()
                    fm_mm(S, pX, b_pX, w, wbuf(wbs, wch, c * 128, c * 128 + 128), hT, b_hT, c * 128, 128, W)
                    xf, b_xf = f32r.next()
                    S.op("act", lambda e: e.activation(out=xf[:, 0:W], in_=pX[:, 0:W], func=AF.Copy), reads=[b_pX], writes=[b_xf])
                    pC, b_pC = psf.next()
                    fm_mm(S, pC, b_pC, w, wbuf(wbs, wch, 512 + c * 128, 512 + c * 128 + 128), hT, b_hT, 512 + c * 128, 128, W)
                    zt, b_zt = f32r.next()
                    S.op("dve", lambda e: e.tensor_tensor(out=zt[:, 0:W], in0=pC[:, 0:W], in1=xf[:, 0:W], op=ALU.mult), reads=[b_pC, b_xf], writes=[b_zt])
                    S.dma(d["zsc"][c * 128:(c + 1) * 128, t0:t0 + W], zt[:, 0:W], reads=[b_zt])
                    if t0 == 0:
                        S.dma(d["send_f"][c * 128:(c + 1) * 128, 0:1], zt[:, 0:1], reads=[b_zt], slow=True)
                    if t0 + W == T:
                        S.dma(d["send_f"][c * 128:(c + 1) * 128, 1:2], zt[:, W - 1:W], reads=[b_zt], slow=True)
                    pB, b_pB = psf.next()
                    fm_mm(S, pB, b_pB, w, wbuf(wbs, wch, 256 + c * 128, 256 + c * 128 + 128), hT, b_hT, 256 + c * 128, 128, W)
                    bt, b_bt = f32r.next()
                    S.op("act", lambda e: e.activation(out=bt[:, 0:W], in_=pB[:, 0:W], func=AF.Copy), reads=[b_pB], writes=[b_bt])
                    S.dma(d["bsc"][c * 128:(c + 1) * 128, t0:t0 + W], bt[:, 0:W], reads=[b_bt])
                def sec_cf(c):
                    pG, b_pG = psf.next()
                    fm_mm(S, pG, b_pG, w, wbuf(wbs, wch, 1440 + c * 128, 1440 + c * 128 + 128), hT, b_hT, 1440 + c * 128, 128, W)
                    sg, b_sg = f32r.next()
                    S.op("act", lambda e: e.activation(out=sg[:, 0:W], in_=pG[:, 0:W], func=AF.Sigmoid), reads=[b_pG], writes=[b_sg])
                    pA, b_pA = psf.next()
                    fm_mm(S, pA, b_pA, w, wbuf(wbs, wch, 1184 + c * 128, 1184 + c * 128 + 128), hT, b_hT, 1184 + c * 128, 128, W)
                    zt, b_zt = f32r.next()
                    S.op("dve", lambda e: e.tensor_tensor(out=zt[:, 0:W], in0=pA[:, 0:W], in1=sg[:, 0:W], op=ALU.mult), reads=[b_pA, b_sg], writes=[b_zt])
                    S.dma(d["zcf"][c * 128:(c + 1) * 128, t0:t0 + W], zt[:, 0:W], reads=[b_zt])
                    if t0 == 0:
                        S.dma(d["send_f"][c * 128:(c + 1) * 128, 2:17], zt[:, 0:15], reads=[b_zt])
                    if t0 + W == T:
                        S.dma(d["send_f"][c * 128:(c + 1) * 128, 17:32], zt[:, W - 15:W], reads=[b_zt])
                def sec_naqk(c):
                    pQ, b_pQ = psf.next()
                    fm_mm(S, pQ, b_pQ, w, wbuf(wbs, wch, 1696 + c * 128, 1696 + c * 128 + 128), hT, b_hT, 1696 + c * 128, 128, W)
                    ob, b_ob = bfr.next()
                    S.op("act", lambda e: e.activation(out=ob[:, 0:W], in_=pQ[:, 0:W], func=AF.Identity, scale=0.125), reads=[b_pQ], writes=[b_ob])
                    S.dma(d["nq"][c * 128:(c + 1) * 128, t0:t0 + W], ob[:, 0:W], reads=[b_ob])
                    pK, b_pK = psf.next()
                    fm_mm(S, pK, b_pK, w, wbuf(wbs, wch, 1952 + c * 128, 1952 + c * 128 + 128), hT, b_hT, 1952 + c * 128, 128, W)
                    ob, b_ob = bfr.next()
                    S.op("dve", lambda e: e.tensor_copy(out=ob[:, 0:W], in_=pK[:, 0:W]), reads=[b_pK], writes=[b_ob])
                    S.dma(d["nk"][c * 128:(c + 1) * 128, t0:t0 + W], ob[:, 0:W], reads=[b_ob])
                    if t0 == 0:
                        S.dma(d["send_nk"][c * 128:(c + 1) * 128, 0:256], ob[:, 0:256], reads=[b_ob])
                    if t0 + W == T:
                        S.dma(d["send_nk"][c * 128:(c + 1) * 128, 256:512], ob[:, W - 256:W], reads=[b_ob])
                def sec_nav(sub):
                    pV, b_pV = psf.next()
                    for k in range(8):
                        S.op("pe", lambda e: e.matmul(out=pV[:, 0:256], lhsT=hT[:, k, sub * 128:(sub + 1) * 128], rhs=w[:, k, 2208:2464], start=(k == 0), stop=(k == 7)),
                             reads=wbuf(wbs, wch, 2208, 2464) + [b_hT], writes=[b_pV])
                    ot, b_ot = nvr.next()
                    S.op("act", lambda e: e.activation(out=ot[:, 0:260].rearrange("p (h e) -> p h e", h=4)[:, :, 0:64], in_=pV[:, 0:256].rearrange("p (h e) -> p h e", h=4), func=AF.Copy),
                         reads=[b_pV], writes=[b_ot])
                    tt = t0 + sub * 128
                    S.dma(d["nv"][tt:tt + 128, :], ot[:, :], reads=[b_ot])
                    if tt < 256:
                        S.dma(d["send_nv"][tt:tt + 128, :], ot[:, :], reads=[b_ot])
                    if T - 256 <= tt < T:
                        S.dma(d["send_nv"][256 + tt - (T - 256):256 + tt - (T - 256) + 128, :], ot[:, :], reads=[b_ot])
                def sec_head(h):
                    pa, b_pa = psf.next()
                    pb, b_pb = psf.next()
                    for c in range(2):
                        S.op("pe", lambda e: e.matmul(out=pa[0:96, 0:W], lhsT=wuq[:, c, h * 96:(h + 1) * 96], rhs=cqn[c][0][:, 0:W], start=(c == 0), stop=(c == 1)),
                             reads=wuq_b + [cqn[c][1]], writes=[b_pa])
                    for c in range(2):
                        S.op("pe", lambda e: e.matmul(out=pb[0:96, 0:W], lhsT=wuqs[:, c, h * 96:(h + 1) * 96], rhs=cqn[c][0][:, 0:W], start=(c == 0), stop=(c == 1)),
                             reads=wuqs_b + [cqn[c][1]], writes=[b_pb])
                    qT, b_qT = qTr.next()
                    S.op("act", lambda e: e.activation(out=qT[0:64, 0:W], in_=pa[0:64, 0:W], func=AF.Copy), reads=[b_pa], writes=[b_qT])
                    rope96(pa, b_pa, pb, b_pb, t0, W, qT, b_qT)
                    S.dma(d["mq"][h * 96:(h + 1) * 96, t0:t0 + W], qT[:, 0:W], reads=[b_qT])
                def sec_kn(h):
                    pk_, b_pk_ = psf.next()
                    S.op("pe", lambda e: e.matmul(out=pk_[0:64, 0:W], lhsT=wkk[:, 0, h * 64:(h + 1) * 64], rhs=ckvn[:, 0:W], start=True, stop=True),
                         reads=wkk_b + [b_ckvn], writes=[b_pk_])
                    S.op("act", lambda e: e.activation(out=kT[h][0][0:64, 0:W], in_=pk_[0:64, 0:W], func=AF.Copy), reads=[b_pk_], writes=[kT[h][1]])
                    S.dma(d["mk"][h * 96:(h + 1) * 96, t0:t0 + W], kT[h][0][:, 0:W], reads=[kT[h][1]])
                def sec_mv(sub):
                    pV, b_pV = psf.next()
                    S.op("pe", lambda e: e.matmul(out=pV[:, 0:256], lhsT=ckvn[:, sub * 128:(sub + 1) * 128], rhs=wkv[:, 0, :], start=True, stop=True),
                         reads=wkv_b + [b_ckvn], writes=[b_pV])
                    ot, b_ot = mvr.next()
                    S.op("dve", lambda e: e.tensor_copy(out=ot[:, :, 0:64], in_=pV[:, 0:256].rearrange("p (h e) -> p h e", h=4)), reads=[b_pV], writes=[b_ot])
                    tt = t0 + sub * 128
                    mvn, mvo = ("mva", tt) if tt < 1024 else (("mvb", tt - 1024) if tt < T else ("mvc", tt - T))
                    S.dma(d[mvn][mvo:mvo + 128, :], ot[:].rearrange("p h e -> p (h e)"), reads=[b_ot])

                sec_sc(0)
                sec_sc(1)
                pro(bi + 1, 0)
                sec_head(0)
                sec_cf(0)
                sec_head(1)
                sec_cf(1)
                pro(bi + 1, 1)
                sec_head(2)
                sec_naqk(0)
                sec_head(3)
                sec_naqk(1)
                pro(bi + 1, 2)
                for sub in range(W // 128):
                    sec_nav(sub)
                for h in range(4):
                    sec_kn(h)
                pro(bi + 1, 3)
                for sub in range(W // 128):
                    sec_mv(sub)
                pp.flush()
                if after_latent is not None and t0 + W == T:
                    after_latent()
                for _ in range(2):
                    if late_p0:
                        late_p0.pop(0)()
            while late_p0:
                late_p0.pop(0)()
            S.barrier()
        S.release()
    return mod, b_mod


def emit_B(P, S, st0, d, K, psf, psb, mod, b_mod, last, p0_next=None):
    nblk = 4 if last else 5
    ntile = 16 if last else 18
    if True:
        S.mark()
        cp, b_cp, kp, b_kp = K["cp"], K["b_cp"], K["kp"], K["b_kp"]
        idb, b_idb, idf, b_idf, onef, b_onef = K["idb"], K["b_idb"], K["idf"], K["b_idf"], K["onef"], K["b_onef"]
        A2, b_A2 = P.sb(st0, "A2", [128, 8, 2], F32)
        for j in range(2):
            S.op("dve", lambda e: e.scalar_tensor_tensor(out=A2[:, :, j], in0=mod[:, 32:40, j], scalar=1.0, in1=cp[:, C_GPRE2:C_GPRE2 + 8],
                                                         op0=ALU.add, op1=ALU.mult), reads=[b_mod, b_cp], writes=[b_A2])
        gfm, b_gfm = P.sb(st0, "gfm", [128, 8], F32)
        dg, b_dg = P.sb(st0, "dg", [128, 128], F32)

        def build_G(st, gc, mc, ring=None):
            ring = ring or psf
            Gs = [P.sb(st, "G", [128, D], F32) for _ in range(2)]
            for seg in range(2):
                S.op("dve", lambda e: e.tensor_tensor(out=gfm[:, :], in0=mod[:, mc:mc + 8, seg], in1=cp[:, gc:gc + 8], op=ALU.mult), reads=[b_mod, b_cp], writes=[b_gfm])
                for k in range(8):
                    S.op("dve", lambda e: e.tensor_scalar(out=dg[:, :], in0=idf[:, :], scalar1=gfm[:, k:k + 1], scalar2=None, op0=ALU.mult), reads=[b_idf, b_gfm], writes=[b_dg])
                    pg, b_pg = ring.next()
                    S.op("pe", lambda e: e.matmul(out=pg[:, 0:128], lhsT=onef[:, :], rhs=dg[:, :], start=True, stop=True), reads=[b_onef, b_dg], writes=[b_pg])
                    S.op("act", lambda e: e.activation(out=Gs[seg][0][:, k * 128:(k + 1) * 128], in_=pg[:, 0:128], func=AF.Copy), reads=[b_pg], writes=[Gs[seg][1]])
            return Gs

        stW = contextlib.ExitStack()
        w1_t = P.sb(stW, "w1", [128, 8, 4096], BF16)[0]
        stgW = None
        stY = contextlib.ExitStack()
        YT, b_YT = P.sb(stY, "YT", [128, 8, TA], BF16)
        b_YTm = [Buf("ytm") for _ in range(4)]

        def epilogue(pY, xtile, out_dst, Gt, wk):
            junk, b_junk, ss, b_ss, ot_ring = wk
            if isinstance(junk, Ring):
                junk, b_junk = junk.next()
            if isinstance(ss, Ring):
                ss, b_ss = ss.next()
            xt, b_xt = xtile
            for i, (p_, b_p) in enumerate(pY):
                S.op("act", lambda e: e.activation(out=junk[:, 0:512], in_=p_[:, 0:512], func=AF.Square, accum_out=ss[:, i:i + 1]), reads=[b_p], writes=[b_junk, b_ss])
            S.op("dve", lambda e: e.tensor_tensor(out=ss[:, 2:3], in0=ss[:, 0:1], in1=ss[:, 1:2], op=ALU.add), reads=[b_ss], writes=[b_ss])
            S.op("act", lambda e: e.activation(out=ss[:, 3:4], in_=ss[:, 2:3], func=AF.Sqrt, scale=1.0 / D, bias=cp[:, C_EPS:C_EPS + 1]), reads=[b_ss, b_cp], writes=[b_ss])
            S.op("dve", lambda e: e.reciprocal(out=ss[:, 4:5], in_=ss[:, 3:4]), reads=[b_ss], writes=[b_ss])
            ot, b_ot = ot_ring.next()
            for i, (p_, b_p) in enumerate(pY):
                S.op("dve", lambda e: e.scalar_tensor_tensor(out=ot[:, i * 512:(i + 1) * 512], in0=p_[:, 0:512], scalar=ss[:, 4:5], in1=Gt[0][:, i * 512:(i + 1) * 512],
                                                             op0=ALU.mult, op1=ALU.mult), reads=[b_p, b_ss, Gt[1]], writes=[b_ot])
            S.op("pool", lambda e: e.tensor_tensor(out=ot[:, :], in0=ot[:, :], in1=xt[:, :], op=ALU.add), reads=[b_ot, b_xt], writes=[b_ot])
            S.dma(out_dst, ot[:, :], reads=[b_ot])

        def tm_to_YT(ytm_ap_fn, b_ytm, chunk0, col0, b_y):
            pt, b_pt = psb.next()
            for c in range(2):
                S.op("pe", lambda e: e.transpose(out=pt[:, c * 128:(c + 1) * 128], in_=ytm_ap_fn(c), identity=idb[:]), reads=[b_ytm, b_idb], writes=[b_pt])
            S.op("act", lambda e: e.activation(out=YT[:, chunk0:chunk0 + 2, col0:col0 + 128], in_=pt[:, 0:256].rearrange("p (c t) -> p c t", c=2), func=AF.Copy), reads=[b_pt], writes=[b_y])

        stSC = contextlib.ExitStack()
        st = stSC
        if True:
            zs, b_zs = P.sb(st, "zs", [128, 2, T + 2], F32)
            b_zsl = []
            b_Bsl = []
            Bs, b_Bs = P.sb(st, "Bs", [128, 2, T], F32)
            acc, b_acc = P.sb(st, "acc", [128, 2, T], F32)
            for c in range(2):
                b_zsl.append(Buf("zsl"))
                S.dma(zs[:, c, 1:T + 1], d["zsc"][c * 128:(c + 1) * 128, 0:T], writes=[b_zsl[-1]])
                S.dma(zs[:, c, 0:1], d["g_f"][c * 128:(c + 1) * 128, 1:2], reads=[d["b_g_f"]], writes=[b_zs], slow=True)
                S.dma(zs[:, c, T + 1:T + 2], d["g_f"][256 + c * 128:256 + (c + 1) * 128, 0:1], reads=[d["b_g_f"]], writes=[b_zs], slow=True)
                b_Bsl.append(Buf("Bsl"))
                S.dma(Bs[:, c, :], d["bsc"][c * 128:(c + 1) * 128, 0:T], writes=[b_Bsl[-1]])
        if True:
            zc, b_zc = P.sb(st, "zc", [128, 2, T + 30], BF16)
            b_acc_sc = b_acc
            stg = None
            wpw, wpw_b, _ = load_cast(P, S, st, "wpw", wsrc(d["wpw"]), 2, 256, stg)
            for c in range(2):
                S.dma(zc[:, c, 15:T + 15], d["zcf"][c * 128:(c + 1) * 128, 0:T], writes=[b_zc], qn="pool")
                S.dma(zc[:, c, 0:15], d["g_f"][c * 128:(c + 1) * 128, 17:32], reads=[d["b_g_f"]], writes=[b_zc], qn="pool")
                S.dma(zc[:, c, T + 15:T + 30], d["g_f"][256 + c * 128:256 + (c + 1) * 128, 2:17], reads=[d["b_g_f"]], writes=[b_zc], qn="pool")
            S.op("dve", lambda e: e.tensor_scalar(out=zc[:, :, 0:15], in0=zc[:, :, 0:15], scalar1=kp[:, K_ML:K_ML + 1], scalar2=None, op0=ALU.mult), reads=[b_zc, b_kp], writes=[b_zc])
            S.op("dve", lambda e: e.tensor_scalar(out=zc[:, :, T + 15:T + 30], in0=zc[:, :, T + 15:T + 30], scalar1=kp[:, K_MR:K_MR + 1], scalar2=None, op0=ALU.mult), reads=[b_zc, b_kp], writes=[b_zc])
            Dg, b_Dg = P.sb(st, "Dg", [128, 2, 31, 128], BF16)
            b_Dgc = [Buf("Dg0"), Buf("Dg1")]
            for c in range(2):
                for j in range(31):
                    col = C_WDW + c * 31 + j
                    if j % 2 == 0:
                        S.op("dve", lambda e: e.tensor_scalar(out=Dg[:, c, j, :], in0=idf[:, :], scalar1=cp[:, col:col + 1], scalar2=None, op0=ALU.mult),
                             reads=[b_idf, b_cp], writes=[b_Dgc[c]])
                    else:
                        S.op("act", lambda e: e.activation(out=Dg[:, c, j, :], in_=idf[:, :], func=AF.Copy, scale=cp[:, col:col + 1]),
                             reads=[b_idf, b_cp], writes=[b_Dgc[c]])
            S.op("dve", lambda e: e.tensor_scalar(out=zs[:, :, 0:1], in0=zs[:, :, 0:1], scalar1=kp[:, K_ML:K_ML + 1], scalar2=None, op0=ALU.mult), reads=[b_zs, b_kp] + b_zsl, writes=[b_zs])
            S.op("dve", lambda e: e.tensor_scalar(out=zs[:, :, T + 1:T + 2], in0=zs[:, :, T + 1:T + 2], scalar1=kp[:, K_MR:K_MR + 1], scalar2=None, op0=ALU.mult), reads=[b_zs, b_kp], writes=[b_zs])

            def sc_conv(z_t, b_z, B_t, b_B, n, ycol0):
                for c in range(2):
                    wc = lambda j: cp[:, C_WSC + c * 3 + j:C_WSC + c * 3 + j + 1]
                    S.op("dve", lambda e: e.tensor_scalar(out=acc[:, c, 0:n], in0=z_t[:, c, 0:n], scalar1=wc(0), scalar2=None, op0=ALU.mult), reads=[b_z, b_cp], writes=[b_acc])
                    for j in (1, 2):
                        S.op("dve", lambda e: e.scalar_tensor_tensor(out=acc[:, c, 0:n], in0=z_t[:, c, j:j + n], scalar=wc(j), in1=acc[:, c, 0:n], op0=ALU.mult, op1=ALU.add),
                             reads=[b_z, b_cp, b_acc], writes=[b_acc])
                    S.op("dve", lambda e: e.tensor_tensor(out=YT[:, c, ycol0:ycol0 + n], in0=acc[:, c, 0:n], in1=B_t[:, c, 0:n], op=ALU.mult), reads=[b_acc] + list(b_B), writes=[b_YTm[0]])

            sc_conv(zs, b_zs, Bs, b_Bsl, T, 0)
            if not last:
                zsc_, b_zsc_ = P.sb(st, "zsc", [128, 2, TC + 2], F32)
                Bc, b_Bc = P.sb(st, "Bc", [128, 2, TC], F32)
                S.op("pool", lambda e: e.memset(zsc_[:], 0.0), writes=[b_zsc_])
                for c in range(2):
                    S.dma(zsc_[:, c, 1:TC + 1], d["zsc"][c * 128:(c + 1) * 128, T:TA], writes=[b_zsc_])
                    S.dma(Bc[:, c, :], d["bsc"][c * 128:(c + 1) * 128, T:TA], writes=[b_Bc])
                sc_conv(zsc_, b_zsc_, Bc, [b_Bc], TC, T)
            d["late_gathers"]()
            sqf = [P.sb(st, "sqf", [128, 512], F32) for _ in range(2)]
            mean, b_mean = P.sb(st, "mean", [128, 512], F32)
            msq, b_msq = P.sb(st, "msq", [128, 512], F32)
            rstd, b_rstd = P.sb(st, "rstd", [128, 512], F32)
            tn = [P.sb(st, "tn", [128, 512], F32) for _ in range(2)]
            sbf = [P.sb(st, "sbf", [128, 512], BF16) for _ in range(2)]

            def cf_conv(z_t, b_z, n, ycol0):
                blocks = [(b0, min(512, n - b0)) for b0 in range(0, n, 512)]
                b_accs = [Buf("accb") for _ in blocks]

                def conv(i):
                    b0, W = blocks[i]
                    for c in range(2):
                        pc, b_pc = psf.next()
                        for j in range(31):
                            S.op("pe", lambda e: e.matmul(out=pc[:, 0:W], lhsT=Dg[:, c, j, :], rhs=z_t[:, c, b0 + j:b0 + j + W], start=(j == 0), stop=(j == 30)),
                                 reads=[b_Dgc[c], b_z], writes=[b_pc])
                        S.op("act", lambda e: e.activation(out=acc[:, c, b0:b0 + W], in_=pc[:, 0:W], func=AF.Identity, bias=cp[:, C_BDW + c:C_BDW + c + 1]),
                             reads=[b_pc, b_cp], writes=[b_accs[i], b_acc_sc])

                def ln(i):
                    b0, W = blocks[i]
                    b_acc = b_accs[i]
                    for c in range(2):
                        S.op("act", lambda e: e.activation(out=sqf[c][0][:, 0:W], in_=acc[:, c, b0:b0 + W], func=AF.Square), reads=[b_acc], writes=[sqf[c][1]])
                    p1, b_p1 = psf.next()
                    p2, b_p2 = psf.next()
                    for c in range(2):
                        S.op("pe", lambda e: e.matmul(out=p1[:, 0:W], lhsT=onef[:, :], rhs=acc[:, c, b0:b0 + W], start=(c == 0), stop=(c == 1)), reads=[b_onef, b_acc], writes=[b_p1])
                    for c in range(2):
                        S.op("pe", lambda e: e.matmul(out=p2[:, 0:W], lhsT=onef[:, :], rhs=sqf[c][0][:, 0:W], start=(c == 0), stop=(c == 1)), reads=[b_onef, sqf[c][1]], writes=[b_p2])
                    S.op("act", lambda e: e.activation(out=mean[:, 0:W], in_=p1[:, 0:W], func=AF.Copy, scale=1.0 / 256), reads=[b_p1], writes=[b_mean])
                    S.op("dve", lambda e: e.tensor_tensor(out=msq[:, 0:W], in0=mean[:, 0:W], in1=mean[:, 0:W], op=ALU.mult), reads=[b_mean], writes=[b_msq])
                    S.op("dve", lambda e: e.scalar_tensor_tensor(out=rstd[:, 0:W], in0=p2[:, 0:W], scalar=1.0 / 256, in1=msq[:, 0:W], op0=ALU.mult, op1=ALU.subtract),
                         reads=[b_p2, b_msq], writes=[b_rstd])
                    S.op("act", lambda e: e.activation(out=rstd[:, 0:W], in_=rstd[:, 0:W], func=AF.Sqrt, bias=cp[:, C_EPS:C_EPS + 1]), reads=[b_rstd, b_cp], writes=[b_rstd])
                    S.op("dve", lambda e: e.reciprocal(out=rstd[:, 0:W], in_=rstd[:, 0:W]), reads=[b_rstd], writes=[b_rstd])
                    for c in range(2):
                        S.op("dve", lambda e: e.tensor_tensor(out=tn[c][0][:, 0:W], in0=acc[:, c, b0:b0 + W], in1=mean[:, 0:W], op=ALU.subtract), reads=[b_acc, b_mean], writes=[tn[c][1]])
                        S.op("dve", lambda e: e.tensor_tensor(out=tn[c][0][:, 0:W], in0=tn[c][0][:, 0:W], in1=rstd[:, 0:W], op=ALU.mult), reads=[tn[c][1], b_rstd], writes=[tn[c][1]])
                        S.op("act", lambda e: e.activation(out=sbf[c][0][:, 0:W], in_=tn[c][0][:, 0:W], func=AF.Silu, scale=cp[:, C_LNG + c:C_LNG + c + 1], bias=cp[:, C_LNB + c:C_LNB + c + 1]),
                             reads=[tn[c][1], b_cp], writes=[sbf[c][1]])
                    for co in range(2):
                        po, b_po = psf.next()
                        for c in range(2):
                            S.op("pe", lambda e: e.matmul(out=po[:, 0:W], lhsT=wpw[:, c, co * 128:(co + 1) * 128], rhs=sbf[c][0][:, 0:W], start=(c == 0), stop=(c == 1)),
                                 reads=wpw_b + [sbf[c][1]], writes=[b_po])
                        S.op("act", lambda e: e.activation(out=YT[:, 4 + co, ycol0 + b0:ycol0 + b0 + W], in_=po[:, 0:W], func=AF.Copy), reads=[b_po], writes=[b_YTm[2]])

                for i in range(len(blocks) + 1):
                    if i < len(blocks):
                        conv(i)
                    if i >= 1:
                        ln(i - 1)

            cf_conv(zc, b_zc, T, 0)
            if not last:
                zcc, b_zcc = P.sb(st, "zcc", [128, 2, TC + 30], BF16)
                S.op("pool", lambda e: e.memset(zcc[:], 0.0), writes=[b_zcc])
                for c in range(2):
                    S.dma(zcc[:, c, 15:TC + 15], d["zcf"][c * 128:(c + 1) * 128, T:TA], writes=[b_zcc], qn="pool")
                cf_conv(zcc, b_zcc, TC, T)
            S.barrier()
        stSC.close()
        S.release()
        S.mark()

        NKT = 34
        stKT = contextlib.ExitStack()
        KT = P.sb(stKT, "KT", [96, 4, NKT * 128], BF16)[0]
        b_KTh = [Buf("KTh") for _ in range(4)]

        with contextlib.ExitStack() as st:
            NK, b_NK = P.sb(st, "NK", [128, 2, T + 512], BF16)
            b_NKs = []
            b_NQs = []
            NKc, b_NKc = P.sb(st, "NKc", [128, 2, TC], BF16)
            NV, b_NV = P.sb(st, "NV", [128, 20, 4, 65], BF16)
            NVc, b_NVc = P.sb(st, "NVc", [128, 2, 4, 65], BF16)
            NQ, b_NQ = P.sb(st, "NQ", [128, 2, TA], BF16)
            RB, b_RB = P.sb(st, "RB", [128, 4, 16, 64], F32)
            S.dma(RB[:].rearrange("p h o c -> p (h o c)"), d["rbt"][:, :], writes=[b_RB])
            CB, b_CB = P.sb(st, "CB", [128, 4, 5, 128], BF16)
            for h in range(4):
                for di, dk in enumerate(range(-2, 3)):
                    for qr in range(2):
                        idx = K_MB + NA_IDX[(2, dk, qr)]
                        S.op("dve", lambda e: e.tensor_scalar(out=CB[:, h, di, qr * 64:(qr + 1) * 64], in0=RB[:, h, 8 - 2 * dk + qr, :], scalar1=kp[:, idx:idx + 1], scalar2=None, op0=ALU.add),
                             reads=[b_RB, b_kp], writes=[b_CB])
            b_NVg = [Buf("NVg") for _ in range(5)]
            for c in range(2):
                b_NKs.append(Buf("NKa"))
                S.dma(NK[:, c, 256:T + 256], d["nk"][c * 128:(c + 1) * 128, 0:T], writes=[b_NKs[-1]])
                b_NKs.append(Buf("NKb"))
                S.dma(NK[:, c, 0:256], d["g_nk"][c * 128:(c + 1) * 128, 256:512], reads=[d["b_g_nk"]], writes=[b_NKs[-1]])
                b_NKs.append(Buf("NKc_"))
                S.dma(NK[:, c, T + 256:T + 512], d["g_nk"][256 + c * 128:256 + (c + 1) * 128, 0:256], reads=[d["b_g_nk"]], writes=[b_NKs[-1]])
                S.dma(NKc[:, c, :], d["nk"][c * 128:(c + 1) * 128, T:TA], writes=[b_NKc])
                b_NQs.append(Buf("NQl"))
                S.dma(NQ[:, c, :], d["nq"][c * 128:(c + 1) * 128, :], writes=[b_NQs[-1]])
            nvsrc = lambda ap: ap.rearrange("(i p) c -> p i c", p=128)
            S.dma(NV[:, 0:2, :, :].rearrange("p i h e -> p i (h e)"), nvsrc(d["g_nv"][256:512, 0:260]), reads=[d["b_g_nv"]], writes=[b_NVg[0]], qn="act")
            for g_ in range(5):
                i0, i1 = max(2, g_ * 4), min(18, g_ * 4 + 4)
                S.dma(NV[:, i0:i1, :, :].rearrange("p i h e -> p i (h e)"), nvsrc(d["nv"][(i0 - 2) * 128:(i1 - 2) * 128, 0:260]), writes=[b_NVg[g_]], qn="act")
            S.dma(NV[:, 18:20, :, :].rearrange("p i h e -> p i (h e)"), nvsrc(d["g_nv"][512:768, 0:260]), reads=[d["b_g_nv"]], writes=[b_NVg[4]], qn="act")
            S.dma(NVc[:, :, :, :].rearrange("p i h e -> p i (h e)"), nvsrc(d["nv"][T:TA, 0:260]), writes=[b_NVc])
            for h in range(4):
                for r in range(2):
                    S.dma(KT[:, h, r * T:(r + 1) * T], d["g_mk"][r * 384 + h * 96:r * 384 + (h + 1) * 96, 0:T], reads=[d["b_g_mk"]], writes=[b_KTh[h]])
                S.dma(KT[:, h, 2 * T:2 * T + TC], d["mk"][h * 96:(h + 1) * 96, T:TA], writes=[b_KTh[h]])
            sbr = Ring([P.sb(st, "nsb", [128, 6, 128], F32) for _ in range(3)])
            ptr = Ring([P.sb(st, "npt", [128, 8, 128], BF16) for _ in range(3)])
            rec, b_rec = P.sb(st, "nrec", [128, 4], F32)
            ytr = Ring([P.sb(st, "nyt", [128, 256], BF16) for _ in range(2)])

            na_po = Ring(psf.items[0:1])
            na_pl = Ring(psf.items[1:7])

            tiles = [(j * 128, na_slots(j), j) for j in range(16)]
            if not last:
                tiles += [(T + jc * 128, [], 0) for jc in range(2)]
            units = []
            for ti, (qcol, slots, j) in enumerate(tiles):
                tctx = {}
                for h in range(4):
                    units.append(dict(qcol=qcol, slots=slots, j=j, h=h, t=tctx))

            def na_stage1(u):
                qcol, slots, j, h = u["qcol"], u["slots"], u["j"], u["h"]
                nl = len(slots)
                hp, c = (h % 2) * 64, h // 2
                pl = [na_pl.next(), na_pl.next()]
                u["pl"] = pl
                q_ap = NQ[hp:hp + 64, c, qcol:qcol + 128]
                interior_all = (2 <= j <= 13) and len(slots) == 5
                interior = interior_all and h % 2 == 0
                for si, dk in enumerate(slots):
                    p_, b_p = pl[si // 4]
                    kc = (j + dk + 2) * 128
                    S.op("pe", lambda e: e.matmul(out=p_[:, (si % 4) * 128:(si % 4 + 1) * 128], lhsT=NK[hp:hp + 64, c, kc:kc + 128], rhs=q_ap, start=True, stop=not interior),
                         reads=b_NKs + b_NQs, writes=[b_p])
                    if interior:
                        S.op("pe", lambda e: e.matmul(out=p_[:, (si % 4) * 128:(si % 4 + 1) * 128], lhsT=idb[:, :], rhs=CB[:, h, dk + 2, :], start=False, stop=True),
                             reads=[b_idb, b_CB], writes=[b_p])
                for i in range(2):
                    si = nl + i
                    p_, b_p = pl[si // 4]
                    S.op("pe", lambda e: e.matmul(out=p_[:, (si % 4) * 128:(si % 4 + 1) * 128], lhsT=NKc[hp:hp + 64, c, i * 128:(i + 1) * 128], rhs=q_ap, start=True, stop=True),
                         reads=[b_NKc] + b_NQs, writes=[b_p])
                sb_, b_sb = sbr.next()
                pt, b_pt = ptr.next()
                u["pt"] = (pt, b_pt)
                if interior:
                    S.op("act", lambda e: e.activation(out=pt[:, 0:4, :], in_=pl[0][0][:, 0:512].rearrange("p (s q) -> p s q", s=4), func=AF.Exp), reads=[pl[0][1]], writes=[b_pt])
                    S.op("act", lambda e: e.activation(out=pt[:, 4:7, :], in_=pl[1][0][:, 0:384].rearrange("p (s q) -> p s q", s=3), func=AF.Exp), reads=[pl[1][1]], writes=[b_pt])
                    return
                for si, dk in enumerate(slots):
                    p_, b_p = pl[si // 4]
                    o0 = 8 - 2 * dk
                    if interior_all:
                        S.op("dve", lambda e: e.tensor_tensor(out=sb_[:, si, :], in0=p_[:, (si % 4) * 128:(si % 4 + 1) * 128], in1=CB[:, h, dk + 2, :], op=ALU.add),
                             reads=[b_p, b_CB], writes=[b_sb])
                        continue
                    for qr in range(2):
                        idx = K_MB + NA_IDX[(j, dk, qr)]
                        S.op("dve", lambda e: e.scalar_tensor_tensor(out=sb_[:, si, qr * 64:(qr + 1) * 64], in0=p_[:, (si % 4) * 128 + qr * 64:(si % 4) * 128 + (qr + 1) * 64],
                                                                     scalar=kp[:, idx:idx + 1], in1=RB[:, h, o0 + qr, :], op0=ALU.add, op1=ALU.add),
                             reads=[b_p, b_kp, b_RB], writes=[b_sb])
                if nl:
                    S.op("act", lambda e: e.activation(out=pt[:, 0:nl, :], in_=sb_[:, 0:nl, :], func=AF.Exp), reads=[b_sb], writes=[b_pt])
                for i in range(2):
                    si = nl + i
                    p_, b_p = pl[si // 4]
                    S.op("act", lambda e: e.activation(out=pt[:, si, :], in_=p_[:, (si % 4) * 128:(si % 4 + 1) * 128], func=AF.Exp), reads=[b_p], writes=[b_pt])

            def na_stage2(u):
                qcol, slots, j, h, tctx = u["qcol"], u["slots"], u["j"], u["h"], u["t"]
                nl = len(slots)
                if h == 0:
                    tctx["po"] = na_po.next()
                    tctx["yt"] = ytr.next()
                po, b_po = tctx["po"]
                yt, b_yt = tctx["yt"]
                pt, b_pt = u["pt"]
                for si in range(nl + 2):
                    if si < nl:
                        v_ap, bv = NV[:, j + slots[si] + 2, h, :], b_NVg[(j + slots[si] + 2) // 4]
                    else:
                        v_ap, bv = NVc[:, si - nl, h, :], b_NVc
                    S.op("pe", lambda e: e.matmul(out=po[:, h * 65:(h + 1) * 65], lhsT=pt[:, si, :], rhs=v_ap, start=(si == 0), stop=(si == nl + 1)),
                         reads=[b_pt, bv], writes=[b_po])
                if h == 3:
                    S.op("dve", lambda e: e.reciprocal(out=rec[:, :], in_=po[:, 0:260].rearrange("p (h e) -> p h e", e=65)[:, :, 64]), reads=[b_po], writes=[b_rec])
                    for hh in range(4):
                        S.op("dve", lambda e: e.tensor_scalar(out=yt[:, hh * 64:(hh + 1) * 64], in0=po[:, hh * 65:hh * 65 + 64], scalar1=rec[:, hh:hh + 1], scalar2=None, op0=ALU.mult),
                             reads=[b_po, b_rec], writes=[b_yt])
                    tm_to_YT(lambda c: yt[:, c * 128:(c + 1) * 128], b_yt, 6, qcol, b_YTm[3])

            NSK = 2
            for i in range(len(units) + NSK):
                if i < len(units):
                    na_stage1(units[i])
                if i - NSK >= 0:
                    na_stage2(units[i - NSK])
            S.barrier()
        S.release()
        S.mark()

        with contextlib.ExitStack() as st:
            MV = P.sb(st, "MV", [128, NKT, 4, 128], BF16)[0]
            b_MVg = [Buf("MVg") for _ in range(5)]
            mvsrc = lambda ap: ap.rearrange("(i p) c -> p i c", p=128)
            for g_ in range(4):
                r, i0 = g_ // 2, (g_ % 2) * 8
                gn = "g_mva" if g_ % 2 == 0 else "g_mvb"
                S.dma(MV[:, g_ * 8:g_ * 8 + 8, :, :].rearrange("p i h e -> p i (h e)"), mvsrc(d[gn][r * 1024:(r + 1) * 1024, :]),
                      reads=[d["b_" + gn]], writes=[b_MVg[g_]], qn="act")
            S.dma(MV[:, 32:34, :, :].rearrange("p i h e -> p i (h e)"), mvsrc(d["mvc"][:, :]), writes=[b_MVg[4]], qn="act")
            p0s = p0_next(st, psf.items[5]) if p0_next else []
            mla_ps = Ring(psf.items[2:5] + psf.items[6:7]) if p0_next else Ring(psf.items[2:7])
            mla_po = Ring(psf.items[0:2])
            QTr = Ring([P.sb(st, "QT", [96, 4, 512], BF16) for _ in range(2)])
            PTr = Ring([P.sb(st, "PT", [128, 512], BF16) for _ in range(4)])
            recr = Ring([P.sb(st, "mrec", [64, 512], F32) for _ in range(1)])
            first = True
            for (t0, W, seg) in BLOCKS[:nblk]:
                QT, b_QT = QTr.next()
                for h in range(4):
                    S.dma(QT[:, h, 0:W], d["mq"][h * 96:(h + 1) * 96, t0:t0 + W], writes=[b_QT])
                kts = list(range(NKT)) if seg == 0 else [32, 33]
                its = [dict(h=h, ki=ki, kt=kt) for h in range(4) for ki, kt in enumerate(kts)]
                hctx = {}

                def mla_s1(it):
                    h, kt = it["h"], it["kt"]
                    pS_, b_pS = mla_ps.next()
                    S.op("pe", lambda e: e.matmul(out=pS_[:, 0:W], lhsT=KT[:, h, kt * 128:(kt + 1) * 128], rhs=QT[:, h, 0:W], start=True, stop=True),
                         reads=[b_KTh[h], b_QT], writes=[b_pS])
                    PT, b_PT = PTr.next()
                    it["PT"] = (PT, b_PT)
                    S.op("act", lambda e: e.activation(out=PT[:, 0:W], in_=pS_[:, 0:W], func=AF.Exp, scale=MLA_SCALE), reads=[b_pS], writes=[b_PT])

                def mla_s2(it):
                    h, ki, kt = it["h"], it["ki"], it["kt"]
                    if ki == 0:
                        hctx[h] = mla_po.next()
                    pO, b_pO = hctx[h]
                    PT, b_PT = it["PT"]
                    S.op("pe", lambda e: e.matmul(out=pO[:, 0:W], lhsT=MV[:, kt, h, :], rhs=PT[:, 0:W], start=(ki == 0), stop=(ki == len(kts) - 1)),
                         reads=[b_PT, b_MVg[kt // 8]], writes=[b_pO])
                    if ki == len(kts) - 1:
                        rec, b_rec = recr.next()
                        S.op("dve", lambda e: e.reciprocal(out=rec[0:64, 0:W], in_=pO[64:128, 0:W]), reads=[b_pO], writes=[b_rec])
                        hp = (h % 2) * 64
                        S.op("dve", lambda e: e.tensor_tensor(out=YT[hp:hp + 64, 2 + h // 2, t0:t0 + W], in0=pO[0:64, 0:W], in1=rec[0:64, 0:W], op=ALU.mult),
                             reads=[b_pO, b_rec], writes=[b_YTm[1]])

                MSK = 2
                for i in range(len(its) + MSK):
                    if i < len(its):
                        mla_s1(its[i])
                    if i - MSK >= 0:
                        mla_s2(its[i - MSK])
                    if i % 8 == 7 and p0s:
                        p0s.pop(0)()
                if first and (t0 >= 512 or not p0_next):
                    first = False
                    w1, w1_b, w1ch = load_cast(P, S, stW, "w1", wsrc(d["wmlp1"]), 8, 4096, stgW, chunk=512, pre=w1_t)
            while p0s:
                p0s.pop(0)()
            S.barrier()
        S.release()
        S.mark()

        stKT.close()
        with contextlib.ExitStack() as st:
            stg = None
            wo, wo_b, woch = load_cast(P, S, st, "wout", wsrc(d["wout"]), 8, 1024, stg)
            ring3 = psf
            G1 = build_G(st, C_GPOST1, 16, ring3)
            steps = []
            xt_ring = Ring([P.sb(st, "xt", [128, D], F32) for _ in range(3)])
            junk_ring = Ring([P.sb(st, "junk", [128, 512], F32) for _ in range(3)])
            ss_ring = Ring([P.sb(st, "ss", [128, 8], F32) for _ in range(3)])
            ot_ring = Ring([P.sb(st, "ot", [128, D], F32) for _ in range(3)])
            wk = (junk_ring, None, ss_ring, None, ot_ring)
            xl = XLoader(S, xt_ring)
            for ti in range(ntile):
                seg = 0 if ti < 16 else 1
                xl.prefetch(("e", ti), d["xin"][ti * 128:(ti + 1) * 128, :])
                pY = [ring3.next(), ring3.next()]
                for nb in range(2):
                    for k in range(8):
                        S.op("pe", lambda e: e.matmul(out=pY[nb][0][:, 0:512], lhsT=YT[:, k, ti * 128:(ti + 1) * 128], rhs=wo[:, k, nb * 512:(nb + 1) * 512], start=(k == 0), stop=(k == 7)),
                             reads=b_YTm + wbuf(wo_b, woch, nb * 512, nb * 512 + 512), writes=[pY[nb][1]])
                epilogue(pY, xl.get(("e", ti), None), d["xmid"][ti * 128:(ti + 1) * 128, :], G1[seg], wk)
                if steps:
                    steps.pop(0)()
            while steps:
                steps.pop(0)()
            S.barrier()
        S.release()
        S.mark()

        stY.close()
        with contextlib.ExitStack() as st:
            stg = stgW
            w2t, w2_b, w2ch = load_cast(P, S, st, "w2", wsrc(d["wmlp2"]), 32, 1024, stg, chunk=512, per_k=True)
            w2 = (w2t,)
            xt_ring = Ring([P.sb(st, "xt", [128, D], F32) for _ in range(2)])
            ss, b_ss = P.sb(st, "ss", [128, 8], F32)
            xn, b_xn = P.sb(st, "xn", [128, D], BF16)
            wkp = (xn, b_xn, ss, b_ss, xn, b_xn)
            ot_ring = Ring([P.sb(st, "ot", [128, D], F32) for _ in range(1)])
            wke = (ot_ring.items[0][0], ot_ring.items[0][1], ss, b_ss, ot_ring)
            hT_ring = Ring([P.sb(st, "hT", [128, 8, 512], BF16) for _ in range(2)])
            h1, b_h1 = P.sb(st, "h1", [128, 32, 512], BF16)
            rl_ring = Ring([P.sb(st, "rl", [128, 512], F32) for _ in range(2)])
            mblocks = BLOCKS[:nblk]
            hTs = {}

            pp = ProPipe()

            def pro2_parts(bi_, sub, wk_):
                t0_, W_, seg_ = mblocks[bi_]
                if bi_ not in hTs:
                    hTs[bi_] = hT_ring.next()
                hT_, b_hT_ = hTs[bi_]
                tt_ = t0_ + sub * 128
                nxt = (("p", tt_ + 128), d["xmid"][tt_ + 128:tt_ + 256, :]) if tt_ + 128 < ntile * 128 else None
                return (lambda: prologue_a(P, S, d["xmid"][tt_:tt_ + 128, :], xl, wk_, cp, b_cp, key=("p", tt_), nxt=nxt),
                        lambda: prologue_b(P, S, wk_, idb, b_idb, psb, hT_, b_hT_, sub * 128,
                                           lambda k: A2[:, k, seg_:seg_ + 1], lambda k: mod[:, 24 + k, seg_:seg_ + 1], b_A2))

            def pro2(bi_, sub):
                if bi_ >= len(mblocks) or sub >= mblocks[bi_][1] // 128:
                    pp.flush()
                    return
                a_fn, b_fn = pro2_parts(bi_, sub, wkp)
                pp.step(a_fn, b_fn)

            xl = XLoader(S, xt_ring)
            ss2, b_ss2 = P.sb(st, "ss2", [128, 4], F32)
            xn2 = h1[:, 0:2, :].rearrange("p a b -> p (a b)")
            wk_alt = (xn2, b_h1, ss2, b_ss2, xn2, b_h1)
            parts = [pro2_parts(0, sub, wkp if sub % 2 == 0 else wk_alt) for sub in range(mblocks[0][1] // 128)]
            parts[0][0]()
            for i_ in range(len(parts)):
                if i_ + 1 < len(parts):
                    parts[i_ + 1][0]()
                parts[i_][1]()
            for bi, (t0, WB, seg) in enumerate(mblocks):
                hT, b_hT = hTs[bi]
                for hc in range(32):
                    ph, b_ph = psf.next()
                    fm_mm(S, ph, b_ph, w1, wbuf(w1_b, w1ch, hc * 128, hc * 128 + 128), hT, b_hT, hc * 128, 128, WB)
                    rl, b_rl = rl_ring.next()
                    S.op("act", lambda e: e.activation(out=rl[:, 0:WB], in_=ph[:, 0:WB], func=AF.Relu), reads=[b_ph], writes=[b_rl])
                    S.op("dve", lambda e: e.tensor_tensor(out=h1[:, hc, 0:WB], in0=rl[:, 0:WB], in1=rl[:, 0:WB], op=ALU.mult), reads=[b_rl], writes=[b_h1])
                if bi == 0:
                    G2 = build_G(st, C_GPOST2, 40)
                nsub = WB // 128
                for sub in range(nsub):
                    for ps_ in range(sub * 4 // nsub, (sub + 1) * 4 // nsub):
                        pro2(bi + 1, ps_)
                    xl.prefetch(("e", t0 + sub * 128), d["xmid"][t0 + sub * 128:t0 + (sub + 1) * 128, :])
                    pY = [psf.next(), psf.next()]
                    for nb in range(2):
                        for hc in range(32):
                            S.op("pe", lambda e: e.matmul(out=pY[nb][0][:, 0:512], lhsT=h1[:, hc, sub * 128:(sub + 1) * 128], rhs=w2[0][:, hc, nb * 512:(nb + 1) * 512], start=(hc == 0), stop=(hc == 31)),
                                 reads=[b_h1, w2_b[nb][hc // 8]], writes=[pY[nb][1]])
                    tt = t0 + sub * 128
                    epilogue(pY, xl.get(("e", tt), None), d["xout"][tt:tt + 128, :], G2[seg], wke)
                pp.flush()
            S.barrier()
        S.release()
        stW.close()


def fm_cols(v):
    return np.ascontiguousarray(np.asarray(v, np.float32).reshape(-1, 128).T)


def rope_perm():
    d = np.arange(32)
    g, i = d // 16, d % 16
    return g * 16 + (i + 8) % 16


def host_tables(core):
    b, s = core // 2, core % 2
    pos = s * T + np.arange(T)
    inv = (10000.0 ** (-np.arange(8, dtype=np.float32) / 8)).astype(np.float32)
    ang_row = (pos // 64).astype(np.float32)[:, None] * inv
    ang_col = (pos % 64).astype(np.float32)[:, None] * inv
    cos = np.ones((32, TA), np.float32)
    sin = np.zeros((32, TA), np.float32)
    for dd in range(32):
        g, i = dd // 16, dd % 16
        ang = (ang_row if g == 0 else ang_col)[:, i % 8]
        cos[dd, :T] = np.cos(ang)
        sin[dd, :T] = np.sin(ang) * (-1.0 if i < 8 else 1.0)
    mb = np.zeros((128, len(NA_IDX)), np.float32)
    for (j, dk, qr), idx in NA_IDX.items():
        r = 32 * s + 2 * j + qr
        w0 = min(max(r - 4, 0), 56)
        for kr in range(2):
            rk = 32 * s + 2 * (j + dk) + kr
            ok = (0 <= rk < 64) and (w0 <= rk < w0 + 8)
            mb[kr * 64:(kr + 1) * 64, idx] = 0.0 if ok else NEG
    return np.stack([cos, sin]), mb


def rb_table(rpb):
    col = np.arange(64)
    cs = np.clip(col - 8, 0, 48)
    in_win = (col[None, :] >= cs[:, None]) & (col[None, :] < cs[:, None] + 16)
    off_c = np.clip(col[None, :] - col[:, None] + 15, 0, 30)
    out = np.full((128, 4, 16, 64), NEG, np.float32)
    for kr in range(2):
        for o2 in range(16):
            off = 15 - o2 + kr
            if off > 14:
                continue
            blk = rpb[:, off][:, off_c]
            blk = np.where(in_win[None], blk, np.float32(NEG))
            out[kr * 64:(kr + 1) * 64, :, o2, :] = np.transpose(blk, (2, 0, 1))
    return out.reshape(128, -1)


LAYER_W = [("wmod", [D, 6 * D]), ("winx", [D, 2560]), ("wuq", [256, 384]), ("wuqs", [256, 384]), ("wukvk", [128, 256]),
           ("wukvv", [128, 256]), ("colpack", [128, NCOL]), ("wout", [D, D]), ("wpw", [256, 256]), ("wmlp1", [D, 4 * D]),
           ("wmlp2", [4 * D, D]), ("rbt", [128, 4 * 16 * 64])]
SCRATCH = [("zsc", [256, TA], F32), ("bsc", [256, TA], F32), ("zcf", [256, TA], F32), ("nq", [256, TA], BF16), ("nk", [256, TA], BF16),
           ("nv", [TA, 272], BF16), ("mq", [384, TA], BF16), ("mk", [384, TA], BF16), ("mva", [1024, 512], BF16), ("mvb", [1024, 512], BF16), ("mvc", [256, 512], BF16),
           ("send_f", [256, 32], F32), ("send_nk", [256, 512], BF16), ("send_nv", [512, 272], BF16),
           ("g_f", [512, 32], F32), ("g_nk", [512, 512], BF16), ("g_nv", [1024, 272], BF16), ("g_mk", [768, TA], BF16),
           ("g_mva", [2048, 512], BF16), ("g_mvb", [2048, 512], BF16), ("xmid", [TA, D], F32)]
PAIRS = [[0, 1], [2, 3], [4, 5], [6, 7]]


def build_fused():
    P = Prog()
    nc = P.nc
    g = {}
    for n, s_, t in [("xin", [TA, D], F32), ("corepack", [128, NCC], F32), ("rope", [2, 32, TA], F32), ("ident", [128, 128], F32)]:
        g[n] = P.dram(n, s_, t, "in")
    xout = P.dram("xout", [T, D], F32, "out")
    x1 = P.dram("x1", [TA, D], F32, "int")
    with contextlib.ExitStack() as st0:
        S = Sched(nc, st0)
        psf, psb = make_psum(P, st0)
        modts = []
        for l in range(2):
            mod_, b_mod_ = P.sb(st0, "mod", [128, 48, 2], F32)
            A1_, b_A1_ = P.sb(st0, "A1", [128, 8, 2], F32)
            modts.append((mod_, b_mod_, A1_, b_A1_))
        kp0, b_kp0 = P.sb(st0, "kp0", [128, NCC], F32)
        S.dma(kp0[:], g["corepack"][:, :], writes=[b_kp0])
        lw = {}
        for l in range(2):
            for n, s_ in LAYER_W:
                lw[(n, l)] = P.dram("%s%d" % (n, l), s_, F32, "in")
        for l in range(2):
            d = dict(g)
            for n, s_ in LAYER_W:
                d[n] = lw[(n, l)]
            for n, s_, t in SCRATCH:
                d[n] = P.dram("%s%d" % (n, l), s_, t, "int")
            d["xin"] = g["xin"] if l == 0 else x1
            d["xout"] = x1 if l == 0 else xout
            with contextlib.ExitStack() as stL:
                K = load_consts(P, S, stL, d)
                def gather(a, b, wait=True):
                    S.collective(d[a], d[b], PAIRS, wait=wait)
                    d["b_" + b] = Buf(b)
                    d["b_" + b].w = ("cc", S.cnt["cc"])

                def small_gathers():
                    gather("send_f", "g_f")
                    gather("send_nk", "g_nk", wait=False)
                    gather("send_nv", "g_nv", wait=False)

                mod, b_mod = emit_A(P, S, stL, d, K, psf, psb, modts[l], need_p0=(l == 0), after_latent=small_gathers)
                d["late_gathers"] = lambda: [gather(a, b, wait=False) for a, b in [("mk", "g_mk"), ("mva", "g_mva"), ("mvb", "g_mvb")]]
                p0n = None
                if l == 0:
                    dn = {"colpack": lw[("colpack", 1)], "wmod": lw[("wmod", 1)]}
                    p0n = lambda st_, pmb: p0_steps(P, S, st_, dn, kp0, b_kp0, pmb[0], pmb[1], *modts[1], gcols=256, nbuf=2)
                emit_B(P, S, stL, d, K, psf, psb, mod, b_mod, last=(l == 1), p0_next=p0n)
                S.barrier()
        print("instructions", S.nins, "waits", S.nwait, "sems", len(S.sem) + len(S.free) + len(S.free_sw))
    return P


_PROG = []


def kernel(x, c, ctx, c_ctx, w_mod, b_mod, g_pre_mix, w_in, w_sc, g_q, w_uq, g_kv, w_ukv, cf_w_dw, cf_b_dw, cf_ln_g, cf_ln_b,
           cf_w_pw, na_rpb, w_out, g_post_mix, g_pre_mlp, w_mlp1, w_mlp2, g_post_mlp):
    f = lambda a: np.ascontiguousarray(np.asarray(a, np.float32))
    x, c, ctx, c_ctx = f(x), f(c), f(ctx), f(c_ctx)
    ident = np.eye(128, dtype=np.float32)
    perm = rope_perm()
    tabs = [host_tables(k) for k in range(8)]
    shared = {}
    for l in range(2):
        wi = f(w_in[l])
        winx = np.concatenate([wi, wi[:, 1088:1152], wi[:, 1152:1184][:, perm]], axis=1)
        wuq = f(w_uq[l])
        wuqs = wuq.copy().reshape(256, 4, 96)
        wuqs[:, :, 64:] = wuqs[:, :, 64:][:, :, perm]
        wuqs = np.ascontiguousarray(wuqs.reshape(256, 384))
        wkv4 = f(w_ukv[l]).reshape(128, 4, 128)
        cpk = np.zeros((128, NCOL), np.float32)
        cpk[:, C_GPRE1:C_GPRE1 + 8] = fm_cols(g_pre_mix[l])
        cpk[:, C_GPOST1:C_GPOST1 + 8] = fm_cols(g_post_mix[l])
        cpk[:, C_GPRE2:C_GPRE2 + 8] = fm_cols(g_pre_mlp[l])
        cpk[:, C_GPOST2:C_GPOST2 + 8] = fm_cols(g_post_mlp[l])
        cpk[:, C_BMOD:C_BMOD + 48] = fm_cols(b_mod[l])
        cpk[:, C_GQ:C_GQ + 2] = fm_cols(g_q[l])
        cpk[:, C_GKV:C_GKV + 1] = fm_cols(g_kv[l])
        wsc = f(w_sc[l])
        wdw = f(cf_w_dw[l])
        for cc in range(2):
            cpk[:, C_WSC + cc * 3:C_WSC + cc * 3 + 3] = wsc[:, cc * 128:(cc + 1) * 128].T
            cpk[:, C_WDW + cc * 31:C_WDW + cc * 31 + 31] = wdw[:, cc * 128:(cc + 1) * 128].T
        cpk[:, C_BDW:C_BDW + 2] = fm_cols(cf_b_dw[l])
        cpk[:, C_LNG:C_LNG + 2] = fm_cols(cf_ln_g[l])
        cpk[:, C_LNB:C_LNB + 2] = fm_cols(cf_ln_b[l])
        cpk[:, C_EPS] = EPS
        lw = dict(wmod=f(w_mod[l]), winx=winx, wuq=wuq, wuqs=wuqs,
                  wukvk=np.ascontiguousarray(wkv4[:, :, :64].reshape(128, 256)),
                  wukvv=np.ascontiguousarray(wkv4[:, :, 64:].reshape(128, 256)),
                  colpack=cpk, wout=f(w_out[l]), wpw=f(cf_w_pw[l]), wmlp1=f(w_mlp1[l]), wmlp2=f(w_mlp2[l]),
                  rbt=rb_table(f(na_rpb[l])))
        for n, v in lw.items():
            shared["%s%d" % (n, l)] = v
    in_maps = []
    for k in range(8):
        kp = np.zeros((128, NCC), np.float32)
        kp[:, K_C:K_C + 8] = fm_cols(c[k // 2])
        kp[:, K_CC:K_CC + 8] = fm_cols(c_ctx)
        kp[:, K_ML] = float(k % 2 == 1)
        kp[:, K_MR] = float(k % 2 == 0)
        kp[:, K_MB:] = tabs[k][1]
        m = dict(shared)
        m.update(xin=np.concatenate([x[k // 2, (k % 2) * T:(k % 2 + 1) * T], ctx[k // 2]], axis=0), corepack=kp, rope=tabs[k][0], ident=ident)
        in_maps.append(m)
    if not _PROG:
        _PROG.append(build_fused())
    res = run_bass_kernel_spmd(_PROG[0].nc, in_maps, core_ids=list(range(8)))
    out = np.zeros((4, 2 * T, D), np.float32)
    for k in range(8):
        out[k // 2, (k % 2) * T:(k % 2 + 1) * T] = np.asarray(res.results[k]["xout"], np.float32)
    return out
```
